# Optimizing a Trainium2 kernel written in Bass

```python
import jax, jax.numpy as jnp
from jax import lax
import numpy as np

D_MODEL = 1024
BATCH = 8
SEQ = 8192
DEPTH = 1

CHUNK = 64
N_LEFT_CHUNKS = 8
A_HEADS = 8
A_HEAD_DIM = 64
A_WIDTH = A_HEADS * A_HEAD_DIM
REL_CLIP = 128
R_HEADS = 8
R_KEY_DIM = 64
R_VAL_DIM = 128
R_QK_WIDTH = R_HEADS * R_KEY_DIM
R_V_WIDTH = R_HEADS * R_VAL_DIM
ROPE_BASE = 10000.0
NORM_EPS = 1e-6
GN_EPS = 1e-5
NEG_INF = -1e30
IN_SPLITS = (A_WIDTH, A_WIDTH, A_WIDTH, A_WIDTH, R_QK_WIDTH, R_QK_WIDTH, R_V_WIDTH, R_V_WIDTH, D_MODEL, D_MODEL)
IN_WIDTH = 4 * A_WIDTH + 2 * R_QK_WIDTH + 2 * R_V_WIDTH + 2 * D_MODEL

kernel_name = "hybrid_chunk_attn_retention_gated"


def rms_norm(x, gain):
    xf = x.astype(jnp.float32)
    y = xf * lax.rsqrt(jnp.mean(xf * xf, axis=-1, keepdims=True) + NORM_EPS)
    return (y * gain.astype(jnp.float32)).astype(x.dtype)


def split_columns(proj):
    outs = []
    start = 0
    for width in IN_SPLITS:
        outs.append(proj[..., start:start + width])
        start += width
    return outs


def rotary(x, positions):
    half = x.shape[-1] // 2
    inv_freq = jnp.power(ROPE_BASE, -jnp.arange(half, dtype=jnp.float32) / half)
    ang = positions.astype(jnp.float32)[:, None] * inv_freq[None, :]
    cos = jnp.cos(ang)[None, :, None, :]
    sin = jnp.sin(ang)[None, :, None, :]
    x1 = x[..., :half].astype(jnp.float32)
    x2 = x[..., half:].astype(jnp.float32)
    out = jnp.concatenate([x1 * cos - x2 * sin, x1 * sin + x2 * cos], axis=-1)
    return out.astype(x.dtype)


def chunk_band_attention(q, k, v, rel_bias):
    B, S, H, d = q.shape
    n_chunks = S // CHUNK
    band = (N_LEFT_CHUNKS + 1) * CHUNK
    q_local = jnp.arange(CHUNK)
    p = jnp.arange(band)
    dist = q_local[:, None] + N_LEFT_CHUNKS * CHUNK - p[None, :]
    idx = jnp.clip(dist, -REL_CLIP, REL_CLIP) + REL_CLIP
    bias = rel_bias[:, idx].astype(jnp.float32)
    valid = (jnp.arange(n_chunks)[:, None] - N_LEFT_CHUNKS + p[None, :] // CHUNK) >= 0
    scale = d ** -0.5

    def one_sequence(args):
        qb, kb, vb = args
        qc = qb.reshape(n_chunks, CHUNK, H, d)
        pad = jnp.zeros((N_LEFT_CHUNKS, CHUNK, H, d), kb.dtype)
        kp = jnp.concatenate([pad, kb.reshape(n_chunks, CHUNK, H, d)], axis=0)
        vp = jnp.concatenate([pad, vb.reshape(n_chunks, CHUNK, H, d)], axis=0)
        k_band = jnp.concatenate([kp[i:i + n_chunks] for i in range(N_LEFT_CHUNKS + 1)], axis=1)
        v_band = jnp.concatenate([vp[i:i + n_chunks] for i in range(N_LEFT_CHUNKS + 1)], axis=1)
        s = jnp.einsum('nqhd,nkhd->nhqk', qc, k_band).astype(jnp.float32) * scale + bias[None]
        s = jnp.where(valid[:, None, None, :], s, NEG_INF)
        w = jax.nn.softmax(s, axis=-1).astype(vb.dtype)
        o = jnp.einsum('nhqk,nkhd->nqhd', w, v_band)
        return o.reshape(S, H, d)

    return lax.map(one_sequence, (q, k, v))


def retention_chunkwise(q, k, v, gn_gain):
    B, S, H, dk = q.shape
    dv = v.shape[-1]
    n_chunks = S // CHUNK
    log_gamma = jnp.log1p(-jnp.exp2(-5.0 - jnp.arange(H, dtype=jnp.float32)))
    j = jnp.arange(CHUNK, dtype=jnp.float32)
    intra_decay = jnp.exp(jnp.abs(j[:, None] - j[None, :])[None] * log_gamma[:, None, None])
    cross_decay = jnp.exp((j + 1.0)[None, :] * log_gamma[:, None])
    state_decay = jnp.exp((CHUNK - 1.0 - j)[None, :] * log_gamma[:, None])
    chunk_decay = jnp.exp(CHUNK * log_gamma)

    qc = q.reshape(B, n_chunks, CHUNK, H, dk) * (dk ** -0.5)
    kc = k.reshape(B, n_chunks, CHUNK, H, dk)
    vc = v.reshape(B, n_chunks, CHUNK, H, dv)

    s = jnp.einsum('bnqhd,bnkhd->bnhqk', qc, kc) * intra_decay
    o_intra = jnp.einsum('bnhqk,bnkhe->bnqhe', s, vc)

    kv = jnp.einsum('bnkhd,hk,bnkhe->nbhde', kc, state_decay, vc).astype(jnp.float32)

    def step(state, kv_n):
        return state * chunk_decay[None, :, None, None] + kv_n, state

    _, state_prev = lax.scan(step, jnp.zeros((B, H, dk, dv), jnp.float32), kv)
    o_cross = jnp.einsum('bnqhd,nbhde->bnqhe', qc, state_prev) * cross_decay.T[:, :, None]

    o = (o_intra + o_cross).astype(jnp.float32).reshape(B, S, H, dv)
    mean = jnp.mean(o, axis=-1, keepdims=True)
    var = jnp.mean(jnp.square(o - mean), axis=-1, keepdims=True)
    o = (o - mean) * lax.rsqrt(var + GN_EPS)
    o = o.reshape(B, S, H * dv) * gn_gain.astype(jnp.float32)
    return o.astype(v.dtype)


def setup_inputs(seed: int = 0) -> dict:
    key = jax.random.key(seed)
    ks = jax.random.split(key, 10)
    x = jax.random.normal(ks[0], (BATCH, SEQ, D_MODEL), jnp.float32)
    norm_gain = 1.0 + 0.01 * jax.random.normal(ks[1], (DEPTH, D_MODEL), jnp.float32)
    w_in = jax.random.normal(ks[2], (DEPTH, D_MODEL, IN_WIDTH), jnp.float32) * D_MODEL ** -0.5
    rel_bias = 0.5 * jax.random.normal(ks[3], (DEPTH, A_HEADS, 2 * REL_CLIP + 1), jnp.float32)
    gn_gain = 1.0 + 0.01 * jax.random.normal(ks[4], (DEPTH, R_V_WIDTH), jnp.float32)
    w_out_attn = jax.random.normal(ks[5], (DEPTH, A_WIDTH, D_MODEL), jnp.float32) * A_WIDTH ** -0.5
    w_out_ret = jax.random.normal(ks[6], (DEPTH, R_V_WIDTH, D_MODEL), jnp.float32) * R_V_WIDTH ** -0.5
    w_out = jax.random.normal(ks[7], (DEPTH, D_MODEL, D_MODEL), jnp.float32) * D_MODEL ** -0.5
    final_gain = 1.0 + 0.01 * jax.random.normal(ks[8], (D_MODEL,), jnp.float32)
    return {"x": x, "norm_gain": norm_gain, "w_in": w_in, "rel_bias": rel_bias,
            "gn_gain": gn_gain, "w_out_attn": w_out_attn, "w_out_ret": w_out_ret,
            "w_out": w_out, "final_gain": final_gain}


def reference(x, norm_gain, w_in, rel_bias, gn_gain, w_out_attn, w_out_ret, w_out, final_gain):
    B, S, _ = x.shape
    positions = jnp.arange(S)
    for l in range(DEPTH):
        h = rms_norm(x, norm_gain[l])
        proj = jnp.einsum('bsd,df->bsf', h, w_in[l])
        q_a, k_a, v_a, g_a, q_r, k_r, v_r, g_r, m_a, m_r = split_columns(proj)

        o_a = chunk_band_attention(q_a.reshape(B, S, A_HEADS, A_HEAD_DIM),
                                   k_a.reshape(B, S, A_HEADS, A_HEAD_DIM),
                                   v_a.reshape(B, S, A_HEADS, A_HEAD_DIM),
                                   rel_bias[l]).reshape(B, S, A_WIDTH)
        branch_a = jnp.einsum('bsf,fd->bsd', o_a * jax.nn.silu(g_a), w_out_attn[l])

        qr = rotary(q_r.reshape(B, S, R_HEADS, R_KEY_DIM), positions)
        kr = rotary(k_r.reshape(B, S, R_HEADS, R_KEY_DIM), positions)
        o_r = retention_chunkwise(qr, kr, v_r.reshape(B, S, R_HEADS, R_VAL_DIM), gn_gain[l])
        branch_r = jnp.einsum('bsf,fd->bsd', o_r * jax.nn.silu(g_r), w_out_ret[l])

        mixed = jax.nn.sigmoid(m_a) * branch_a + jax.nn.sigmoid(m_r) * branch_r
        x = x + jnp.einsum('bsd,de->bse', mixed, w_out[l]).astype(x.dtype)
    return rms_norm(x, final_gain)
```

```python
import numpy as np
from contextlib import ExitStack

import concourse.bass as bass
import concourse.mybir as mybir
from concourse.ap import AP
from concourse.bass_utils import run_bass_kernel_spmd

F32 = mybir.dt.float32
BF16 = mybir.dt.bfloat16
AF = mybir.ActivationFunctionType
ALU = mybir.AluOpType

D = 1024
KT = 8
RING = 6
NG = 14
NORM_EPS = 1e-6
GN_EPS = 1e-5
MASK_NEG = -30000.0


def cap(ap, free, off=0):
    return AP(ap.tensor, ap.offset + off, [list(ap.ap[0])] + [list(f) for f in free])


class _DryIns:
    def then_inc(self, *a, **kw):
        return self


_DRYINS = _DryIns()
DRY = False
POLICY = {"w3": (1, 2)}


class _Probe:
    def __init__(self, eng):
        self.eng = eng
        self.n = 128
        self.name = None

    def __getattr__(self, name):
        real = getattr(self.eng, name) if self.eng is not None else (lambda *a, **kw: _DRYINS)

        def f(*a, **kw):
            self.name = name
            if name == "matmul":
                ap = kw.get("rhs")
            elif name == "bn_stats":
                ap = kw.get("in_")
            else:
                ap = kw.get("out", a[0] if a else None)
            try:
                self.n = int(ap.free_size())
            except Exception:
                self.n = 128
            return real(*a, **kw)
        return f


class Sched:
    def __init__(self, nc, es):
        self.nc = nc
        self.es = es
        self.eng = {"pe": nc.tensor, "act": nc.scalar, "dve": nc.vector, "pool": nc.gpsimd, "sp": nc.sync}
        self.sem = {}
        self.cnt = {}
        for k in self.eng:
            self.sem[k] = es.enter_context(nc.semaphore("sem_" + k))
            self.cnt[k] = 0
        self.seen = {k: {} for k in self.eng}
        self.writer = {}
        self.readers = {}
        self.dsem = {}
        self.dcnt = {}
        self.pending = {k: False for k in self.eng}
        self.nwait = 0
        self.nins = 0
        self.t_eng = {k: 0.0 for k in self.eng}
        self.tok_end = {}
        self.step_end = None

    def _cost(self, e, name, n):
        if e == "pe":
            return max(n, 64) / 2400.0 + 0.015
        if e == "act":
            return 0.22 + n / 1200.0
        if e == "dve":
            return 0.07 + n / 960.0
        if e == "pool":
            return 0.9 if name == "tensor_tensor" and n <= 8 else 0.18 + n / 800.0
        return 0.05

    def _time(self, e, deps, cost, tok, done_extra=0.0):
        ready = 0.0
        for t in deps:
            te = self.tok_end.get(t, 0.0) + (0.15 if t[0] != e else 0.0)
            if te > ready:
                ready = te
        start = max(ready, self.t_eng[e])
        self.t_eng[e] = start + cost
        end = start + cost + done_extra
        if self.tok_end.get(tok, 0.0) < end:
            self.tok_end[tok] = end
        if self.step_end is None or self.step_end < end:
            self.step_end = end

    def _semobj(self, k):
        if isinstance(k, tuple):
            return self.dsem[k[1]]
        return self.sem[k]

    def _deps(self, reads, writes):
        deps = []
        for b in reads:
            if b in self.writer:
                deps.append(self.writer[b])
        for b in writes:
            if b in self.writer:
                deps.append(self.writer[b])
            deps.extend(self.readers.get(b, ()))
        return deps

    def _wait(self, e, deps):
        best = {}
        for (k, v) in deps:
            if best.get(k, 0) < v:
                best[k] = v
        for k, v in best.items():
            if k == e and e == "pe":
                continue
            if self.seen[e].get(k, 0) >= v:
                continue
            if not DRY:
                self.eng[e].wait_ge(self._semobj(k), v)
            self.nwait += 1
            self.seen[e][k] = v

    def _record(self, tok, reads, writes):
        for b in reads:
            self.readers.setdefault(b, []).append(tok)
        for b in writes:
            self.writer[b] = tok
            self.readers[b] = []

    @staticmethod
    def _split(reads, writes):
        isb = lambda b: isinstance(b, str) and len(b) == 2 and b[0] == "b" and b[1].isdigit()
        r2 = [b for b in reads if not isb(b)]
        w2 = list(writes) + [b for b in reads if isb(b)]
        return r2, w2

    def op(self, e, fn, reads=(), writes=(), inc=True):
        reads, writes = self._split(reads, writes)
        deps = self._deps(reads, writes)
        self._wait(e, deps)
        pr = _Probe(None if DRY else self.eng[e])
        ins = fn(pr)
        self.nins += 1
        if inc:
            self.cnt[e] += 1
            ins.then_inc(self.sem[e], 1)
            tok = (e, self.cnt[e])
            self.pending[e] = False
        else:
            tok = (e, self.cnt[e] + 1)
            self.pending[e] = True
        self._time(e, deps, self._cost(e, pr.name, pr.n), tok)
        self._record(tok, reads, writes)
        return ins

    def dma(self, q, out, in_, reads, writes, key):
        if key not in self.dsem:
            self.dsem[key] = None if DRY else self.es.enter_context(self.nc.semaphore("dsem_%d" % len(self.dsem)))
            self.dcnt[key] = 0
        deps = self._deps(reads, writes)
        self._wait(q, deps)
        ins = _DRYINS if DRY else self.eng[q].dma_start(out=out, in_=in_)
        self.nins += 1
        self.dcnt[key] += 16
        ins.then_inc(self.dsem[key], 16)
        tok = (("d", key), self.dcnt[key])
        try:
            nbytes = int(out.free_size()) * 128 * 4
        except Exception:
            nbytes = 1 << 19
        self._time(q, deps, 0.06, tok, done_extra=2.0 + nbytes / 250e3)
        self._record(tok, reads, writes)
        return ins

    def finish(self, q="sp"):
        self.makespan = max(self.t_eng.values())
        if DRY:
            return
        for key, sem in self.dsem.items():
            if self.dcnt[key] > 0:
                self.eng[q].wait_ge(sem, self.dcnt[key])
        for e in self.eng:
            assert not self.pending[e], e
            if e != q and self.cnt[e] > 0:
                self.eng[q].wait_ge(self.sem[e], self.cnt[e])


GREEDY = False


def build(NB, dbg_names=(), stop=99):
    NT = 2 * NB
    S = NT * 128
    nc = bass.Bass("TRN2", target_bir_lowering=False, dynamic_dma_scratch_size=1024)

    def din(name, shape, dt=F32):
        return nc.dram_tensor(name, shape, dt, kind="ExternalInput").ap()

    x_d = din("x", [S, D])
    win_d = din("w_in", [D, 7168])
    woa_d = din("w_oa", [512, D])
    wor_d = din("w_or", [D, D])
    wo_d = din("w_o", [D, D])
    gpk_d = din("gain_pk", [128, 8])
    gnpk_d = din("gn_pk", [128, 8])
    fg_d = din("fgain", [128, D])
    biasT_d = din("biasT", [128, 2048])
    bfar_d = din("bfar", [128, 8])
    ident_d = din("ident", [128, 128])
    Dt_d = din("Dt", [128, 1024])
    cdT_d = din("cdT", [128, 512])
    sd_d = din("sd", [128, 8])
    cd2_d = din("cd2", [128, 4])
    rot_d = din("rot", [NT * 128, 128])
    out_d = nc.dram_tensor("out", [S, D], F32, kind="ExternalOutput").ap()
    wsc_d = nc.dram_tensor("wsc", [NG, 128, 4096], BF16).ap()
    dbg_out = {}

    with ExitStack() as es:
        def sb(name, shape, dt):
            return es.enter_context(nc.sbuf_tensor(name, shape, dt))

        def ps(name, shape, dt):
            return es.enter_context(nc.psum_tensor(name, shape, dt))

        w_oa = sb("w_oa_sb", [128, 4, D], BF16)
        w_or = sb("w_or_sb", [128, 8, D], BF16)
        w_o = sb("w_o_sb", [128, 8, D], BF16)
        NW = 4
        wbuf = [sb("wbuf%d" % i, [128, 8, 512], BF16) for i in range(NW)]
        hT = sb("hT", [128, 8, 256], BF16)
        xy = [sb("xy%d" % i, [128, D], F32) for i in range(4)]
        h_bf = sb("h_bf", [128, D], BF16)
        qT_a2 = [sb("qT_a%d" % i, [128, 4, 256], BF16) for i in range(1)] * 2
        kT_a = sb("kT_a", [128, 4, 768], BF16)
        Vaug = sb("Vaug", [128, 6, 8, 65], BF16)
        gg_a2 = [sb("gg_a%d" % i, [128, 2, 512], BF16) for i in range(1)] * 2
        qk_rot2 = sb("qk_rot", [128, 2, 1024], BF16)
        k_dec = sb("k_dec", [128, 2, 512], BF16)
        qT_r = sb("qT_r", [128, 4, 256], BF16)
        qcT_r = sb("qcT_r", [128, 4, 256], BF16)
        kT_r = sb("kT_r", [128, 4, 256], BF16)
        v_r = sb("v_r", [128, 2, 1024], BF16)
        gg_r = sb("gg_r", [128, 2, 1024], BF16)
        t_ma = sb("t_ma", [128, 2, 1024], BF16)
        t_mr = sb("t_mr", [128, 2, 1024], BF16)
        PT = [sb("PT%d" % i, [128, 5, 128], BF16) for i in range(2)]
        Pr = sb("Pr", [128, 8, 128], BF16)
        state = sb("state", [128, 4, 128], F32)
        state_bf = sb("state_bf", [128, 4, 128], BF16)
        oa_tmp = sb("oa_tmp", [128, 512], F32)
        og_a = sb("og_a", [128, 512], BF16)
        ogT_a = sb("ogT_a", [128, 4, 128], BF16)
        ogT_a_b = sb("ogT_a_b", [128, 4, 128], BF16)
        or_n = sb("or_n", [128, 1024], BF16)
        ogT_r = sb("ogT_r", [128, 8, 128], BF16)
        ogT_r_b = sb("ogT_r_b", [128, 8, 128], BF16)
        u_bf = sb("u_bf", [128, 1024], BF16)
        v_bf = sb("v_bf", [128, 1024], BF16)
        mixedT = sb("mixedT", [128, 8, 128], BF16)
        t1 = sb("t1", [128, 512], F32)
        t2 = sb("t2", [128, 512], F32)
        rot = [sb("rot%d" % i, [128, 128], F32) for i in range(2)]
        Dt = sb("Dt_sb", [128, 8, 128], F32)
        cdT = sb("cdT_sb", [128, 4, 128], F32)
        bias_hi = sb("bias_hi", [128, 8, 256], BF16)
        fgain = sb("fgain_sb", [128, D], F32)
        ident_f = sb("ident_f", [128, 128], F32)
        ident = sb("ident_b", [128, 128], BF16)
        maskT = sb("maskT", [128, 128], BF16)
        sd = sb("sd_sb", [128, 8], F32)
        cd2 = sb("cd2_sb", [128, 4], F32)
        bfar = sb("bfar_sb", [128, 8], F32)
        gpk = sb("gpk_sb", [128, 8], F32)
        gnpk = sb("gnpk_sb", [128, 8], F32)
        gnh = sb("gnh_sb", [128, 8], F32)
        ss = sb("ss", [128, 8], F32)
        ss2 = sb("ss2", [128, 8], F32)
        negh = sb("negh", [128, 8], F32)
        bnst = sb("bnst", [128, 8, 6], F32)
        gq = sb("gq", [128, 5, 8], F32)
        grs = sb("grs", [128, 8], F32)
        gve = sb("gve", [128, 8], F32)
        rinv = sb("rinv", [128, 8], F32)

        psA = ps("psA", [128, 1024], F32)
        psB = ps("psB", [128, 512], F32)
        psC = ps("psC", [128, 1024], F32)
        psD = ps("psD", [128, 1024], F32)
        psT32 = ps("psT", [128, 512], F32)
        psT = psT32[:, :].bitcast(BF16)
        bankap = {0: psA[:, 0:512], 1: psA[:, 512:1024], 2: psB[:, 0:512], 3: psC[:, 0:512], 4: psC[:, 512:1024],
                  5: psD[:, 0:512], 6: psD[:, 512:1024], 7: psT32[:, 0:512]}
        psTv = psT.rearrange("p (k t) -> p k t", k=8)

        sch = Sched(nc, es)
        if True:
            op = sch.op
            dma = sch.dma

            consts = [(ident_f[:], ident_d), (Dt[:].rearrange("p a b -> p (a b)"), Dt_d),
                      (cdT[:].rearrange("p a b -> p (a b)"), cdT_d),
                      (fgain[:], fg_d),
                      (sd[:], sd_d), (cd2[:], cd2_d), (bfar[:], bfar_d), (gpk[:], gpk_d), (gnpk[:], gnpk_d)]
            names = ["ident_f", "Dt", "cdT", "fgain", "sd", "cd2", "bfar", "gpk", "gnpk"]
            for (o, i), nm in zip(consts, names):
                dma("sp", o, i[:, :], [], [nm], "const")
            for nm in names:
                sch.writer[nm] = (("d", "const"), sch.dcnt["const"])
            for hb in range(2):
                xs, xn = xy[2 + hb], ("sin", 2 + hb)
                dma("sp", xs[:], biasT_d[:, hb * 1024:(hb + 1) * 1024], [], [xn], ("sin", 2 + hb))
                hi = bias_hi[:, hb * 4:(hb + 1) * 4, :].rearrange("p a b -> p (a b)")
                op("dve", lambda e: e.tensor_copy(hi, xs[:]), [xn], [("bias_hi", hb)])
            op("dve", lambda e: e.tensor_copy(ident[:], ident_f[:]), ["ident_f"], ["ident"])
            op("dve", lambda e: e.memset(negh[:], -0.5), [], ["negh"])
            op("dve", lambda e: e.memset(maskT[:], 0.0), [], ["maskT"])
            op("dve", lambda e: e.memset(maskT[0:64, 64:128], MASK_NEG), [], ["maskT"])
            op("dve", lambda e: e.memset(state[:], 0.0), [], ["state"])
            op("dve", lambda e: e.memset(state_bf[:], 0.0), [], ["state_bf"])
            op("dve", lambda e: e.memset(Vaug[:, :, :, 64:65], 1.0), [], ["Vaug_ones"])
            op("dve", lambda e: e.memset(PT[0][:], 0.0), [], ["PT0"])
            op("dve", lambda e: e.memset(PT[1][:], 0.0), [], ["PT1"])
            op("dve", lambda e: e.tensor_scalar(out=gnh[:], in0=gnpk[:], scalar1=0.5, scalar2=None,
                                               op0=ALU.mult), ["gnpk"], ["gnh"])

            F32v = lambda t2d: t2d.bitcast(F32)
            sin = [(xy[q][:], ["xy%d" % q]) for q in range(4)]
            for q in range(NW):
                wv = F32v(wbuf[q][:].rearrange("p k c -> p (k c)"))
                sin.append((wv[:, 0:1024], ["wbuf%d" % q]))
                sin.append((wv[:, 1024:2048], ["wbuf%d" % q]))
            sout = [(h_bf[:], ["h_bf"]), (u_bf[:], [("u_bf", 0), ("u_bf", 1)]), (v_bf[:], [("v_bf", 0), ("v_bf", 1)]),
                    (or_n[:], ["or_n"])]
            for s_ in range(2):
                sout.append((t_ma[:, s_, :], [("tm", True, s_, 0), ("tm", True, s_, 1)]))
                sout.append((t_mr[:, s_, :], [("tm", False, s_, 0), ("tm", False, s_, 1)]))
                sout.append((gg_r[:, s_, :], [("gg_r", s_, 0), ("gg_r", s_, 1)]))
            NSI, NSO = len(sin), len(sout)
            win_chunks = [(k, cb) for k in range(KT) for cb in range(7)]
            LA = NSI - 2
            for j in range(len(win_chunks) + LA):
                if j < len(win_chunks):
                    k, cb = win_chunks[j]
                    dma("sp", sin[j % NSI][0], win_d[k * 128:(k + 1) * 128, cb * 1024:(cb + 1) * 1024],
                        [], [("sin", j % NSI)], ("sin", j % NSI))
                if j >= LA:
                    jc = j - LA
                    k, cb = win_chunks[jc]
                    xs, xn = sin[jc % NSI][0], ("sin", jc % NSI)
                    st, sn = sout[jc % NSO][0], ("sout", jc % NSO)
                    if jc % 2 == 0:
                        op("dve", lambda en: en.tensor_scalar(
                            out=st, in0=xs, scalar1=gpk[:, k:k + 1], scalar2=None, op0=ALU.mult),
                           [xn, "gpk"], [sn])
                    else:
                        op("act", lambda en: en.activation(
                            out=st, in_=xs, func=AF.Copy, scale=gpk[:, k:k + 1]),
                           [xn, "gpk"], [sn])
                    dst = AP(wsc_d.tensor, wsc_d.offset + (2 * cb) * 128 * 4096 + k * 512,
                             [[4096, 128], [128 * 4096, 2], [1, 512]])
                    dma("act", dst, st.rearrange("p (g c) -> p g c", g=2),
                        [sn], [("wscq", jc % NSO)], ("sout", jc % NSO))
            for q, (_, names) in enumerate(sin):
                toks = [sch.writer[("sin", q)]] + list(sch.readers.get(("sin", q), []))
                for nm in names:
                    sch.readers.setdefault(nm, []).extend(toks)
            for q, (_, names) in enumerate(sout):
                toks = [sch.writer[("sout", q)]] + list(sch.readers.get(("sout", q), []))
                for nm in names:
                    sch.readers.setdefault(nm, []).extend(toks)

            res_chunks = ([(w_oa, woa_d, k, None, 0.5) for k in range(4)]
                          + [(w_or, wor_d, k, gnh, None) for k in range(8)]
                          + [(w_o, wo_d, k, None, 0.5) for k in range(8)])

            def resident_gen():
                n = len(res_chunks)
                for j in range(n + 1):
                    if j < n:
                        dst, dram, k, sc_t, const = res_chunks[j]
                        sl_ = 2 + j % 2
                        dma("sp", xy[sl_][:], dram[k * 128:(k + 1) * 128, :], [], ["xy%d" % sl_], ("x", sl_))
                    if j >= 1:
                        dst, dram, k, sc_t, const = res_chunks[j - 1]
                        sl_ = 2 + (j - 1) % 2
                        xs, xn = xy[sl_], "xy%d" % sl_
                        wname = ("w", id(dst), k)
                        if sc_t is not None:
                            op("dve", lambda en: en.tensor_scalar(
                                out=dst[:, k, :], in0=xs[:], scalar1=sc_t[:, k:k + 1], scalar2=None,
                                op0=ALU.mult), [xn, "gnh"], [wname])
                        elif j % 2 == 0:
                            op("dve", lambda en: en.tensor_scalar(
                                out=dst[:, k, :], in0=xs[:], scalar1=const, scalar2=None, op0=ALU.mult),
                               [xn], [wname])
                        else:
                            op("act", lambda en: en.activation(
                                out=dst[:, k, :], in_=xs[:], func=AF.Copy, scale=const), [xn], [wname])
                    yield

            def load_x(i):
                slot = i % 2
                dma("sp", xy[slot][:], x_d[i * 128:(i + 1) * 128, :], [], ["xy%d" % slot], ("x", slot))
                dma("sp", rot[slot][:], rot_d[i * 128:(i + 1) * 128, :], [], ["rot%d" % slot], ("rot", slot))

            def load_y(i):
                slot = 2 + (i % 2)
                dma("sp", xy[slot][:], x_d[i * 128:(i + 1) * 128, :], [], ["xy%d" % slot], ("x", slot))

            WORDER = [0, 1, 2, 3, 6, 7, 4, 8, 9, 5, 10, 11, 12, 13]
            WPOS = {g_: p_ for p_, g_ in enumerate(WORDER)}

            def load_w(pos):
                g = WORDER[pos]
                dma("sp", wbuf[pos % NW][:].rearrange("p k c -> p (k c)"), wsc_d[g, :, :],
                    [("wscq", q_) for q_ in range(NSO)], ["wbuf%d" % (pos % NW)], ("w", pos % NW))

            projbanks = [0, 1, 2, 3, 5, 6]
            pb = [0]

            def next_bank():
                b = projbanks[pb[0] % len(projbanks)]
                pb[0] += 1
                return b

            evq = [0]

            def evac_eng():
                evq[0] += 1
                return "act" if evq[0] % 2 else "dve"

            def copy_op(e, out, in_, reads, writes, scale=None):
                if e == "act":
                    if scale is None:
                        op("act", lambda en: en.activation(out=out, in_=in_, func=AF.Copy), reads, writes)
                    else:
                        op("act", lambda en: en.activation(out=out, in_=in_, func=AF.Copy, scale=scale),
                           reads, writes)
                else:
                    if scale is None:
                        op("dve", lambda en: en.tensor_copy(out, in_), reads, writes)
                    else:
                        op("dve", lambda en: en.tensor_scalar(out=out, in0=in_, scalar1=scale, scalar2=None,
                                                            op0=ALU.mult), reads, writes)

            def transposes(src, nblk, reads, tag):
                for k in range(nblk):
                    op("pe", lambda en, k=k: en.transpose(out=psTv[:, k, :], in_=src[:, k * 128:(k + 1) * 128],
                                                         identity=ident[:]),
                       reads + ["ident"], ["b7"], inc=(k == nblk - 1))

            def phaseA1(b):
                for s in range(2):
                    i = 2 * b + s
                    xs = xy[s]
                    xn = "xy%d" % s
                    op("dve", lambda en: en.scalar_tensor_tensor(out=t1[:, 0:512].bitcast(BF16), in0=xs[:], scalar=1.0,
                                                                in1=xs[:], op0=ALU.mult, op1=ALU.mult,
                                                                accum_out=ss[:, 0:1]), [xn], ["t1", "ss0"])
                    yield
                    op("pool", lambda en: en.tensor_scalar(out=ss[:, 1:2], in0=ss[:, 0:1], scalar1=1.0 / D,
                                                          scalar2=NORM_EPS, op0=ALU.mult, op1=ALU.add),
                       ["ss0"], ["ss1"])
                    op("pool", lambda en: en.tensor_tensor(out=ss[:, 2:3], in0=ss[:, 1:2], in1=negh[:, 0:1],
                                                          op=ALU.pow), ["ss1", "negh"], ["ss2"])
                    yield
                    yield
                    op("dve", lambda en: en.tensor_scalar(out=h_bf[:], in0=xs[:], scalar1=ss[:, 2:3],
                                                         scalar2=None, op0=ALU.mult), [xn, "ss2"], ["h_bf"])
                    yield
                    transposes(h_bf, 8, ["h_bf"], "h")
                    op("dve", lambda en: en.tensor_copy(hT[:, :, s * 128:(s + 1) * 128], psTv), ["b7"], [("hT", s)])
                    yield

            def phaseA2(b, groups, banks, gate=None):
                projbanks[:] = banks
                deferred = []
                qT_a = qT_a2[b % 2]
                gg_a = gg_a2[b % 2]
                for g in groups:
                    pos = WPOS[g]
                    wb = wbuf[pos % NW]
                    wn = "wbuf%d" % (pos % NW)
                    if g == 10 and gate is not None:
                        yield from wait(gate)
                    if g < 2:
                        for fp in range(2):
                            bk = next_bank()
                            for f2 in range(2):
                                ft = 2 * fp + f2
                                P = bankap[bk][:, f2 * 256:(f2 + 1) * 256]
                                for k in range(KT):
                                    op("pe", lambda en, k=k: en.matmul(out=P, lhsT=wb[:, k, ft * 128:(ft + 1) * 128],
                                                                      rhs=hT[:, k, :], start=(k == 0), stop=(k == KT - 1)),
                                       [wn, ("hT", 0), ("hT", 1)], ["b%d" % bk], inc=(k == KT - 1))
                            yield
                            e = evac_eng()
                            P2 = bankap[bk]
                            if g == 0:
                                copy_op(e, qT_a[:, 2 * fp:2 * fp + 2, :].rearrange("p a b -> p (a b)"), P2, ["b%d" % bk],
                                        [("qT_a", 0, 2 * fp), ("qT_a", 0, 2 * fp + 1)], scale=0.125)
                            else:
                                r0 = ((2 * b) % RING) * 128
                                copy_op(e, kT_a[:, 2 * fp:2 * fp + 2, r0:r0 + 256],
                                        P2.rearrange("p (a b) -> p a b", a=2), ["b%d" % bk],
                                        [("kT_a", 2 * fp + a_, (2 * b + t_) % RING) for a_ in range(2) for t_ in range(2)])
                    else:
                        for s in range(2):
                            i = 2 * b + s
                            bk = next_bank()
                            P = bankap[bk]
                            bn = "b%d" % bk
                            for k in range(KT):
                                op("pe", lambda en, k=k: en.matmul(out=P, lhsT=hT[:, k, s * 128:(s + 1) * 128],
                                                                  rhs=wb[:, k, :], start=(k == 0), stop=(k == KT - 1)),
                                   [wn, ("hT", s)], [bn], inc=(k == KT - 1))
                            yield
                            if deferred and g != 5:
                                deferred.pop(0)()
                            if g == 2:
                                sl = i % RING
                                copy_op(evac_eng(), Vaug[:, sl, :, 0:64],
                                        cap(P, [[64, 8], [1, 64]]), [bn], [("Vaug", sl)])
                            elif g in (3, 8, 9):
                                op("act", lambda en: en.activation(out=t1[:], in_=P, func=AF.Tanh, scale=0.5),
                                   [bn], ["t1"])
                                yield
                                if g == 3:
                                    dst = gg_a[:, s, :]
                                    dn = ("gg_a", 0, s)
                                else:
                                    dst = gg_r[:, s, (g - 8) * 512:(g - 7) * 512]
                                    dn = ("gg_r", s, g - 8)
                                op("dve", lambda en: en.scalar_tensor_tensor(out=dst, in0=t1[:], scalar=1.0, in1=P,
                                                                            op0=ALU.add, op1=ALU.mult),
                                   ["t1", bn], [dn])
                            elif g in (4, 5):
                                qk_rot = qk_rot2[:, s, :]
                                rt = rot[s]
                                rn = "rot%d" % s
                                cc = cap(rt[:, 0:64], [[0, 8], [1, 64]])
                                ssn = cap(rt[:, 64:128], [[0, 8], [32, 2], [1, 32]])
                                Pv = cap(P, [[64, 8], [1, 64]])
                                Psw = cap(P, [[64, 8], [-32, 2], [1, 32]], off=32)
                                op("dve", lambda en: en.tensor_tensor(out=cap(t1[:], [[64, 8], [1, 64]]), in0=Pv,
                                                                     in1=cc, op=ALU.mult), [bn, rn], ["t1"])
                                op("dve", lambda en: en.tensor_tensor(out=cap(t2[:], [[64, 8], [32, 2], [1, 32]]),
                                                                     in0=Psw, in1=ssn, op=ALU.mult), [bn, rn], ["t2"])
                                off = (g - 4) * 512
                                op("dve", lambda en: en.tensor_tensor(out=qk_rot[:, off:off + 512], in0=t1[:],
                                                                     in1=t2[:], op=ALU.add),
                                   ["t1", "t2"], [("qk_rot", s, g - 4)])
                                if g == 5:
                                    op("dve", lambda en: en.tensor_tensor(
                                        out=cap(k_dec[:, s, :], [[64, 8], [1, 64]]),
                                        in0=cap(qk_rot[:, 512:1024], [[64, 8], [1, 64]]),
                                        in1=cap(sd[:], [[1, 8], [0, 64]]), op=ALU.mult),
                                       [("qk_rot", s, 1), "sd"], [("k_dec", s)])
                                    def rot_T(s=s, qk_rot=qk_rot):
                                        transposes(qk_rot, 8, [("qk_rot", s, 0), ("qk_rot", s, 1)], "r")
                                        sl = slice(s * 128, (s + 1) * 128)
                                        op("act", lambda en: en.activation(out=qT_r[:, :, sl], in_=psTv[:, 0:4, :],
                                                                           func=AF.Copy), ["b7"], [("qT_r", s)])
                                        op("act", lambda en: en.activation(out=kT_r[:, :, sl], in_=psTv[:, 4:8, :],
                                                                           func=AF.Copy), ["b7"], [("kT_r", s)])
                                        op("dve", lambda en: en.tensor_tensor(out=qcT_r[:, :, sl], in0=psTv[:, 0:4, :],
                                                                             in1=cdT[:], op=ALU.mult),
                                           ["b7", "cdT"], [("qcT_r", s)])
                                    deferred.append(rot_T)
                            elif g in (6, 7):
                                copy_op(evac_eng(), v_r[:, s, (g - 6) * 512:(g - 5) * 512], P, [bn],
                                        [("v_r", s, g - 6)])
                            else:
                                tgt = t_ma if g < 12 else t_mr
                                hh = (g - 10) % 2
                                op("act", lambda en: en.activation(out=tgt[:, s, hh * 512:(hh + 1) * 512], in_=P,
                                                                   func=AF.Tanh, scale=0.5),
                                   [bn], [("tm", g < 12, s, hh)])
                    if pos + NW < NG:
                        load_w(pos + NW)
                    yield
                while deferred:
                    deferred.pop(0)()

            ogT_a2 = [ogT_a, ogT_a_b]
            ogT_r2 = [ogT_r, ogT_r_b]

            def O_h(h):
                c0 = (h % 4) * 65
                return psC[:, c0:c0 + 65]

            def ORh(h):
                c0 = (h % 2) * 512 + (h // 2) * 128
                return psD[:, c0:c0 + 128]

            def att(i):
                s = i % 2
                sl = slice(s * 128, (s + 1) * 128)
                kts = [kt for kt in range(5) if i - 4 + kt >= 0]
                far = [kt for kt in kts if kt < 3]
                bp = (i // 2) % 2
                qT_a = qT_a2[bp]
                gg_a = gg_a2[bp]

                def nearbank(h):
                    if h % 2 == 0:
                        return psB[:, 0:256], "b2"
                    return psC[:, 512:768], "b4"

                def scores(h):
                    hp, r0 = h // 2, 64 * (h % 2)
                    NB_, nbn = nearbank(h)
                    fb = "b%d" % (h % 2)
                    FA = psA[:, (h % 2) * 512:(h % 2 + 1) * 512]
                    c0 = 0 if 3 in kts else 128
                    masked = 0 in kts
                    if masked:
                        op("pe", lambda en: en.matmul(out=FA[:, 0:128], lhsT=ident[:], rhs=maskT[:], start=True,
                                                      stop=False), ["ident", "maskT"], [fb], inc=False)
                    op("pe", lambda en: en.matmul(out=NB_[:, c0:256], lhsT=ident[:], rhs=bias_hi[:, h, c0:256],
                                                  start=True, stop=False), [("bias_hi", h // 4), "ident"], [nbn],
                       inc=False)
                    first_far = not masked
                    for kt in kts:
                        ring = ((i - 4 + kt) % RING) * 128
                        if kt < 3:
                            out = FA[:, kt * 128:(kt + 1) * 128]
                            bn = fb
                            st, sp = first_far, (kt == far[-1])
                            first_far = False
                        else:
                            out = NB_[:, (kt - 3) * 128:(kt - 2) * 128]
                            bn = nbn
                            st, sp = False, (kt == 4)
                        op("pe", lambda en: en.matmul(out=out, lhsT=kT_a[r0:r0 + 64, hp, ring:ring + 128],
                                                      rhs=qT_a[r0:r0 + 64, hp, sl], start=st, stop=sp),
                           [("kT_a", hp, (i - 4 + kt) % RING), ("qT_a", 0, hp)], [bn],
                           inc=(kt == 4 or (far and kt == far[-1])))

                def softmax(h):
                    pt = PT[h % 2]
                    pn = "PT%d" % (h % 2)
                    SA = psA[:, (h % 2) * 512:(h % 2 + 1) * 512]
                    b01 = "b%d" % (h % 2)
                    if far:
                        a, bnd = far[0], far[-1] + 1
                        op("act", lambda en: en.activation(
                            out=pt[:, a:bnd, :].rearrange("p a b -> p (a b)"), in_=SA[:, a * 128:bnd * 128],
                            func=AF.Exp, bias=bfar[:, h:h + 1]), [b01, "bfar"], [pn])
                    c0 = 0 if 3 in kts else 128
                    NB_, nbn = nearbank(h)
                    op("act", lambda en: en.activation(
                        out=pt[:, 3:5, :].rearrange("p a b -> p (a b)")[:, c0:256], in_=NB_[:, c0:256], func=AF.Exp),
                       [nbn], [pn])

                def pv(h):
                    pt = PT[h % 2]
                    pn = "PT%d" % (h % 2)
                    first = True
                    for kt in kts:
                        slot = (i - 4 + kt) % RING
                        last = (kt == 4)
                        op("pe", lambda en, first=first: en.matmul(out=O_h(h), lhsT=pt[:, kt, :],
                                                                 rhs=Vaug[:, slot, h, :], start=first, stop=last),
                           [pn, ("Vaug", slot), "Vaug_ones"], ["b3"], inc=last)
                        first = False

                def epilogue(half):
                    hs = slice(half * 4, half * 4 + 4)
                    cs = slice(half * 256, half * 256 + 256)
                    op("dve", lambda en: en.reciprocal(out=rinv[:, hs], in_=cap(psC[:, 0:512], [[65, 4]], off=64)),
                       ["b3"], [("rinv", half)])
                    op("dve", lambda en: en.tensor_tensor(
                        out=cap(oa_tmp[:, cs], [[64, 4], [1, 64]]),
                        in0=cap(psC[:, 0:512], [[65, 4], [1, 64]]),
                        in1=cap(rinv[:, hs], [[1, 4], [0, 64]]), op=ALU.mult),
                       ["b3", ("rinv", half)], [("oa_tmp", half)])
                    op("dve", lambda en: en.tensor_tensor(out=og_a[:, cs], in0=oa_tmp[:, cs], in1=gg_a[:, s, cs],
                                                         op=ALU.mult),
                       [("oa_tmp", half), ("gg_a", 0, s)], [("og_a", half)])

                scores(0)
                for h in range(8):
                    if h + 1 < 8:
                        scores(h + 1)
                    softmax(h)
                    yield
                    pv(h)
                    if h % 4 == 3:
                        epilogue(h // 4)
                    yield
                dst = ogT_a2[i % 2]
                transposes(og_a, 4, [("og_a", 0), ("og_a", 1)], "a")
                op("dve", lambda en: en.tensor_copy(dst[:], psTv[:, 0:4, :]), ["b7"], [("ogT_a", i % 2)])
                yield

            def ret(i):
                s = i % 2
                sl = slice(s * 128, (s + 1) * 128)
                for par in range(2):
                    for hh in range(4):
                        h = 2 * hh + par
                        hp, r0 = h // 2, 64 * (h % 2)
                        op("pe", lambda en: en.matmul(out=ORh(h), lhsT=kT_r[r0:r0 + 64, hp, sl],
                                                      rhs=qT_r[r0:r0 + 64, hp, sl], start=True, stop=True),
                           [("kT_r", s), ("qT_r", s)], ["b%d" % (5 + par)], inc=(hh == 3))
                yield
                for par in range(2):
                    op("dve", lambda en: en.tensor_tensor(
                        out=Pr[:, par * 4:(par + 1) * 4, :].rearrange("p a b -> p (a b)"),
                        in0=psD[:, par * 512:(par + 1) * 512],
                        in1=Dt[:, par * 4:(par + 1) * 4, :].rearrange("p a b -> p (a b)"), op=ALU.mult),
                       ["b%d" % (5 + par), "Dt"], [("Pr", par)])
                yield
                cross = i > 0
                for h in range(8):
                    hp, r0 = h // 2, 64 * (h % 2)
                    bn = "b%d" % (5 + h % 2)
                    op("pe", lambda en: en.matmul(out=ORh(h), lhsT=Pr[:, (h % 2) * 4 + h // 2, :],
                                                  rhs=v_r[:, s, h * 128:(h + 1) * 128], start=True, stop=not cross),
                       [("Pr", h % 2), ("v_r", s, h // 4)], [bn], inc=(not cross))
                    if cross:
                        op("pe", lambda en: en.matmul(out=ORh(h), lhsT=qcT_r[r0:r0 + 64, hp, sl],
                                                      rhs=state_bf[r0:r0 + 64, hp, :], start=False, stop=True),
                           [("qcT_r", s), "state_bf"], [bn], inc=True)
                yield
                for h in range(8):
                    op("dve", lambda en: en.bn_stats(out=bnst[:, h, :], in_=ORh(h)),
                       ["b%d" % (5 + h % 2)], ["bnst"])
                    if h % 4 == 3:
                        yield
                me, mo = bnst[:, :, 1], bnst[:, :, 4]
                M2e, M2o = bnst[:, :, 2], bnst[:, :, 5]
                dv = lambda fn, r, w: op("dve", fn, r, w)
                dv(lambda en: en.tensor_tensor(out=gq[:, 0, :], in0=me, in1=mo, op=ALU.add), ["bnst"], ["gq0"])
                dv(lambda en: en.tensor_tensor(out=gq[:, 1, :], in0=me, in1=mo, op=ALU.subtract), ["bnst"], ["gq1"])
                dv(lambda en: en.tensor_tensor(out=gq[:, 2, :], in0=gq[:, 1, :], in1=gq[:, 1, :], op=ALU.mult),
                   ["gq1"], ["gq2"])
                dv(lambda en: en.tensor_tensor(out=gq[:, 3, :], in0=M2e, in1=M2o, op=ALU.add), ["bnst"], ["gq3"])
                dv(lambda en: en.tensor_scalar(out=gq[:, 3, :], in0=gq[:, 3, :], scalar1=1.0 / 128, scalar2=GN_EPS,
                                               op0=ALU.mult, op1=ALU.add), ["gq3"], ["gq3"])
                dv(lambda en: en.scalar_tensor_tensor(out=gve[:], in0=gq[:, 2, :], scalar=0.25, in1=gq[:, 3, :],
                                                      op0=ALU.mult, op1=ALU.add), ["gq2", "gq3"], ["gve"])
                yield
                yield
                op("pool", lambda en: en.tensor_tensor(out=grs[:], in0=gve[:], in1=negh[:], op=ALU.pow),
                   ["gve", "negh"], ["grs"])
                yield
                yield
                yield
                dv(lambda en: en.scalar_tensor_tensor(out=gq[:, 4, :], in0=gq[:, 0, :], scalar=-0.5, in1=grs[:],
                                                      op0=ALU.mult, op1=ALU.mult), ["gq0", "grs"], ["gq4"])
                dv(lambda en: en.tensor_scalar(out=gq[:, 1, :], in0=gq[:, 0, :], scalar1=0.5, scalar2=None,
                                               op0=ALU.mult), ["gq0", "gq1"], ["gq1"])
                yield
                for hh in range(4):
                    h = hh
                    op("act", lambda en: en.activation(out=or_n[:, h * 128:(h + 1) * 128], in_=ORh(h),
                                                       func=AF.Identity, scale=grs[:, h:h + 1],
                                                       bias=gq[:, 4, h:h + 1]),
                       ["b%d" % (5 + h % 2), "gq4", "grs"], [("or_n", h)])
                    h = 4 + hh
                    op("dve", lambda en: en.tensor_scalar(out=or_n[:, h * 128:(h + 1) * 128], in0=ORh(h),
                                                         scalar1=gq[:, 1, h:h + 1], scalar2=grs[:, h:h + 1],
                                                         op0=ALU.subtract, op1=ALU.mult),
                       ["b%d" % (5 + h % 2), "gq1", "grs"], [("or_n", h)])
                    if hh % 2 == 1:
                        yield
                yield
                yield
                op("dve", lambda en: en.tensor_tensor(out=or_n[:], in0=or_n[:], in1=gg_r[:, s, :], op=ALU.mult),
                   [("or_n", h_) for h_ in range(8)] + [("gg_r", s, 0), ("gg_r", s, 1)],
                   ["or_n"] + [("or_n", h_) for h_ in range(8)])
                yield
                yield
                for h in range(8):
                    hp, p0 = h // 2, 64 * (h % 2)
                    op("pe", lambda en: en.matmul(out=psD[p0:p0 + 64, hp * 128:(hp + 1) * 128],
                                                  lhsT=k_dec[:, s, h * 64:(h + 1) * 64],
                                                  rhs=v_r[:, s, h * 128:(h + 1) * 128], start=True, stop=True),
                       [("k_dec", s), ("v_r", s, h // 4)], ["b5"], inc=(h == 7))
                dst = ogT_r2[i % 2]
                transposes(or_n, 8, ["or_n"] + [("or_n", h_) for h_ in range(8)], "r2")
                op("dve", lambda en: en.tensor_copy(dst[:], psTv), ["b7"], [("ogT_r", i % 2)])
                yield
                for hp in range(4):
                    op("dve", lambda en: en.scalar_tensor_tensor(
                        out=state[:, hp, :], in0=state[:, hp, :], scalar=cd2[:, hp:hp + 1],
                        in1=psD[:, hp * 128:(hp + 1) * 128], op0=ALU.mult, op1=ALU.add),
                       ["state", "cd2", "b5"], ["state"])
                yield
                op("pool", lambda en: en.tensor_copy(state_bf[:], state[:]), ["state"], ["state_bf"])
                yield

            def tail(i):
                s = i % 2
                oa = ogT_a2[i % 2]
                orr = ogT_r2[i % 2]
                P4 = psC[:, 512:1024]
                for n in range(2):
                    cs = slice(n * 512, (n + 1) * 512)
                    for k in range(4):
                        op("pe", lambda en: en.matmul(out=P4, lhsT=oa[:, k, :], rhs=w_oa[:, k, cs],
                                                      start=(k == 0), stop=(k == 3)),
                           [("ogT_a", i % 2), ("w", id(w_oa), k)], ["b4"], inc=(k == 3))
                    yield
                    op("dve", lambda en: en.scalar_tensor_tensor(out=u_bf[:, cs], in0=t_ma[:, s, cs], scalar=1.0, in1=P4,
                                                                op0=ALU.add, op1=ALU.mult),
                       [("tm", True, s, n), "b4"], [("u_bf", n)])
                    yield
                    for k in range(8):
                        op("pe", lambda en: en.matmul(out=P4, lhsT=orr[:, k, :], rhs=w_or[:, k, cs],
                                                      start=(k == 0), stop=(k == 7)),
                           [("ogT_r", i % 2), ("w", id(w_or), k)], ["b4"], inc=(k == 7))
                    yield
                    op("dve", lambda en: en.scalar_tensor_tensor(out=v_bf[:, cs], in0=t_mr[:, s, cs], scalar=1.0, in1=P4,
                                                                op0=ALU.add, op1=ALU.mult),
                       [("tm", False, s, n), "b4"], [("v_bf", n)])
                    op("dve", lambda en: en.tensor_tensor(out=u_bf[:, cs], in0=u_bf[:, cs], in1=v_bf[:, cs], op=ALU.add),
                       [("u_bf", n), ("v_bf", n)], [("u_bf", n)])
                    yield
                transposes(u_bf, 8, [("u_bf", 0), ("u_bf", 1)], "m")
                op("act", lambda en: en.activation(out=mixedT[:], in_=psTv, func=AF.Copy), ["b7"], ["mixedT"])
                yield
                yb = xy[2 + s]
                yn = "xy%d" % (2 + s)
                for n in range(2):
                    cs = slice(n * 512, (n + 1) * 512)
                    for k in range(8):
                        op("pe", lambda en: en.matmul(out=P4, lhsT=mixedT[:, k, :], rhs=w_o[:, k, cs],
                                                      start=(k == 0), stop=(k == 7)),
                           ["mixedT", ("w", id(w_o), k)], ["b4"], inc=(k == 7))
                    yield
                    op("dve", lambda en: en.tensor_tensor(out=yb[:, cs], in0=P4, in1=yb[:, cs], op=ALU.add),
                       ["b4", yn], [yn])
                    yield
                op("act", lambda en: en.activation(out=v_bf[:], in_=yb[:], func=AF.Square, accum_out=ss2[:, 0:1]),
                   [yn], [("v_bf", 0), ("v_bf", 1), "ss2_0"])
                yield
                op("pool", lambda en: en.tensor_scalar(out=ss2[:, 1:2], in0=ss2[:, 0:1], scalar1=1.0 / D,
                                                      scalar2=NORM_EPS, op0=ALU.mult, op1=ALU.add),
                   ["ss2_0"], ["ss2_1"])
                op("pool", lambda en: en.tensor_tensor(out=ss2[:, 2:3], in0=ss2[:, 1:2], in1=negh[:, 0:1],
                                                      op=ALU.pow), ["ss2_1", "negh"], ["ss2_2"])
                yield
                op("dve", lambda en: en.scalar_tensor_tensor(out=yb[:], in0=yb[:], scalar=ss2[:, 2:3], in1=fgain[:],
                                                            op0=ALU.mult, op1=ALU.mult),
                   [yn, "ss2_2", "fgain"], [yn])
                dma("sp", out_d[i * 128:(i + 1) * 128, :], yb[:], [yn], [], ("out", s))
                yield

            rrp = [0]

            def run(chains, wts=(1, 1, 1, 1)):
                chains = list(chains)
                chains0 = list(chains)
                rrp[0] = 0
                rt = {id(c): 0.0 for c in chains}
                while chains:
                    progressed = False
                    if GREEDY:
                        order = sorted(chains, key=lambda c_: rt[id(c_)])
                    elif POLICY.get("H") is not None:
                        k_ = rrp[0] % len(chains)
                        rr = chains[k_:] + chains[:k_]
                        lo = min(rt[id(c_)] for c_ in chains)
                        order = ([c_ for c_ in rr if rt[id(c_)] <= lo + POLICY["H"]]
                                 + sorted([c_ for c_ in rr if rt[id(c_)] > lo + POLICY["H"]], key=lambda c_: rt[id(c_)]))
                    else:
                        slots = [c_ for ci, c_ in enumerate(chains0) if c_ in chains
                                 for _ in range(wts[min(ci, len(wts) - 1)])]
                        k_ = rrp[0] % len(slots)
                        order = slots[k_:] + slots[:k_]
                    rrp[0] += 1
                    for c in order:
                        sch.step_end = None
                        try:
                            r = next(c)
                        except StopIteration:
                            chains.remove(c)
                            progressed = True
                            break
                        if r == "blocked":
                            continue
                        if sch.step_end is not None:
                            rt[id(c)] = sch.step_end
                        progressed = True
                        break
                    assert progressed

            done = set()

            def mark(gen, key):
                yield from gen
                done.add(key)

            def wait(keys):
                while not all(k in done for k in keys):
                    yield "blocked"

            ATT_G = [0, 1, 2, 3]
            BK3 = [0, 1, 2, 3, 5, 6]
            REST_G = list(range(4, NG))

            def seq(*gens):
                for g in gens:
                    yield from g

            load_x(0)
            load_x(1)
            for g in range(NW):
                load_w(g)
            if stop >= 1:
                run([seq(phaseA1(0), phaseA2(0, WORDER, BK3)), resident_gen()])
            for b in range(NB):
                if stop < 2:
                    break
                nxt = b + 1 < NB
                if nxt:
                    load_x(2 * b + 2)
                    load_x(2 * b + 3)
                    for g in range(NW):
                        load_w(g)
                t0, t1_ = 2 * b, 2 * b + 1
                run([seq(att(t0), att(t1_)), seq(ret(t0), ret(t1_))] + ([phaseA1(b + 1)] if nxt else []),
                    wts=POLICY.get("w_att", (1, 1, 1)))
                load_y(t0)
                load_y(t1_)
                tails = seq(tail(t0), mark(tail(t1_), ("tail", t1_)))
                if nxt:
                    run([tails, phaseA2(b + 1, WORDER, BK3, gate=[("tail", t1_)])], wts=POLICY.get("w3", (1, 1)))
                else:
                    run([tails])
            sch.finish("sp")

    print("[kernel] build: %d instructions, %d waits, %d dma sems, model %.0f us" % (
        sch.nins, sch.nwait, len(sch.dsem), sch.makespan))
    if DRY:
        return sch.makespan
    return nc


def _tables(NT):
    H = 8
    hs = np.arange(H, dtype=np.float64)
    lg = np.log1p(-np.exp2(-5.0 - hs))
    j = np.arange(128, dtype=np.float64)
    kk = j[:, None]
    qq = j[None, :]
    same = (kk // 64) == (qq // 64)
    past = (kk // 64) < (qq // 64)
    dist = np.abs(qq - kk)
    Dt = np.zeros((128, H, 128))
    for h in range(H):
        Dt[:, (h % 2) * 4 + h // 2, :] = np.where(same | past, np.exp(dist * lg[h]), 0.0) * 0.125
    cdT = np.zeros((128, 4, 128))
    cd2 = np.zeros((128, 4))
    for p in range(128):
        for pr in range(4):
            h = 2 * pr + p // 64
            cdT[p, pr, :] = 0.125 * np.exp((j + 1.0) * lg[h])
            cd2[p, pr] = np.exp(128.0 * lg[h])
    sd = np.exp((127.0 - j)[:, None] * lg[None, :])
    half = 32
    inv_freq = np.power(10000.0, -np.arange(half, dtype=np.float64) / half)
    pos = np.arange(NT * 128, dtype=np.float64)
    ang = pos[:, None] * inv_freq[None, :]
    c, sn = np.cos(ang), np.sin(ang)
    rot = np.concatenate([c, c, -sn, sn], axis=1)
    ident = np.eye(128)
    f = lambda a: np.ascontiguousarray(a, dtype=np.float32)
    return dict(Dt=f(Dt.reshape(128, 1024)), cdT=f(cdT.reshape(128, 512)), cd2=f(cd2), sd=f(sd), rot=f(rot),
                ident=f(ident))


def _bias_tables(rel_bias):
    k = np.arange(128)[:, None]
    q = np.arange(128)[None, :]
    biasT = np.zeros((128, 8, 2, 128), np.float32)
    for jj in range(2):
        dist = q - k + 128 * (1 - jj)
        idx = np.clip(dist, -128, 128) + 128
        biasT[:, :, jj, :] = np.transpose(rel_bias[:, idx], (1, 0, 2))
    m = (k >= 64) & (q < 64)
    biasT[:, :, 1, :] = np.where(m[:, None, :], np.float32(MASK_NEG), biasT[:, :, 1, :])
    bfar = np.ascontiguousarray(np.broadcast_to(rel_bias[:, 256][None, :], (128, 8)), dtype=np.float32)
    return np.ascontiguousarray(biasT.reshape(128, 2048)), bfar


def make_in_maps(x, norm_gain, w_in, rel_bias, gn_gain, w_out_attn, w_out_ret, w_out, final_gain, NT):
    f = lambda a: np.ascontiguousarray(np.asarray(a), dtype=np.float32)
    tabs = _tables(NT)
    biasT, bfar = _bias_tables(f(rel_bias)[0])
    shared = dict(
        w_in=f(w_in)[0], w_oa=f(w_out_attn)[0], w_or=f(w_out_ret)[0], w_o=f(w_out)[0],
        gain_pk=f(f(norm_gain)[0].reshape(8, 128).T), gn_pk=f(f(gn_gain)[0].reshape(8, 128).T),
        fgain=f(np.broadcast_to(f(final_gain)[None, :], (128, D))),
        biasT=biasT, bfar=bfar, **tabs)
    xs = f(x)
    maps = []
    for c in range(xs.shape[0]):
        m = dict(shared)
        m["x"] = np.ascontiguousarray(xs[c, :NT * 128, :])
        maps.append(m)
    return maps


_NC_CACHE = {}


def kernel(x, norm_gain, w_in, rel_bias, gn_gain, w_out_attn, w_out_ret, w_out, final_gain):
    x = np.asarray(x)
    B, S, _ = x.shape
    NT = S // 128
    NB = NT // 2
    if NB not in _NC_CACHE:
        _NC_CACHE[NB] = build(NB)
    nc = _NC_CACHE[NB]
    maps = make_in_maps(x, norm_gain, w_in, rel_bias, gn_gain, w_out_attn, w_out_ret, w_out, final_gain, NT)
    res = run_bass_kernel_spmd(nc, maps, core_ids=list(range(B)))
    out = np.stack([np.asarray(r["out"]) for r in res.results], axis=0)
    return out.astype(np.float32, copy=False)
```

```python
import numpy as np
from contextlib import ExitStack

import concourse.bass as bass
import concourse.mybir as mybir
from concourse.ap import AP
from concourse.bass_utils import run_bass_kernel_spmd

F32 = mybir.dt.float32
BF16 = mybir.dt.bfloat16
AF = mybir.ActivationFunctionType
ALU = mybir.AluOpType

D = 1024
KT = 8
RING = 6
NG = 14
NORM_EPS = 1e-6
GN_EPS = 1e-5
MASK_NEG = -30000.0


def cap(ap, free, off=0):
    return AP(ap.tensor, ap.offset + off, [list(ap.ap[0])] + [list(f) for f in free])


class _DryIns:
    def then_inc(self, *a, **kw):
        return self


_DRYINS = _DryIns()
DRY = False
POLICY = {}


class _Probe:
    def __init__(self, eng):
        self.eng = eng
        self.n = 128
        self.name = None

    def __getattr__(self, name):
        real = getattr(self.eng, name) if self.eng is not None else (lambda *a, **kw: _DRYINS)

        def f(*a, **kw):
            self.name = name
            if name == "matmul":
                ap = kw.get("rhs")
            elif name == "bn_stats":
                ap = kw.get("in_")
            else:
                ap = kw.get("out", a[0] if a else None)
            try:
                self.n = int(ap.free_size())
            except Exception:
                self.n = 128
            return real(*a, **kw)
        return f


class Sched:
    def __init__(self, nc, es):
        self.nc = nc
        self.es = es
        self.eng = {"pe": nc.tensor, "act": nc.scalar, "dve": nc.vector, "pool": nc.gpsimd, "sp": nc.sync}
        self.sem = {}
        self.cnt = {}
        for k in self.eng:
            self.sem[k] = es.enter_context(nc.semaphore("sem_" + k))
            self.cnt[k] = 0
        self.seen = {k: {} for k in self.eng}
        self.writer = {}
        self.readers = {}
        self.dsem = {}
        self.dcnt = {}
        self.pending = {k: False for k in self.eng}
        self.nwait = 0
        self.nins = 0
        self.t_eng = {k: 0.0 for k in self.eng}
        self.tok_end = {}
        self.step_end = None

    def _cost(self, e, name, n):
        if e == "pe":
            return max(n, 64) / 2400.0 + 0.015
        if e == "act":
            return 0.22 + n / 1200.0
        if e == "dve":
            return 0.07 + n / 960.0
        if e == "pool":
            return 0.9 if name == "tensor_tensor" and n <= 8 else 0.18 + n / 800.0
        return 0.05

    def _time(self, e, deps, cost, tok, done_extra=0.0):
        ready = 0.0
        for t in deps:
            te = self.tok_end.get(t, 0.0) + (0.15 if t[0] != e else 0.0)
            if te > ready:
                ready = te
        start = max(ready, self.t_eng[e])
        self.t_eng[e] = start + cost
        end = start + cost + done_extra
        if self.tok_end.get(tok, 0.0) < end:
            self.tok_end[tok] = end
        if self.step_end is None or self.step_end < end:
            self.step_end = end

    def _semobj(self, k):
        if isinstance(k, tuple):
            return self.dsem[k[1]]
        return self.sem[k]

    def _deps(self, reads, writes):
        deps = []
        for b in reads:
            if b in self.writer:
                deps.append(self.writer[b])
        for b in writes:
            if b in self.writer:
                deps.append(self.writer[b])
            deps.extend(self.readers.get(b, ()))
        return deps

    def _wait(self, e, deps):
        best = {}
        for (k, v) in deps:
            if best.get(k, 0) < v:
                best[k] = v
        for k, v in best.items():
            if k == e and e == "pe":
                continue
            if self.seen[e].get(k, 0) >= v:
                continue
            if not DRY:
                self.eng[e].wait_ge(self._semobj(k), v)
            self.nwait += 1
            self.seen[e][k] = v

    def _record(self, tok, reads, writes):
        for b in reads:
            self.readers.setdefault(b, []).append(tok)
        for b in writes:
            self.writer[b] = tok
            self.readers[b] = []

    @staticmethod
    def _split(reads, writes):
        isb = lambda b: isinstance(b, str) and len(b) == 2 and b[0] == "b" and b[1].isdigit()
        r2 = [b for b in reads if not isb(b)]
        w2 = list(writes) + [b for b in reads if isb(b)]
        return r2, w2

    def op(self, e, fn, reads=(), writes=(), inc=True):
        reads, writes = self._split(reads, writes)
        deps = self._deps(reads, writes)
        self._wait(e, deps)
        pr = _Probe(None if DRY else self.eng[e])
        ins = fn(pr)
        self.nins += 1
        if inc:
            self.cnt[e] += 1
            ins.then_inc(self.sem[e], 1)
            tok = (e, self.cnt[e])
            self.pending[e] = False
        else:
            tok = (e, self.cnt[e] + 1)
            self.pending[e] = True
        self._time(e, deps, self._cost(e, pr.name, pr.n), tok)
        self._record(tok, reads, writes)
        return ins

    def dma(self, q, out, in_, reads, writes, key):
        if key not in self.dsem:
            self.dsem[key] = None if DRY else self.es.enter_context(self.nc.semaphore("dsem_%d" % len(self.dsem)))
            self.dcnt[key] = 0
        deps = self._deps(reads, writes)
        self._wait(q, deps)
        ins = _DRYINS if DRY else self.eng[q].dma_start(out=out, in_=in_)
        self.nins += 1
        self.dcnt[key] += 16
        ins.then_inc(self.dsem[key], 16)
        tok = (("d", key), self.dcnt[key])
        try:
            nbytes = int(out.free_size()) * 128 * 4
        except Exception:
            nbytes = 1 << 19
        self._time(q, deps, 0.06, tok, done_extra=2.0 + nbytes / 250e3)
        self._record(tok, reads, writes)
        return ins

    def finish(self, q="sp"):
        self.makespan = max(self.t_eng.values())
        if DRY:
            return
        for key, sem in self.dsem.items():
            if self.dcnt[key] > 0:
                self.eng[q].wait_ge(sem, self.dcnt[key])
        for e in self.eng:
            assert not self.pending[e], e
            if e != q and self.cnt[e] > 0:
                self.eng[q].wait_ge(self.sem[e], self.cnt[e])


GREEDY = False


def build(NB, dbg_names=(), stop=99):
    NT = 2 * NB
    S = NT * 128
    nc = bass.Bass("TRN2", target_bir_lowering=False, dynamic_dma_scratch_size=1024)

    def din(name, shape, dt=F32):
        return nc.dram_tensor(name, shape, dt, kind="ExternalInput").ap()

    x_d = din("x", [S, D])
    win_d = din("w_in", [D, 7168])
    woa_d = din("w_oa", [512, D])
    wor_d = din("w_or", [D, D])
    wo_d = din("w_o", [D, D])
    gpk_d = din("gain_pk", [128, 8])
    gnpk_d = din("gn_pk", [128, 8])
    fg_d = din("fgain", [128, D])
    biasT_d = din("biasT", [128, 2048])
    bfar_d = din("bfar", [128, 8])
    ident_d = din("ident", [128, 128])
    Dt_d = din("Dt", [128, 1024])
    cdT_d = din("cdT", [128, 512])
    sd_d = din("sd", [128, 8])
    cd2_d = din("cd2", [128, 4])
    rot_d = din("rot", [NT * 128, 128])
    out_d = nc.dram_tensor("out", [S, D], F32, kind="ExternalOutput").ap()
    wsc_d = nc.dram_tensor("wsc", [NG, 128, 4096], BF16).ap()
    dbg_out = {}

    with ExitStack() as es:
        def sb(name, shape, dt):
            return es.enter_context(nc.sbuf_tensor(name, shape, dt))

        def ps(name, shape, dt):
            return es.enter_context(nc.psum_tensor(name, shape, dt))

        w_oa = sb("w_oa_sb", [128, 4, D], BF16)
        w_or = sb("w_or_sb", [128, 8, D], BF16)
        w_o = sb("w_o_sb", [128, 8, D], BF16)
        NW = 4
        wbuf = [sb("wbuf%d" % i, [128, 8, 512], BF16) for i in range(NW)]
        hT = sb("hT", [128, 8, 256], BF16)
        xy = [sb("xy%d" % i, [128, D], F32) for i in range(4)]
        h_bf = sb("h_bf", [128, D], BF16)
        qT_a2 = [sb("qT_a%d" % i, [128, 4, 256], BF16) for i in range(1)] * 2
        kT_a = sb("kT_a", [128, 4, 768], BF16)
        Vaug = sb("Vaug", [128, 6, 8, 65], BF16)
        gg_a2 = [sb("gg_a%d" % i, [128, 2, 512], BF16) for i in range(1)] * 2
        qk_rot2 = sb("qk_rot", [128, 2, 1024], BF16)
        k_dec = sb("k_dec", [128, 2, 512], BF16)
        qT_r = sb("qT_r", [128, 4, 256], BF16)
        qcT_r = sb("qcT_r", [128, 4, 256], BF16)
        kT_r = sb("kT_r", [128, 4, 256], BF16)
        v_r = sb("v_r", [128, 2, 1024], BF16)
        gg_r = sb("gg_r", [128, 2, 1024], BF16)
        t_ma = sb("t_ma", [128, 2, 1024], BF16)
        t_mr = sb("t_mr", [128, 2, 1024], BF16)
        PT = [sb("PT%d" % i, [128, 5, 128], BF16) for i in range(2)]
        Pr = sb("Pr", [128, 8, 128], BF16)
        state = sb("state", [128, 4, 128], F32)
        state_bf = sb("state_bf", [128, 4, 128], BF16)
        oa_tmp = sb("oa_tmp", [128, 512], F32)
        og_a = sb("og_a", [128, 512], BF16)
        ogT_a = sb("ogT_a", [128, 4, 128], BF16)
        ogT_a_b = sb("ogT_a_b", [128, 4, 128], BF16)
        or_n = sb("or_n", [128, 1024], BF16)
        ogT_r = sb("ogT_r", [128, 8, 128], BF16)
        ogT_r_b = sb("ogT_r_b", [128, 8, 128], BF16)
        u_bf = sb("u_bf", [128, 1024], BF16)
        v_bf = sb("v_bf", [128, 1024], BF16)
        mixedT = sb("mixedT", [128, 8, 128], BF16)
        t1 = sb("t1", [128, 512], F32)
        t2 = sb("t2", [128, 512], F32)
        rot = [sb("rot%d" % i, [128, 128], F32) for i in range(2)]
        Dt = sb("Dt_sb", [128, 8, 128], F32)
        cdT = sb("cdT_sb", [128, 4, 128], F32)
        bias_hi = sb("bias_hi", [128, 8, 256], BF16)
        fgain = sb("fgain_sb", [128, D], F32)
        ident_f = sb("ident_f", [128, 128], F32)
        ident = sb("ident_b", [128, 128], BF16)
        maskT = sb("maskT", [128, 128], BF16)
        sd = sb("sd_sb", [128, 8], F32)
        cd2 = sb("cd2_sb", [128, 4], F32)
        bfar = sb("bfar_sb", [128, 8], F32)
        gpk = sb("gpk_sb", [128, 8], F32)
        gnpk = sb("gnpk_sb", [128, 8], F32)
        gnh = sb("gnh_sb", [128, 8], F32)
        ss = sb("ss", [128, 8], F32)
        ss2 = sb("ss2", [128, 8], F32)
        negh = sb("negh", [128, 8], F32)
        bnst = sb("bnst", [128, 8, 6], F32)
        gq = sb("gq", [128, 5, 8], F32)
        grs = sb("grs", [128, 8], F32)
        gve = sb("gve", [128, 8], F32)
        rinv = sb("rinv", [128, 8], F32)

        psA = ps("psA", [128, 1024], F32)
        psB = ps("psB", [128, 512], F32)
        psC = ps("psC", [128, 1024], F32)
        psD = ps("psD", [128, 1024], F32)
        psT32 = ps("psT", [128, 512], F32)
        psT = psT32[:, :].bitcast(BF16)
        bankap = {0: psA[:, 0:512], 1: psA[:, 512:1024], 2: psB[:, 0:512], 3: psC[:, 0:512], 4: psC[:, 512:1024],
                  5: psD[:, 0:512], 6: psD[:, 512:1024], 7: psT32[:, 0:512]}
        psTv = psT.rearrange("p (k t) -> p k t", k=8)

        sch = Sched(nc, es)
        if True:
            op = sch.op
            dma = sch.dma

            consts = [(ident_f[:], ident_d), (Dt[:].rearrange("p a b -> p (a b)"), Dt_d),
                      (cdT[:].rearrange("p a b -> p (a b)"), cdT_d),
                      (fgain[:], fg_d),
                      (sd[:], sd_d), (cd2[:], cd2_d), (bfar[:], bfar_d), (gpk[:], gpk_d), (gnpk[:], gnpk_d)]
            names = ["ident_f", "Dt", "cdT", "fgain", "sd", "cd2", "bfar", "gpk", "gnpk"]
            for (o, i), nm in zip(consts, names):
                dma("sp", o, i[:, :], [], [nm], "const")
            for nm in names:
                sch.writer[nm] = (("d", "const"), sch.dcnt["const"])
            for hb in range(2):
                xs, xn = xy[2 + hb], ("sin", 2 + hb)
                dma("sp", xs[:], biasT_d[:, hb * 1024:(hb + 1) * 1024], [], [xn], ("sin", 2 + hb))
                hi = bias_hi[:, hb * 4:(hb + 1) * 4, :].rearrange("p a b -> p (a b)")
                op("dve", lambda e: e.tensor_copy(hi, xs[:]), [xn], [("bias_hi", hb)])
            op("dve", lambda e: e.tensor_copy(ident[:], ident_f[:]), ["ident_f"], ["ident"])
            op("dve", lambda e: e.memset(negh[:], -0.5), [], ["negh"])
            op("dve", lambda e: e.memset(maskT[:], 0.0), [], ["maskT"])
            op("dve", lambda e: e.memset(maskT[0:64, 64:128], MASK_NEG), [], ["maskT"])
            op("dve", lambda e: e.memset(state[:], 0.0), [], ["state"])
            op("dve", lambda e: e.memset(state_bf[:], 0.0), [], ["state_bf"])
            op("dve", lambda e: e.memset(Vaug[:, :, :, 64:65], 1.0), [], ["Vaug_ones"])
            op("dve", lambda e: e.memset(PT[0][:], 0.0), [], ["PT0"])
            op("dve", lambda e: e.memset(PT[1][:], 0.0), [], ["PT1"])
            op("dve", lambda e: e.tensor_scalar(out=gnh[:], in0=gnpk[:], scalar1=0.5, scalar2=None,
                                               op0=ALU.mult), ["gnpk"], ["gnh"])

            F32v = lambda t2d: t2d.bitcast(F32)
            sin = [(xy[q][:], ["xy%d" % q]) for q in range(4)]
            for q in range(NW):
                wv = F32v(wbuf[q][:].rearrange("p k c -> p (k c)"))
                sin.append((wv[:, 0:1024], ["wbuf%d" % q]))
                sin.append((wv[:, 1024:2048], ["wbuf%d" % q]))
            sout = [(h_bf[:], ["h_bf"]), (u_bf[:], [("u_bf", 0), ("u_bf", 1)]), (v_bf[:], [("v_bf", 0), ("v_bf", 1)]),
                    (or_n[:], ["or_n"])]
            for s_ in range(2):
                sout.append((t_ma[:, s_, :], [("tm", True, s_, 0), ("tm", True, s_, 1)]))
                sout.append((t_mr[:, s_, :], [("tm", False, s_, 0), ("tm", False, s_, 1)]))
                sout.append((gg_r[:, s_, :], [("gg_r", s_, 0), ("gg_r", s_, 1)]))
            NSI, NSO = len(sin), len(sout)
            win_chunks = [(k, cb) for k in range(KT) for cb in range(7)]
            LA = NSI - 2
            for j in range(len(win_chunks) + LA):
                if j < len(win_chunks):
                    k, cb = win_chunks[j]
                    dma("sp", sin[j % NSI][0], win_d[k * 128:(k + 1) * 128, cb * 1024:(cb + 1) * 1024],
                        [], [("sin", j % NSI)], ("sin", j % NSI))
                if j >= LA:
                    jc = j - LA
                    k, cb = win_chunks[jc]
                    xs, xn = sin[jc % NSI][0], ("sin", jc % NSI)
                    st, sn = sout[jc % NSO][0], ("sout", jc % NSO)
                    if jc % 2 == 0:
                        op("dve", lambda en: en.tensor_scalar(
                            out=st, in0=xs, scalar1=gpk[:, k:k + 1], scalar2=None, op0=ALU.mult),
                           [xn, "gpk"], [sn])
                    else:
                        op("act", lambda en: en.activation(
                            out=st, in_=xs, func=AF.Copy, scale=gpk[:, k:k + 1]),
                           [xn, "gpk"], [sn])
                    dst = AP(wsc_d.tensor, wsc_d.offset + (2 * cb) * 128 * 4096 + k * 512,
                             [[4096, 128], [128 * 4096, 2], [1, 512]])
                    dma("act", dst, st.rearrange("p (g c) -> p g c", g=2),
                        [sn], [("wscq", jc % NSO)], ("sout", jc % NSO))
            for q, (_, names) in enumerate(sin):
                toks = [sch.writer[("sin", q)]] + list(sch.readers.get(("sin", q), []))
                for nm in names:
                    sch.readers.setdefault(nm, []).extend(toks)
            for q, (_, names) in enumerate(sout):
                toks = [sch.writer[("sout", q)]] + list(sch.readers.get(("sout", q), []))
                for nm in names:
                    sch.readers.setdefault(nm, []).extend(toks)

            res_chunks = ([(w_oa, woa_d, k, None, 0.5) for k in range(4)]
                          + [(w_or, wor_d, k, gnh, None) for k in range(8)]
                          + [(w_o, wo_d, k, None, 0.5) for k in range(8)])

            def resident_gen():
                n = len(res_chunks)
                for j in range(n + 1):
                    if j < n:
                        dst, dram, k, sc_t, const = res_chunks[j]
                        sl_ = 2 + j % 2
                        dma("sp", xy[sl_][:], dram[k * 128:(k + 1) * 128, :], [], ["xy%d" % sl_], ("x", sl_))
                    if j >= 1:
                        dst, dram, k, sc_t, const = res_chunks[j - 1]
                        sl_ = 2 + (j - 1) % 2
                        xs, xn = xy[sl_], "xy%d" % sl_
                        wname = ("w", id(dst), k)
                        if sc_t is not None:
                            op("dve", lambda en: en.tensor_scalar(
                                out=dst[:, k, :], in0=xs[:], scalar1=sc_t[:, k:k + 1], scalar2=None,
                                op0=ALU.mult), [xn, "gnh"], [wname])
                        elif j % 2 == 0:
                            op("dve", lambda en: en.tensor_scalar(
                                out=dst[:, k, :], in0=xs[:], scalar1=const, scalar2=None, op0=ALU.mult),
                               [xn], [wname])
                        else:
                            op("act", lambda en: en.activation(
                                out=dst[:, k, :], in_=xs[:], func=AF.Copy, scale=const), [xn], [wname])
                    yield

            def load_x(i):
                slot = i % 2
                dma("sp", xy[slot][:], x_d[i * 128:(i + 1) * 128, :], [], ["xy%d" % slot], ("x", slot))
                dma("sp", rot[slot][:], rot_d[i * 128:(i + 1) * 128, :], [], ["rot%d" % slot], ("rot", slot))

            def load_y(i):
                slot = 2 + (i % 2)
                dma("sp", xy[slot][:], x_d[i * 128:(i + 1) * 128, :], [], ["xy%d" % slot], ("x", slot))

            WORDER = [0, 1, 2, 3, 6, 7, 4, 8, 9, 5, 10, 11, 12, 13]
            WPOS = {g_: p_ for p_, g_ in enumerate(WORDER)}

            def load_w(pos):
                g = WORDER[pos]
                dma("sp", wbuf[pos % NW][:].rearrange("p k c -> p (k c)"), wsc_d[g, :, :],
                    [("wscq", q_) for q_ in range(NSO)], ["wbuf%d" % (pos % NW)], ("w", pos % NW))

            projbanks = [0, 1, 2, 3, 5, 6]
            pb = [0]

            def next_bank():
                b = projbanks[pb[0] % len(projbanks)]
                pb[0] += 1
                return b

            evq = [0]

            def evac_eng():
                evq[0] += 1
                return "act" if evq[0] % 2 else "dve"

            def copy_op(e, out, in_, reads, writes, scale=None):
                if e == "act":
                    if scale is None:
                        op("act", lambda en: en.activation(out=out, in_=in_, func=AF.Copy), reads, writes)
                    else:
                        op("act", lambda en: en.activation(out=out, in_=in_, func=AF.Copy, scale=scale),
                           reads, writes)
                else:
                    if scale is None:
                        op("dve", lambda en: en.tensor_copy(out, in_), reads, writes)
                    else:
                        op("dve", lambda en: en.tensor_scalar(out=out, in0=in_, scalar1=scale, scalar2=None,
                                                            op0=ALU.mult), reads, writes)

            def transposes(src, nblk, reads, tag):
                for k in range(nblk):
                    op("pe", lambda en, k=k: en.transpose(out=psTv[:, k, :], in_=src[:, k * 128:(k + 1) * 128],
                                                         identity=ident[:]),
                       reads + ["ident"], ["b7"], inc=(k == nblk - 1))

            def phaseA1(b):
                for s in range(2):
                    i = 2 * b + s
                    xs = xy[s]
                    xn = "xy%d" % s
                    op("dve", lambda en: en.scalar_tensor_tensor(out=t1[:, 0:512].bitcast(BF16), in0=xs[:], scalar=1.0,
                                                                in1=xs[:], op0=ALU.mult, op1=ALU.mult,
                                                                accum_out=ss[:, 0:1]), [xn], ["t1", "ss0"])
                    yield
                    op("pool", lambda en: en.tensor_scalar(out=ss[:, 1:2], in0=ss[:, 0:1], scalar1=1.0 / D,
                                                          scalar2=NORM_EPS, op0=ALU.mult, op1=ALU.add),
                       ["ss0"], ["ss1"])
                    op("pool", lambda en: en.tensor_tensor(out=ss[:, 2:3], in0=ss[:, 1:2], in1=negh[:, 0:1],
                                                          op=ALU.pow), ["ss1", "negh"], ["ss2"])
                    yield
                    yield
                    op("dve", lambda en: en.tensor_scalar(out=h_bf[:], in0=xs[:], scalar1=ss[:, 2:3],
                                                         scalar2=None, op0=ALU.mult), [xn, "ss2"], ["h_bf"])
                    yield
                    transposes(h_bf, 8, ["h_bf"], "h")
                    op("dve", lambda en: en.tensor_copy(hT[:, :, s * 128:(s + 1) * 128], psTv), ["b7"], [("hT", s)])
                    yield

            def phaseA2(b, groups, banks, gate=None, gate_ret=None):
                projbanks[:] = banks
                deferred = []
                ret_gated = False
                qT_a = qT_a2[b % 2]
                gg_a = gg_a2[b % 2]
                for g in groups:
                    pos = WPOS[g]
                    wb = wbuf[pos % NW]
                    wn = "wbuf%d" % (pos % NW)
                    if g == 10 and gate is not None:
                        yield from wait(gate)
                    if g >= 4 and gate_ret is not None and not ret_gated:
                        ret_gated = True
                        yield from wait(gate_ret)
                    if g < 2:
                        for fp in range(2):
                            bk = next_bank()
                            for f2 in range(2):
                                ft = 2 * fp + f2
                                P = bankap[bk][:, f2 * 256:(f2 + 1) * 256]
                                for k in range(KT):
                                    op("pe", lambda en, k=k: en.matmul(out=P, lhsT=wb[:, k, ft * 128:(ft + 1) * 128],
                                                                      rhs=hT[:, k, :], start=(k == 0), stop=(k == KT - 1)),
                                       [wn, ("hT", 0), ("hT", 1)], ["b%d" % bk], inc=(k == KT - 1))
                            yield
                            e = evac_eng()
                            P2 = bankap[bk]
                            if g == 0:
                                copy_op(e, qT_a[:, 2 * fp:2 * fp + 2, :].rearrange("p a b -> p (a b)"), P2, ["b%d" % bk],
                                        [("qT_a", 0, 2 * fp), ("qT_a", 0, 2 * fp + 1)], scale=0.125)
                            else:
                                r0 = ((2 * b) % RING) * 128
                                copy_op(e, kT_a[:, 2 * fp:2 * fp + 2, r0:r0 + 256],
                                        P2.rearrange("p (a b) -> p a b", a=2), ["b%d" % bk],
                                        [("kT_a", 2 * fp + a_, (2 * b + t_) % RING) for a_ in range(2) for t_ in range(2)])
                    else:
                        for s in range(2):
                            i = 2 * b + s
                            bk = next_bank()
                            P = bankap[bk]
                            bn = "b%d" % bk
                            for k in range(KT):
                                op("pe", lambda en, k=k: en.matmul(out=P, lhsT=hT[:, k, s * 128:(s + 1) * 128],
                                                                  rhs=wb[:, k, :], start=(k == 0), stop=(k == KT - 1)),
                                   [wn, ("hT", s)], [bn], inc=(k == KT - 1))
                            yield
                            if deferred and g != 5:
                                deferred.pop(0)()
                            if g == 2:
                                sl = i % RING
                                copy_op(evac_eng(), Vaug[:, sl, :, 0:64],
                                        cap(P, [[64, 8], [1, 64]]), [bn], [("Vaug", sl)])
                            elif g in (3, 8, 9):
                                op("act", lambda en: en.activation(out=t1[:], in_=P, func=AF.Tanh, scale=0.5),
                                   [bn], ["t1"])
                                yield
                                if g == 3:
                                    dst = gg_a[:, s, :]
                                    dn = ("gg_a", 0, s)
                                else:
                                    dst = gg_r[:, s, (g - 8) * 512:(g - 7) * 512]
                                    dn = ("gg_r", s, g - 8)
                                op("dve", lambda en: en.scalar_tensor_tensor(out=dst, in0=t1[:], scalar=1.0, in1=P,
                                                                            op0=ALU.add, op1=ALU.mult),
                                   ["t1", bn], [dn])
                            elif g in (4, 5):
                                qk_rot = qk_rot2[:, s, :]
                                rt = rot[s]
                                rn = "rot%d" % s
                                cc = cap(rt[:, 0:64], [[0, 8], [1, 64]])
                                ssn = cap(rt[:, 64:128], [[0, 8], [32, 2], [1, 32]])
                                Pv = cap(P, [[64, 8], [1, 64]])
                                Psw = cap(P, [[64, 8], [-32, 2], [1, 32]], off=32)
                                op("dve", lambda en: en.tensor_tensor(out=cap(t1[:], [[64, 8], [1, 64]]), in0=Pv,
                                                                     in1=cc, op=ALU.mult), [bn, rn], ["t1"])
                                op("dve", lambda en: en.tensor_tensor(out=cap(t2[:], [[64, 8], [32, 2], [1, 32]]),
                                                                     in0=Psw, in1=ssn, op=ALU.mult), [bn, rn], ["t2"])
                                off = (g - 4) * 512
                                op("dve", lambda en: en.tensor_tensor(out=qk_rot[:, off:off + 512], in0=t1[:],
                                                                     in1=t2[:], op=ALU.add),
                                   ["t1", "t2"], [("qk_rot", s, g - 4)])
                                if g == 5:
                                    op("dve", lambda en: en.tensor_tensor(
                                        out=cap(k_dec[:, s, :], [[64, 8], [1, 64]]),
                                        in0=cap(qk_rot[:, 512:1024], [[64, 8], [1, 64]]),
                                        in1=cap(sd[:], [[1, 8], [0, 64]]), op=ALU.mult),
                                       [("qk_rot", s, 1), "sd"], [("k_dec", s)])
                                    def rot_T(s=s, qk_rot=qk_rot):
                                        transposes(qk_rot, 8, [("qk_rot", s, 0), ("qk_rot", s, 1)], "r")
                                        sl = slice(s * 128, (s + 1) * 128)
                                        op("act", lambda en: en.activation(out=qT_r[:, :, sl], in_=psTv[:, 0:4, :],
                                                                           func=AF.Copy), ["b7"], [("qT_r", s)])
                                        op("act", lambda en: en.activation(out=kT_r[:, :, sl], in_=psTv[:, 4:8, :],
                                                                           func=AF.Copy), ["b7"], [("kT_r", s)])
                                        op("dve", lambda en: en.tensor_tensor(out=qcT_r[:, :, sl], in0=psTv[:, 0:4, :],
                                                                             in1=cdT[:], op=ALU.mult),
                                           ["b7", "cdT"], [("qcT_r", s)])
                                    deferred.append(rot_T)
                            elif g in (6, 7):
                                copy_op(evac_eng(), v_r[:, s, (g - 6) * 512:(g - 5) * 512], P, [bn],
                                        [("v_r", s, g - 6)])
                            else:
                                tgt = t_ma if g < 12 else t_mr
                                hh = (g - 10) % 2
                                op("act", lambda en: en.activation(out=tgt[:, s, hh * 512:(hh + 1) * 512], in_=P,
                                                                   func=AF.Tanh, scale=0.5),
                                   [bn], [("tm", g < 12, s, hh)])
                    if pos + NW < NG:
                        load_w(pos + NW)
                    yield
                while deferred:
                    deferred.pop(0)()

            ogT_a2 = [ogT_a, ogT_a_b]
            ogT_r2 = [ogT_r, ogT_r_b]

            def O_h(h):
                c0 = (h % 4) * 65
                return psC[:, c0:c0 + 65]

            def ORh(h):
                c0 = (h % 2) * 512 + (h // 2) * 128
                return psD[:, c0:c0 + 128]

            def att(i):
                s = i % 2
                sl = slice(s * 128, (s + 1) * 128)
                kts = [kt for kt in range(5) if i - 4 + kt >= 0]
                far = [kt for kt in kts if kt < 3]
                bp = (i // 2) % 2
                qT_a = qT_a2[bp]
                gg_a = gg_a2[bp]

                def nearbank(h):
                    if h % 2 == 0:
                        return psB[:, 0:256], "b2"
                    return psC[:, 512:768], "b4"

                def scores(h):
                    hp, r0 = h // 2, 64 * (h % 2)
                    NB_, nbn = nearbank(h)
                    fb = "b%d" % (h % 2)
                    FA = psA[:, (h % 2) * 512:(h % 2 + 1) * 512]
                    c0 = 0 if 3 in kts else 128
                    masked = 0 in kts
                    if masked:
                        op("pe", lambda en: en.matmul(out=FA[:, 0:128], lhsT=ident[:], rhs=maskT[:], start=True,
                                                      stop=False), ["ident", "maskT"], [fb], inc=False)
                    op("pe", lambda en: en.matmul(out=NB_[:, c0:256], lhsT=ident[:], rhs=bias_hi[:, h, c0:256],
                                                  start=True, stop=False), [("bias_hi", h // 4), "ident"], [nbn],
                       inc=False)
                    first_far = not masked
                    for kt in kts:
                        ring = ((i - 4 + kt) % RING) * 128
                        if kt < 3:
                            out = FA[:, kt * 128:(kt + 1) * 128]
                            bn = fb
                            st, sp = first_far, (kt == far[-1])
                            first_far = False
                        else:
                            out = NB_[:, (kt - 3) * 128:(kt - 2) * 128]
                            bn = nbn
                            st, sp = False, (kt == 4)
                        op("pe", lambda en: en.matmul(out=out, lhsT=kT_a[r0:r0 + 64, hp, ring:ring + 128],
                                                      rhs=qT_a[r0:r0 + 64, hp, sl], start=st, stop=sp),
                           [("kT_a", hp, (i - 4 + kt) % RING), ("qT_a", 0, hp)], [bn],
                           inc=(kt == 4 or (far and kt == far[-1])))

                def softmax(h):
                    pt = PT[h % 2]
                    pn = "PT%d" % (h % 2)
                    SA = psA[:, (h % 2) * 512:(h % 2 + 1) * 512]
                    b01 = "b%d" % (h % 2)
                    if far:
                        a, bnd = far[0], far[-1] + 1
                        op("act", lambda en: en.activation(
                            out=pt[:, a:bnd, :].rearrange("p a b -> p (a b)"), in_=SA[:, a * 128:bnd * 128],
                            func=AF.Exp, bias=bfar[:, h:h + 1]), [b01, "bfar"], [pn])
                    c0 = 0 if 3 in kts else 128
                    NB_, nbn = nearbank(h)
                    op("act", lambda en: en.activation(
                        out=pt[:, 3:5, :].rearrange("p a b -> p (a b)")[:, c0:256], in_=NB_[:, c0:256], func=AF.Exp),
                       [nbn], [pn])

                def pv(h):
                    pt = PT[h % 2]
                    pn = "PT%d" % (h % 2)
                    first = True
                    for kt in kts:
                        slot = (i - 4 + kt) % RING
                        last = (kt == 4)
                        op("pe", lambda en, first=first: en.matmul(out=O_h(h), lhsT=pt[:, kt, :],
                                                                 rhs=Vaug[:, slot, h, :], start=first, stop=last),
                           [pn, ("Vaug", slot), "Vaug_ones"], ["b3"], inc=last)
                        first = False

                def epilogue(half):
                    hs = slice(half * 4, half * 4 + 4)
                    cs = slice(half * 256, half * 256 + 256)
                    op("dve", lambda en: en.reciprocal(out=rinv[:, hs], in_=cap(psC[:, 0:512], [[65, 4]], off=64)),
                       ["b3"], [("rinv", half)])
                    op("dve", lambda en: en.tensor_tensor(
                        out=cap(oa_tmp[:, cs], [[64, 4], [1, 64]]),
                        in0=cap(psC[:, 0:512], [[65, 4], [1, 64]]),
                        in1=cap(rinv[:, hs], [[1, 4], [0, 64]]), op=ALU.mult),
                       ["b3", ("rinv", half)], [("oa_tmp", half)])
                    op("dve", lambda en: en.tensor_tensor(out=og_a[:, cs], in0=oa_tmp[:, cs], in1=gg_a[:, s, cs],
                                                         op=ALU.mult),
                       [("oa_tmp", half), ("gg_a", 0, s)], [("og_a", half)])

                scores(0)
                for h in range(8):
                    if h + 1 < 8:
                        scores(h + 1)
                    softmax(h)
                    yield
                    pv(h)
                    if h % 4 == 3:
                        epilogue(h // 4)
                    yield
                dst = ogT_a2[i % 2]
                transposes(og_a, 4, [("og_a", 0), ("og_a", 1)], "a")
                op("dve", lambda en: en.tensor_copy(dst[:], psTv[:, 0:4, :]), ["b7"], [("ogT_a", i % 2)])
                yield

            def ret(i):
                s = i % 2
                sl = slice(s * 128, (s + 1) * 128)
                for par in range(2):
                    for hh in range(4):
                        h = 2 * hh + par
                        hp, r0 = h // 2, 64 * (h % 2)
                        op("pe", lambda en: en.matmul(out=ORh(h), lhsT=kT_r[r0:r0 + 64, hp, sl],
                                                      rhs=qT_r[r0:r0 + 64, hp, sl], start=True, stop=True),
                           [("kT_r", s), ("qT_r", s)], ["b%d" % (5 + par)], inc=(hh == 3))
                yield
                for par in range(2):
                    op("dve", lambda en: en.tensor_tensor(
                        out=Pr[:, par * 4:(par + 1) * 4, :].rearrange("p a b -> p (a b)"),
                        in0=psD[:, par * 512:(par + 1) * 512],
                        in1=Dt[:, par * 4:(par + 1) * 4, :].rearrange("p a b -> p (a b)"), op=ALU.mult),
                       ["b%d" % (5 + par), "Dt"], [("Pr", par)])
                yield
                cross = i > 0
                for h in range(8):
                    hp, r0 = h // 2, 64 * (h % 2)
                    bn = "b%d" % (5 + h % 2)
                    op("pe", lambda en: en.matmul(out=ORh(h), lhsT=Pr[:, (h % 2) * 4 + h // 2, :],
                                                  rhs=v_r[:, s, h * 128:(h + 1) * 128], start=True, stop=not cross),
                       [("Pr", h % 2), ("v_r", s, h // 4)], [bn], inc=(not cross))
                    if cross:
                        op("pe", lambda en: en.matmul(out=ORh(h), lhsT=qcT_r[r0:r0 + 64, hp, sl],
                                                      rhs=state_bf[r0:r0 + 64, hp, :], start=False, stop=True),
                           [("qcT_r", s), "state_bf"], [bn], inc=True)
                yield
                for h in range(8):
                    op("dve", lambda en: en.bn_stats(out=bnst[:, h, :], in_=ORh(h)),
                       ["b%d" % (5 + h % 2)], ["bnst"])
                    if h % 4 == 3:
                        yield
                me, mo = bnst[:, :, 1], bnst[:, :, 4]
                M2e, M2o = bnst[:, :, 2], bnst[:, :, 5]
                dv = lambda fn, r, w: op("dve", fn, r, w)
                dv(lambda en: en.tensor_tensor(out=gq[:, 0, :], in0=me, in1=mo, op=ALU.add), ["bnst"], ["gq0"])
                dv(lambda en: en.tensor_tensor(out=gq[:, 1, :], in0=me, in1=mo, op=ALU.subtract), ["bnst"], ["gq1"])
                dv(lambda en: en.tensor_tensor(out=gq[:, 2, :], in0=gq[:, 1, :], in1=gq[:, 1, :], op=ALU.mult),
                   ["gq1"], ["gq2"])
                dv(lambda en: en.tensor_tensor(out=gq[:, 3, :], in0=M2e, in1=M2o, op=ALU.add), ["bnst"], ["gq3"])
                dv(lambda en: en.tensor_scalar(out=gq[:, 3, :], in0=gq[:, 3, :], scalar1=1.0 / 128, scalar2=GN_EPS,
                                               op0=ALU.mult, op1=ALU.add), ["gq3"], ["gq3"])
                dv(lambda en: en.scalar_tensor_tensor(out=gve[:], in0=gq[:, 2, :], scalar=0.25, in1=gq[:, 3, :],
                                                      op0=ALU.mult, op1=ALU.add), ["gq2", "gq3"], ["gve"])
                yield
                yield
                op("pool", lambda en: en.tensor_tensor(out=grs[:], in0=gve[:], in1=negh[:], op=ALU.pow),
                   ["gve", "negh"], ["grs"])
                yield
                yield
                yield
                dv(lambda en: en.scalar_tensor_tensor(out=gq[:, 4, :], in0=gq[:, 0, :], scalar=-0.5, in1=grs[:],
                                                      op0=ALU.mult, op1=ALU.mult), ["gq0", "grs"], ["gq4"])
                dv(lambda en: en.tensor_scalar(out=gq[:, 1, :], in0=gq[:, 0, :], scalar1=0.5, scalar2=None,
                                               op0=ALU.mult), ["gq0", "gq1"], ["gq1"])
                yield
                for hh in range(4):
                    h = hh
                    op("act", lambda en: en.activation(out=or_n[:, h * 128:(h + 1) * 128], in_=ORh(h),
                                                       func=AF.Identity, scale=grs[:, h:h + 1],
                                                       bias=gq[:, 4, h:h + 1]),
                       ["b%d" % (5 + h % 2), "gq4", "grs"], [("or_n", h)])
                    h = 4 + hh
                    op("dve", lambda en: en.tensor_scalar(out=or_n[:, h * 128:(h + 1) * 128], in0=ORh(h),
                                                         scalar1=gq[:, 1, h:h + 1], scalar2=grs[:, h:h + 1],
                                                         op0=ALU.subtract, op1=ALU.mult),
                       ["b%d" % (5 + h % 2), "gq1", "grs"], [("or_n", h)])
                    if hh % 2 == 1:
                        yield
                yield
                yield
                op("dve", lambda en: en.tensor_tensor(out=or_n[:], in0=or_n[:], in1=gg_r[:, s, :], op=ALU.mult),
                   [("or_n", h_) for h_ in range(8)] + [("gg_r", s, 0), ("gg_r", s, 1)],
                   ["or_n"] + [("or_n", h_) for h_ in range(8)])
                yield
                yield
                for h in range(8):
                    hp, p0 = h // 2, 64 * (h % 2)
                    op("pe", lambda en: en.matmul(out=psD[p0:p0 + 64, hp * 128:(hp + 1) * 128],
                                                  lhsT=k_dec[:, s, h * 64:(h + 1) * 64],
                                                  rhs=v_r[:, s, h * 128:(h + 1) * 128], start=True, stop=True),
                       [("k_dec", s), ("v_r", s, h // 4)], ["b5"], inc=(h == 7))
                dst = ogT_r2[i % 2]
                transposes(or_n, 8, ["or_n"] + [("or_n", h_) for h_ in range(8)], "r2")
                op("dve", lambda en: en.tensor_copy(dst[:], psTv), ["b7"], [("ogT_r", i % 2)])
                yield
                for hp in range(4):
                    op("dve", lambda en: en.scalar_tensor_tensor(
                        out=state[:, hp, :], in0=state[:, hp, :], scalar=cd2[:, hp:hp + 1],
                        in1=psD[:, hp * 128:(hp + 1) * 128], op0=ALU.mult, op1=ALU.add),
                       ["state", "cd2", "b5"], ["state"])
                yield
                op("pool", lambda en: en.tensor_copy(state_bf[:], state[:]), ["state"], ["state_bf"])
                yield

            def tail(i):
                s = i % 2
                oa = ogT_a2[i % 2]
                orr = ogT_r2[i % 2]
                P4 = psC[:, 512:1024]
                for n in range(2):
                    cs = slice(n * 512, (n + 1) * 512)
                    for k in range(4):
                        op("pe", lambda en: en.matmul(out=P4, lhsT=oa[:, k, :], rhs=w_oa[:, k, cs],
                                                      start=(k == 0), stop=(k == 3)),
                           [("ogT_a", i % 2), ("w", id(w_oa), k)], ["b4"], inc=(k == 3))
                    yield
                    op("dve", lambda en: en.scalar_tensor_tensor(out=u_bf[:, cs], in0=t_ma[:, s, cs], scalar=1.0, in1=P4,
                                                                op0=ALU.add, op1=ALU.mult),
                       [("tm", True, s, n), "b4"], [("u_bf", n)])
                    yield
                    for k in range(8):
                        op("pe", lambda en: en.matmul(out=P4, lhsT=orr[:, k, :], rhs=w_or[:, k, cs],
                                                      start=(k == 0), stop=(k == 7)),
                           [("ogT_r", i % 2), ("w", id(w_or), k)], ["b4"], inc=(k == 7))
                    yield
                    op("dve", lambda en: en.scalar_tensor_tensor(out=v_bf[:, cs], in0=t_mr[:, s, cs], scalar=1.0, in1=P4,
                                                                op0=ALU.add, op1=ALU.mult),
                       [("tm", False, s, n), "b4"], [("v_bf", n)])
                    op("dve", lambda en: en.tensor_tensor(out=u_bf[:, cs], in0=u_bf[:, cs], in1=v_bf[:, cs], op=ALU.add),
                       [("u_bf", n), ("v_bf", n)], [("u_bf", n)])
                    yield
                transposes(u_bf, 8, [("u_bf", 0), ("u_bf", 1)], "m")
                op("act", lambda en: en.activation(out=mixedT[:], in_=psTv, func=AF.Copy), ["b7"], ["mixedT"])
                yield
                yb = xy[2 + s]
                yn = "xy%d" % (2 + s)
                for n in range(2):
                    cs = slice(n * 512, (n + 1) * 512)
                    for k in range(8):
                        op("pe", lambda en: en.matmul(out=P4, lhsT=mixedT[:, k, :], rhs=w_o[:, k, cs],
                                                      start=(k == 0), stop=(k == 7)),
                           ["mixedT", ("w", id(w_o), k)], ["b4"], inc=(k == 7))
                    yield
                    op("dve", lambda en: en.tensor_tensor(out=yb[:, cs], in0=P4, in1=yb[:, cs], op=ALU.add),
                       ["b4", yn], [yn])
                    yield
                op("act", lambda en: en.activation(out=v_bf[:], in_=yb[:], func=AF.Square, accum_out=ss2[:, 0:1]),
                   [yn], [("v_bf", 0), ("v_bf", 1), "ss2_0"])
                yield
                op("pool", lambda en: en.tensor_scalar(out=ss2[:, 1:2], in0=ss2[:, 0:1], scalar1=1.0 / D,
                                                      scalar2=NORM_EPS, op0=ALU.mult, op1=ALU.add),
                   ["ss2_0"], ["ss2_1"])
                op("pool", lambda en: en.tensor_tensor(out=ss2[:, 2:3], in0=ss2[:, 1:2], in1=negh[:, 0:1],
                                                      op=ALU.pow), ["ss2_1", "negh"], ["ss2_2"])
                yield
                op("dve", lambda en: en.scalar_tensor_tensor(out=yb[:], in0=yb[:], scalar=ss2[:, 2:3], in1=fgain[:],
                                                            op0=ALU.mult, op1=ALU.mult),
                   [yn, "ss2_2", "fgain"], [yn])
                dma("sp", out_d[i * 128:(i + 1) * 128, :], yb[:], [yn], [], ("out", s))
                yield

            rrp = [0]

            def run(chains):
                chains = list(chains)
                chains0 = list(chains)
                rrp[0] = 0
                rt = {id(c): 0.0 for c in chains}
                while chains:
                    progressed = False
                    if GREEDY:
                        order = sorted(chains, key=lambda c_: rt[id(c_)])
                    elif POLICY.get("H") is not None:
                        k_ = rrp[0] % len(chains)
                        rr = chains[k_:] + chains[:k_]
                        lo = min(rt[id(c_)] for c_ in chains)
                        order = ([c_ for c_ in rr if rt[id(c_)] <= lo + POLICY["H"]]
                                 + sorted([c_ for c_ in rr if rt[id(c_)] > lo + POLICY["H"]], key=lambda c_: rt[id(c_)]))
                    else:
                        wts = POLICY.get("w", (1, 1, 1, 1))
                        slots = [c_ for ci, c_ in enumerate(chains0) if c_ in chains
                                 for _ in range(wts[min(ci, len(wts) - 1)])]
                        k_ = rrp[0] % len(slots)
                        order = slots[k_:] + slots[:k_]
                    rrp[0] += 1
                    for c in order:
                        sch.step_end = None
                        try:
                            r = next(c)
                        except StopIteration:
                            chains.remove(c)
                            progressed = True
                            break
                        if r == "blocked":
                            continue
                        if sch.step_end is not None:
                            rt[id(c)] = sch.step_end
                        progressed = True
                        break
                    assert progressed

            done = set()

            def mark(gen, key):
                yield from gen
                done.add(key)

            def wait(keys):
                while not all(k in done for k in keys):
                    yield "blocked"

            ATT_G = [0, 1, 2, 3]
            BK3 = [0, 1, 2, 3, 5, 6]
            REST_G = list(range(4, NG))

            def seq(*gens):
                for g in gens:
                    yield from g

            load_x(0)
            load_x(1)
            for g in range(NW):
                load_w(g)
            if stop >= 1:
                run([seq(phaseA1(0), phaseA2(0, WORDER, BK3)), resident_gen()])
            for b in range(NB):
                if stop < 2:
                    break
                nxt = b + 1 < NB
                if nxt:
                    load_x(2 * b + 2)
                    load_x(2 * b + 3)
                    for g in range(NW):
                        load_w(g)
                t0, t1_ = 2 * b, 2 * b + 1
                run([seq(att(t0), att(t1_))] + ([phaseA1(b + 1)] if nxt else []))
                load_y(t0)
                load_y(t1_)
                rets = seq(mark(ret(t0), ("ret", t0)), mark(ret(t1_), ("ret", t1_)))
                tails = seq(wait([("ret", t0)]), tail(t0), wait([("ret", t1_)]), mark(tail(t1_), ("tail", t1_)))
                if nxt:
                    run([rets, tails, phaseA2(b + 1, WORDER, [0, 1, 2, 3], gate=[("tail", t1_)],
                                              gate_ret=[("ret", t1_)])])
                else:
                    run([rets, tails])
            sch.finish("sp")

    print("[kernel] build: %d instructions, %d waits, %d dma sems, model %.0f us" % (
        sch.nins, sch.nwait, len(sch.dsem), sch.makespan))
    if DRY:
        return sch.makespan
    return nc


def _tables(NT):
    H = 8
    hs = np.arange(H, dtype=np.float64)
    lg = np.log1p(-np.exp2(-5.0 - hs))
    j = np.arange(128, dtype=np.float64)
    kk = j[:, None]
    qq = j[None, :]
    same = (kk // 64) == (qq // 64)
    past = (kk // 64) < (qq // 64)
    dist = np.abs(qq - kk)
    Dt = np.zeros((128, H, 128))
    for h in range(H):
        Dt[:, (h % 2) * 4 + h // 2, :] = np.where(same | past, np.exp(dist * lg[h]), 0.0) * 0.125
    cdT = np.zeros((128, 4, 128))
    cd2 = np.zeros((128, 4))
    for p in range(128):
        for pr in range(4):
            h = 2 * pr + p // 64
            cdT[p, pr, :] = 0.125 * np.exp((j + 1.0) * lg[h])
            cd2[p, pr] = np.exp(128.0 * lg[h])
    sd = np.exp((127.0 - j)[:, None] * lg[None, :])
    half = 32
    inv_freq = np.power(10000.0, -np.arange(half, dtype=np.float64) / half)
    pos = np.arange(NT * 128, dtype=np.float64)
    ang = pos[:, None] * inv_freq[None, :]
    c, sn = np.cos(ang), np.sin(ang)
    rot = np.concatenate([c, c, -sn, sn], axis=1)
    ident = np.eye(128)
    f = lambda a: np.ascontiguousarray(a, dtype=np.float32)
    return dict(Dt=f(Dt.reshape(128, 1024)), cdT=f(cdT.reshape(128, 512)), cd2=f(cd2), sd=f(sd), rot=f(rot),
                ident=f(ident))


def _bias_tables(rel_bias):
    k = np.arange(128)[:, None]
    q = np.arange(128)[None, :]
    biasT = np.zeros((128, 8, 2, 128), np.float32)
    for jj in range(2):
        dist = q - k + 128 * (1 - jj)
        idx = np.clip(dist, -128, 128) + 128
        biasT[:, :, jj, :] = np.transpose(rel_bias[:, idx], (1, 0, 2))
    m = (k >= 64) & (q < 64)
    biasT[:, :, 1, :] = np.where(m[:, None, :], np.float32(MASK_NEG), biasT[:, :, 1, :])
    bfar = np.ascontiguousarray(np.broadcast_to(rel_bias[:, 256][None, :], (128, 8)), dtype=np.float32)
    return np.ascontiguousarray(biasT.reshape(128, 2048)), bfar


def make_in_maps(x, norm_gain, w_in, rel_bias, gn_gain, w_out_attn, w_out_ret, w_out, final_gain, NT):
    f = lambda a: np.ascontiguousarray(np.asarray(a), dtype=np.float32)
    tabs = _tables(NT)
    biasT, bfar = _bias_tables(f(rel_bias)[0])
    shared = dict(
        w_in=f(w_in)[0], w_oa=f(w_out_attn)[0], w_or=f(w_out_ret)[0], w_o=f(w_out)[0],
        gain_pk=f(f(norm_gain)[0].reshape(8, 128).T), gn_pk=f(f(gn_gain)[0].reshape(8, 128).T),
        fgain=f(np.broadcast_to(f(final_gain)[None, :], (128, D))),
        biasT=biasT, bfar=bfar, **tabs)
    xs = f(x)
    maps = []
    for c in range(xs.shape[0]):
        m = dict(shared)
        m["x"] = np.ascontiguousarray(xs[c, :NT * 128, :])
        maps.append(m)
    return maps


_NC_CACHE = {}


def kernel(x, norm_gain, w_in, rel_bias, gn_gain, w_out_attn, w_out_ret, w_out, final_gain):
    x = np.asarray(x)
    B, S, _ = x.shape
    NT = S // 128
    NB = NT // 2
    if NB not in _NC_CACHE:
        _NC_CACHE[NB] = build(NB)
    nc = _NC_CACHE[NB]
    maps = make_in_maps(x, norm_gain, w_in, rel_bias, gn_gain, w_out_attn, w_out_ret, w_out, final_gain, NT)
    res = run_bass_kernel_spmd(nc, maps, core_ids=list(range(B)))
    out = np.stack([np.asarray(r["out"]) for r in res.results], axis=0)
    return out.astype(np.float32, copy=False)
```

```python
import numpy as np
from contextlib import ExitStack

import concourse.bass as bass
import concourse.mybir as mybir
from concourse.ap import AP
from concourse.bass_utils import run_bass_kernel_spmd

F32 = mybir.dt.float32
BF16 = mybir.dt.bfloat16
AF = mybir.ActivationFunctionType
ALU = mybir.AluOpType

D = 1024
KT = 8
RING = 6
NG = 14
NORM_EPS = 1e-6
GN_EPS = 1e-5
MASK_NEG = -30000.0


def cap(ap, free, off=0):
    return AP(ap.tensor, ap.offset + off, [list(ap.ap[0])] + [list(f) for f in free])


class _DryIns:
    def then_inc(self, *a, **kw):
        return self


_DRYINS = _DryIns()
DRY = False
POLICY = {}


class _Probe:
    def __init__(self, eng):
        self.eng = eng
        self.n = 128
        self.name = None

    def __getattr__(self, name):
        real = getattr(self.eng, name) if self.eng is not None else (lambda *a, **kw: _DRYINS)

        def f(*a, **kw):
            self.name = name
            if name == "matmul":
                ap = kw.get("rhs")
            elif name == "bn_stats":
                ap = kw.get("in_")
            else:
                ap = kw.get("out", a[0] if a else None)
            try:
                self.n = int(ap.free_size())
            except Exception:
                self.n = 128
            return real(*a, **kw)
        return f


class Sched:
    def __init__(self, nc, es):
        self.nc = nc
        self.es = es
        self.eng = {"pe": nc.tensor, "act": nc.scalar, "dve": nc.vector, "pool": nc.gpsimd, "sp": nc.sync}
        self.sem = {}
        self.cnt = {}
        for k in self.eng:
            self.sem[k] = es.enter_context(nc.semaphore("sem_" + k))
            self.cnt[k] = 0
        self.seen = {k: {} for k in self.eng}
        self.writer = {}
        self.readers = {}
        self.dsem = {}
        self.dcnt = {}
        self.pending = {k: False for k in self.eng}
        self.nwait = 0
        self.nins = 0
        self.t_eng = {k: 0.0 for k in self.eng}
        self.tok_end = {}
        self.step_end = None

    def _cost(self, e, name, n):
        if e == "pe":
            return max(n, 64) / 2400.0 + 0.015
        if e == "act":
            return 0.22 + n / 1200.0
        if e == "dve":
            return 0.07 + n / 960.0
        if e == "pool":
            return 0.9 if name == "tensor_tensor" and n <= 8 else 0.18 + n / 800.0
        return 0.05

    def _time(self, e, deps, cost, tok, done_extra=0.0):
        ready = 0.0
        for t in deps:
            te = self.tok_end.get(t, 0.0) + (0.15 if t[0] != e else 0.0)
            if te > ready:
                ready = te
        start = max(ready, self.t_eng[e])
        self.t_eng[e] = start + cost
        end = start + cost + done_extra
        if self.tok_end.get(tok, 0.0) < end:
            self.tok_end[tok] = end
        if self.step_end is None or self.step_end < end:
            self.step_end = end

    def _semobj(self, k):
        if isinstance(k, tuple):
            return self.dsem[k[1]]
        return self.sem[k]

    def _deps(self, reads, writes):
        deps = []
        for b in reads:
            if b in self.writer:
                deps.append(self.writer[b])
        for b in writes:
            if b in self.writer:
                deps.append(self.writer[b])
            deps.extend(self.readers.get(b, ()))
        return deps

    def _wait(self, e, deps):
        best = {}
        for (k, v) in deps:
            if best.get(k, 0) < v:
                best[k] = v
        for k, v in best.items():
            if k == e and e == "pe":
                continue
            if self.seen[e].get(k, 0) >= v:
                continue
            if not DRY:
                self.eng[e].wait_ge(self._semobj(k), v)
            self.nwait += 1
            self.seen[e][k] = v

    def _record(self, tok, reads, writes):
        for b in reads:
            self.readers.setdefault(b, []).append(tok)
        for b in writes:
            self.writer[b] = tok
            self.readers[b] = []

    @staticmethod
    def _split(reads, writes):
        isb = lambda b: isinstance(b, str) and len(b) == 2 and b[0] == "b" and b[1].isdigit()
        r2 = [b for b in reads if not isb(b)]
        w2 = list(writes) + [b for b in reads if isb(b)]
        return r2, w2

    def op(self, e, fn, reads=(), writes=(), inc=True):
        reads, writes = self._split(reads, writes)
        deps = self._deps(reads, writes)
        self._wait(e, deps)
        pr = _Probe(None if DRY else self.eng[e])
        ins = fn(pr)
        self.nins += 1
        if inc:
            self.cnt[e] += 1
            ins.then_inc(self.sem[e], 1)
            tok = (e, self.cnt[e])
            self.pending[e] = False
        else:
            tok = (e, self.cnt[e] + 1)
            self.pending[e] = True
        self._time(e, deps, self._cost(e, pr.name, pr.n), tok)
        self._record(tok, reads, writes)
        return ins

    def dma(self, q, out, in_, reads, writes, key):
        if key not in self.dsem:
            self.dsem[key] = None if DRY else self.es.enter_context(self.nc.semaphore("dsem_%d" % len(self.dsem)))
            self.dcnt[key] = 0
        deps = self._deps(reads, writes)
        self._wait(q, deps)
        ins = _DRYINS if DRY else self.eng[q].dma_start(out=out, in_=in_)
        self.nins += 1
        self.dcnt[key] += 16
        ins.then_inc(self.dsem[key], 16)
        tok = (("d", key), self.dcnt[key])
        try:
            nbytes = int(out.free_size()) * 128 * 4
        except Exception:
            nbytes = 1 << 19
        self._time(q, deps, 0.06, tok, done_extra=2.0 + nbytes / 250e3)
        self._record(tok, reads, writes)
        return ins

    def finish(self, q="sp"):
        self.makespan = max(self.t_eng.values())
        if DRY:
            return
        for key, sem in self.dsem.items():
            if self.dcnt[key] > 0:
                self.eng[q].wait_ge(sem, self.dcnt[key])
        for e in self.eng:
            assert not self.pending[e], e
            if e != q and self.cnt[e] > 0:
                self.eng[q].wait_ge(self.sem[e], self.cnt[e])


GREEDY = False


def build(NB, dbg_names=(), stop=99):
    NT = 2 * NB
    S = NT * 128
    nc = bass.Bass("TRN2", target_bir_lowering=False, dynamic_dma_scratch_size=1024)

    def din(name, shape, dt=F32):
        return nc.dram_tensor(name, shape, dt, kind="ExternalInput").ap()

    x_d = din("x", [S, D])
    win_d = din("w_in", [D, 7168])
    woa_d = din("w_oa", [512, D])
    wor_d = din("w_or", [D, D])
    wo_d = din("w_o", [D, D])
    gpk_d = din("gain_pk", [128, 8])
    gnpk_d = din("gn_pk", [128, 8])
    fg_d = din("fgain", [128, D])
    biasT_d = din("biasT", [128, 2048])
    bfar_d = din("bfar", [128, 8])
    ident_d = din("ident", [128, 128])
    Dt_d = din("Dt", [128, 1024])
    cdT_d = din("cdT", [128, 512])
    sd_d = din("sd", [128, 8])
    cd2_d = din("cd2", [128, 4])
    rot_d = din("rot", [NT * 128, 128])
    out_d = nc.dram_tensor("out", [S, D], F32, kind="ExternalOutput").ap()
    wsc_d = nc.dram_tensor("wsc", [NG, 128, 4096], BF16).ap()
    dbg_out = {}

    with ExitStack() as es:
        def sb(name, shape, dt):
            return es.enter_context(nc.sbuf_tensor(name, shape, dt))

        def ps(name, shape, dt):
            return es.enter_context(nc.psum_tensor(name, shape, dt))

        w_oa = sb("w_oa_sb", [128, 4, D], BF16)
        w_or = sb("w_or_sb", [128, 8, D], BF16)
        w_o = sb("w_o_sb", [128, 8, D], BF16)
        NW = 4
        wbuf = [sb("wbuf%d" % i, [128, 8, 512], BF16) for i in range(NW)]
        hT = sb("hT", [128, 8, 256], BF16)
        xy = [sb("xy%d" % i, [128, D], F32) for i in range(4)]
        h_bf = sb("h_bf", [128, D], BF16)
        qT_a2 = [sb("qT_a%d" % i, [128, 4, 256], BF16) for i in range(1)] * 2
        kT_a = sb("kT_a", [128, 4, 768], BF16)
        Vaug = sb("Vaug", [128, 6, 8, 65], BF16)
        gg_a2 = [sb("gg_a%d" % i, [128, 2, 512], BF16) for i in range(1)] * 2
        qk_rot2 = sb("qk_rot", [128, 2, 1024], BF16)
        k_dec = sb("k_dec", [128, 2, 512], BF16)
        qT_r = sb("qT_r", [128, 4, 256], BF16)
        qcT_r = sb("qcT_r", [128, 4, 256], BF16)
        kT_r = sb("kT_r", [128, 4, 256], BF16)
        v_r = sb("v_r", [128, 2, 1024], BF16)
        gg_r = sb("gg_r", [128, 2, 1024], BF16)
        t_ma = sb("t_ma", [128, 2, 1024], BF16)
        t_mr = sb("t_mr", [128, 2, 1024], BF16)
        PT = [sb("PT%d" % i, [128, 5, 128], BF16) for i in range(2)]
        Pr = sb("Pr", [128, 8, 128], BF16)
        state = sb("state", [128, 4, 128], F32)
        state_bf = sb("state_bf", [128, 4, 128], BF16)
        oa_tmp = sb("oa_tmp", [128, 512], F32)
        og_a = sb("og_a", [128, 512], BF16)
        ogT_a = sb("ogT_a", [128, 4, 128], BF16)
        ogT_a_b = sb("ogT_a_b", [128, 4, 128], BF16)
        or_n = sb("or_n", [128, 1024], BF16)
        ogT_r = sb("ogT_r", [128, 8, 128], BF16)
        ogT_r_b = sb("ogT_r_b", [128, 8, 128], BF16)
        u_bf = sb("u_bf", [128, 1024], BF16)
        v_bf = sb("v_bf", [128, 1024], BF16)
        mixedT = sb("mixedT", [128, 8, 128], BF16)
        t1 = sb("t1", [128, 512], F32)
        t2 = sb("t2", [128, 512], F32)
        rot = [sb("rot%d" % i, [128, 128], F32) for i in range(2)]
        Dt = sb("Dt_sb", [128, 8, 128], F32)
        cdT = sb("cdT_sb", [128, 4, 128], F32)
        bias_hi = sb("bias_hi", [128, 8, 256], BF16)
        fgain = sb("fgain_sb", [128, D], F32)
        ident_f = sb("ident_f", [128, 128], F32)
        ident = sb("ident_b", [128, 128], BF16)
        maskT = sb("maskT", [128, 128], BF16)
        sd = sb("sd_sb", [128, 8], F32)
        cd2 = sb("cd2_sb", [128, 4], F32)
        bfar = sb("bfar_sb", [128, 8], F32)
        gpk = sb("gpk_sb", [128, 8], F32)
        gnpk = sb("gnpk_sb", [128, 8], F32)
        gnh = sb("gnh_sb", [128, 8], F32)
        ss = sb("ss", [128, 8], F32)
        ss2 = sb("ss2", [128, 8], F32)
        negh = sb("negh", [128, 8], F32)
        bnst = sb("bnst", [128, 8, 6], F32)
        gq = sb("gq", [128, 5, 8], F32)
        grs = sb("grs", [128, 8], F32)
        gve = sb("gve", [128, 8], F32)
        rinv = sb("rinv", [128, 8], F32)

        psA = ps("psA", [128, 1024], F32)
        psB = ps("psB", [128, 512], F32)
        psC = ps("psC", [128, 1024], F32)
        psD = ps("psD", [128, 1024], F32)
        psT32 = ps("psT", [128, 512], F32)
        psT = psT32[:, :].bitcast(BF16)
        bankap = {0: psA[:, 0:512], 1: psA[:, 512:1024], 2: psB[:, 0:512], 3: psC[:, 0:512], 4: psC[:, 512:1024],
                  5: psD[:, 0:512], 6: psD[:, 512:1024], 7: psT32[:, 0:512]}
        psTv = psT.rearrange("p (k t) -> p k t", k=8)

        sch = Sched(nc, es)
        if True:
            op = sch.op
            dma = sch.dma

            consts = [(ident_f[:], ident_d), (Dt[:].rearrange("p a b -> p (a b)"), Dt_d),
                      (cdT[:].rearrange("p a b -> p (a b)"), cdT_d),
                      (fgain[:], fg_d),
                      (sd[:], sd_d), (cd2[:], cd2_d), (bfar[:], bfar_d), (gpk[:], gpk_d), (gnpk[:], gnpk_d)]
            names = ["ident_f", "Dt", "cdT", "fgain", "sd", "cd2", "bfar", "gpk", "gnpk"]
            for (o, i), nm in zip(consts, names):
                dma("sp", o, i[:, :], [], [nm], "const")
            for nm in names:
                sch.writer[nm] = (("d", "const"), sch.dcnt["const"])
            for hb in range(2):
                xs, xn = xy[2 + hb], ("sin", 2 + hb)
                dma("sp", xs[:], biasT_d[:, hb * 1024:(hb + 1) * 1024], [], [xn], ("sin", 2 + hb))
                hi = bias_hi[:, hb * 4:(hb + 1) * 4, :].rearrange("p a b -> p (a b)")
                op("dve", lambda e: e.tensor_copy(hi, xs[:]), [xn], [("bias_hi", hb)])
            op("dve", lambda e: e.tensor_copy(ident[:], ident_f[:]), ["ident_f"], ["ident"])
            op("dve", lambda e: e.memset(negh[:], -0.5), [], ["negh"])
            op("dve", lambda e: e.memset(maskT[:], 0.0), [], ["maskT"])
            op("dve", lambda e: e.memset(maskT[0:64, 64:128], MASK_NEG), [], ["maskT"])
            op("dve", lambda e: e.memset(state[:], 0.0), [], ["state"])
            op("dve", lambda e: e.memset(state_bf[:], 0.0), [], ["state_bf"])
            op("dve", lambda e: e.memset(Vaug[:, :, :, 64:65], 1.0), [], ["Vaug_ones"])
            op("dve", lambda e: e.memset(PT[0][:], 0.0), [], ["PT0"])
            op("dve", lambda e: e.memset(PT[1][:], 0.0), [], ["PT1"])
            op("dve", lambda e: e.tensor_scalar(out=gnh[:], in0=gnpk[:], scalar1=0.5, scalar2=None,
                                               op0=ALU.mult), ["gnpk"], ["gnh"])

            F32v = lambda t2d: t2d.bitcast(F32)
            sin = [(xy[q][:], ["xy%d" % q]) for q in range(4)]
            for q in range(NW):
                wv = F32v(wbuf[q][:].rearrange("p k c -> p (k c)"))
                sin.append((wv[:, 0:1024], ["wbuf%d" % q]))
                sin.append((wv[:, 1024:2048], ["wbuf%d" % q]))
            sout = [(h_bf[:], ["h_bf"]), (u_bf[:], [("u_bf", 0), ("u_bf", 1)]), (v_bf[:], [("v_bf", 0), ("v_bf", 1)]),
                    (or_n[:], ["or_n"])]
            for s_ in range(2):
                sout.append((t_ma[:, s_, :], [("tm", True, s_, 0), ("tm", True, s_, 1)]))
                sout.append((t_mr[:, s_, :], [("tm", False, s_, 0), ("tm", False, s_, 1)]))
                sout.append((gg_r[:, s_, :], [("gg_r", s_, 0), ("gg_r", s_, 1)]))
            NSI, NSO = len(sin), len(sout)
            win_chunks = [(k, cb) for k in range(KT) for cb in range(7)]
            LA = NSI - 2
            for j in range(len(win_chunks) + LA):
                if j < len(win_chunks):
                    k, cb = win_chunks[j]
                    dma("sp", sin[j % NSI][0], win_d[k * 128:(k + 1) * 128, cb * 1024:(cb + 1) * 1024],
                        [], [("sin", j % NSI)], ("sin", j % NSI))
                if j >= LA:
                    jc = j - LA
                    k, cb = win_chunks[jc]
                    xs, xn = sin[jc % NSI][0], ("sin", jc % NSI)
                    st, sn = sout[jc % NSO][0], ("sout", jc % NSO)
                    if jc % 2 == 0:
                        op("dve", lambda en: en.tensor_scalar(
                            out=st, in0=xs, scalar1=gpk[:, k:k + 1], scalar2=None, op0=ALU.mult),
                           [xn, "gpk"], [sn])
                    else:
                        op("act", lambda en: en.activation(
                            out=st, in_=xs, func=AF.Copy, scale=gpk[:, k:k + 1]),
                           [xn, "gpk"], [sn])
                    dst = AP(wsc_d.tensor, wsc_d.offset + (2 * cb) * 128 * 4096 + k * 512,
                             [[4096, 128], [128 * 4096, 2], [1, 512]])
                    dma("act", dst, st.rearrange("p (g c) -> p g c", g=2),
                        [sn], [("wscq", jc % NSO)], ("sout", jc % NSO))
            for q, (_, names) in enumerate(sin):
                toks = [sch.writer[("sin", q)]] + list(sch.readers.get(("sin", q), []))
                for nm in names:
                    sch.readers.setdefault(nm, []).extend(toks)
            for q, (_, names) in enumerate(sout):
                toks = [sch.writer[("sout", q)]] + list(sch.readers.get(("sout", q), []))
                for nm in names:
                    sch.readers.setdefault(nm, []).extend(toks)

            res_chunks = ([(w_oa, woa_d, k, None, 0.5) for k in range(4)]
                          + [(w_or, wor_d, k, gnh, None) for k in range(8)]
                          + [(w_o, wo_d, k, None, 0.5) for k in range(8)])

            def resident_gen():
                n = len(res_chunks)
                for j in range(n + 1):
                    if j < n:
                        dst, dram, k, sc_t, const = res_chunks[j]
                        sl_ = 2 + j % 2
                        dma("sp", xy[sl_][:], dram[k * 128:(k + 1) * 128, :], [], ["xy%d" % sl_], ("x", sl_))
                    if j >= 1:
                        dst, dram, k, sc_t, const = res_chunks[j - 1]
                        sl_ = 2 + (j - 1) % 2
                        xs, xn = xy[sl_], "xy%d" % sl_
                        wname = ("w", id(dst), k)
                        if sc_t is not None:
                            op("dve", lambda en: en.tensor_scalar(
                                out=dst[:, k, :], in0=xs[:], scalar1=sc_t[:, k:k + 1], scalar2=None,
                                op0=ALU.mult), [xn, "gnh"], [wname])
                        elif j % 2 == 0:
                            op("dve", lambda en: en.tensor_scalar(
                                out=dst[:, k, :], in0=xs[:], scalar1=const, scalar2=None, op0=ALU.mult),
                               [xn], [wname])
                        else:
                            op("act", lambda en: en.activation(
                                out=dst[:, k, :], in_=xs[:], func=AF.Copy, scale=const), [xn], [wname])
                    yield

            def load_x(i):
                slot = i % 2
                dma("sp", xy[slot][:], x_d[i * 128:(i + 1) * 128, :], [], ["xy%d" % slot], ("x", slot))
                dma("sp", rot[slot][:], rot_d[i * 128:(i + 1) * 128, :], [], ["rot%d" % slot], ("rot", slot))

            def load_y(i):
                slot = 2 + (i % 2)
                dma("sp", xy[slot][:], x_d[i * 128:(i + 1) * 128, :], [], ["xy%d" % slot], ("x", slot))

            WORDER = [0, 1, 2, 3, 6, 7, 4, 8, 9, 5, 10, 11, 12, 13]
            WPOS = {g_: p_ for p_, g_ in enumerate(WORDER)}

            def load_w(pos):
                g = WORDER[pos]
                dma("sp", wbuf[pos % NW][:].rearrange("p k c -> p (k c)"), wsc_d[g, :, :],
                    [("wscq", q_) for q_ in range(NSO)], ["wbuf%d" % (pos % NW)], ("w", pos % NW))

            projbanks = [0, 1, 2, 3, 5, 6]
            pb = [0]

            def next_bank():
                b = projbanks[pb[0] % len(projbanks)]
                pb[0] += 1
                return b

            evq = [0]

            def evac_eng():
                evq[0] += 1
                return "act" if evq[0] % 2 else "dve"

            def copy_op(e, out, in_, reads, writes, scale=None):
                if e == "act":
                    if scale is None:
                        op("act", lambda en: en.activation(out=out, in_=in_, func=AF.Copy), reads, writes)
                    else:
                        op("act", lambda en: en.activation(out=out, in_=in_, func=AF.Copy, scale=scale),
                           reads, writes)
                else:
                    if scale is None:
                        op("dve", lambda en: en.tensor_copy(out, in_), reads, writes)
                    else:
                        op("dve", lambda en: en.tensor_scalar(out=out, in0=in_, scalar1=scale, scalar2=None,
                                                            op0=ALU.mult), reads, writes)

            def transposes(src, nblk, reads, tag):
                for k in range(nblk):
                    op("pe", lambda en, k=k: en.transpose(out=psTv[:, k, :], in_=src[:, k * 128:(k + 1) * 128],
                                                         identity=ident[:]),
                       reads + ["ident"], ["b7"], inc=(k == nblk - 1))

            def phaseA1(b):
                for s in range(2):
                    i = 2 * b + s
                    xs = xy[s]
                    xn = "xy%d" % s
                    op("act", lambda en: en.activation(out=t1[:, 0:512].bitcast(BF16), in_=xs[:], func=AF.Square,
                                                       accum_out=ss[:, 0:1]), [xn], ["t1", "ss0"])
                    yield
                    op("pool", lambda en: en.tensor_scalar(out=ss[:, 1:2], in0=ss[:, 0:1], scalar1=1.0 / D,
                                                          scalar2=NORM_EPS, op0=ALU.mult, op1=ALU.add),
                       ["ss0"], ["ss1"])
                    op("pool", lambda en: en.tensor_tensor(out=ss[:, 2:3], in0=ss[:, 1:2], in1=negh[:, 0:1],
                                                          op=ALU.pow), ["ss1", "negh"], ["ss2"])
                    yield
                    yield
                    op("dve", lambda en: en.tensor_scalar(out=h_bf[:], in0=xs[:], scalar1=ss[:, 2:3],
                                                         scalar2=None, op0=ALU.mult), [xn, "ss2"], ["h_bf"])
                    yield
                    transposes(h_bf, 8, ["h_bf"], "h")
                    op("act", lambda en: en.activation(out=hT[:, :, s * 128:(s + 1) * 128], in_=psTv, func=AF.Copy),
                       ["b7"], [("hT", s)])
                    yield

            def phaseA2(b, groups, banks, gate=None):
                projbanks[:] = banks
                deferred = []
                qT_a = qT_a2[b % 2]
                gg_a = gg_a2[b % 2]
                for g in groups:
                    pos = WPOS[g]
                    wb = wbuf[pos % NW]
                    wn = "wbuf%d" % (pos % NW)
                    if g == 10 and gate is not None:
                        yield from wait(gate)
                    if g < 2:
                        for fp in range(2):
                            bk = next_bank()
                            for f2 in range(2):
                                ft = 2 * fp + f2
                                P = bankap[bk][:, f2 * 256:(f2 + 1) * 256]
                                for k in range(KT):
                                    op("pe", lambda en, k=k: en.matmul(out=P, lhsT=wb[:, k, ft * 128:(ft + 1) * 128],
                                                                      rhs=hT[:, k, :], start=(k == 0), stop=(k == KT - 1)),
                                       [wn, ("hT", 0), ("hT", 1)], ["b%d" % bk], inc=(k == KT - 1))
                            yield
                            e = evac_eng()
                            P2 = bankap[bk]
                            if g == 0:
                                copy_op(e, qT_a[:, 2 * fp:2 * fp + 2, :].rearrange("p a b -> p (a b)"), P2, ["b%d" % bk],
                                        [("qT_a", 0, 2 * fp), ("qT_a", 0, 2 * fp + 1)], scale=0.125)
                            else:
                                r0 = ((2 * b) % RING) * 128
                                copy_op(e, kT_a[:, 2 * fp:2 * fp + 2, r0:r0 + 256],
                                        P2.rearrange("p (a b) -> p a b", a=2), ["b%d" % bk],
                                        [("kT_a", 2 * fp + a_, (2 * b + t_) % RING) for a_ in range(2) for t_ in range(2)])
                    else:
                        for s in range(2):
                            i = 2 * b + s
                            bk = next_bank()
                            P = bankap[bk]
                            bn = "b%d" % bk
                            for k in range(KT):
                                op("pe", lambda en, k=k: en.matmul(out=P, lhsT=hT[:, k, s * 128:(s + 1) * 128],
                                                                  rhs=wb[:, k, :], start=(k == 0), stop=(k == KT - 1)),
                                   [wn, ("hT", s)], [bn], inc=(k == KT - 1))
                            yield
                            if deferred and g != 5:
                                deferred.pop(0)()
                            if g == 2:
                                sl = i % RING
                                copy_op(evac_eng(), Vaug[:, sl, :, 0:64],
                                        cap(P, [[64, 8], [1, 64]]), [bn], [("Vaug", sl)])
                            elif g in (3, 8, 9):
                                op("act", lambda en: en.activation(out=t1[:], in_=P, func=AF.Tanh, scale=0.5),
                                   [bn], ["t1"])
                                yield
                                if g == 3:
                                    dst = gg_a[:, s, :]
                                    dn = ("gg_a", 0, s)
                                else:
                                    dst = gg_r[:, s, (g - 8) * 512:(g - 7) * 512]
                                    dn = ("gg_r", s, g - 8)
                                op("dve", lambda en: en.scalar_tensor_tensor(out=dst, in0=t1[:], scalar=1.0, in1=P,
                                                                            op0=ALU.add, op1=ALU.mult),
                                   ["t1", bn], [dn])
                            elif g in (4, 5):
                                qk_rot = qk_rot2[:, s, :]
                                rt = rot[s]
                                rn = "rot%d" % s
                                cc = cap(rt[:, 0:64], [[0, 8], [1, 64]])
                                ssn = cap(rt[:, 64:128], [[0, 8], [32, 2], [1, 32]])
                                Pv = cap(P, [[64, 8], [1, 64]])
                                Psw = cap(P, [[64, 8], [-32, 2], [1, 32]], off=32)
                                op("dve", lambda en: en.tensor_tensor(out=cap(t1[:], [[64, 8], [1, 64]]), in0=Pv,
                                                                     in1=cc, op=ALU.mult), [bn, rn], ["t1"])
                                op("dve", lambda en: en.tensor_tensor(out=cap(t2[:], [[64, 8], [32, 2], [1, 32]]),
                                                                     in0=Psw, in1=ssn, op=ALU.mult), [bn, rn], ["t2"])
                                off = (g - 4) * 512
                                op("dve", lambda en: en.tensor_tensor(out=qk_rot[:, off:off + 512], in0=t1[:],
                                                                     in1=t2[:], op=ALU.add),
                                   ["t1", "t2"], [("qk_rot", s, g - 4)])
                                if g == 5:
                                    op("dve", lambda en: en.tensor_tensor(
                                        out=cap(k_dec[:, s, :], [[64, 8], [1, 64]]),
                                        in0=cap(qk_rot[:, 512:1024], [[64, 8], [1, 64]]),
                                        in1=cap(sd[:], [[1, 8], [0, 64]]), op=ALU.mult),
                                       [("qk_rot", s, 1), "sd"], [("k_dec", s)])
                                    def rot_T(s=s, qk_rot=qk_rot):
                                        transposes(qk_rot, 8, [("qk_rot", s, 0), ("qk_rot", s, 1)], "r")
                                        sl = slice(s * 128, (s + 1) * 128)
                                        op("act", lambda en: en.activation(out=qT_r[:, :, sl], in_=psTv[:, 0:4, :],
                                                                           func=AF.Copy), ["b7"], [("qT_r", s)])
                                        op("act", lambda en: en.activation(out=kT_r[:, :, sl], in_=psTv[:, 4:8, :],
                                                                           func=AF.Copy), ["b7"], [("kT_r", s)])
                                        op("dve", lambda en: en.tensor_tensor(out=qcT_r[:, :, sl], in0=psTv[:, 0:4, :],
                                                                             in1=cdT[:], op=ALU.mult),
                                           ["b7", "cdT"], [("qcT_r", s)])
                                    deferred.append(rot_T)
                            elif g in (6, 7):
                                copy_op(evac_eng(), v_r[:, s, (g - 6) * 512:(g - 5) * 512], P, [bn],
                                        [("v_r", s, g - 6)])
                            else:
                                tgt = t_ma if g < 12 else t_mr
                                hh = (g - 10) % 2
                                op("act", lambda en: en.activation(out=tgt[:, s, hh * 512:(hh + 1) * 512], in_=P,
                                                                   func=AF.Tanh, scale=0.5),
                                   [bn], [("tm", g < 12, s, hh)])
                    if pos + NW < NG:
                        load_w(pos + NW)
                    yield
                while deferred:
                    deferred.pop(0)()

            ogT_a2 = [ogT_a, ogT_a_b]
            ogT_r2 = [ogT_r, ogT_r_b]

            def O_h(h):
                c0 = (h % 4) * 65
                return psC[:, c0:c0 + 65]

            def ORh(h):
                c0 = (h % 2) * 512 + (h // 2) * 128
                return psD[:, c0:c0 + 128]

            def att(i):
                s = i % 2
                sl = slice(s * 128, (s + 1) * 128)
                kts = [kt for kt in range(5) if i - 4 + kt >= 0]
                far = [kt for kt in kts if kt < 3]
                bp = (i // 2) % 2
                qT_a = qT_a2[bp]
                gg_a = gg_a2[bp]

                def nearbank(h):
                    if h % 2 == 0:
                        return psB[:, 0:256], "b2"
                    return psC[:, 512:768], "b4"

                def scores(h):
                    hp, r0 = h // 2, 64 * (h % 2)
                    NB_, nbn = nearbank(h)
                    fb = "b%d" % (h % 2)
                    FA = psA[:, (h % 2) * 512:(h % 2 + 1) * 512]
                    c0 = 0 if 3 in kts else 128
                    masked = 0 in kts
                    if masked:
                        op("pe", lambda en: en.matmul(out=FA[:, 0:128], lhsT=ident[:], rhs=maskT[:], start=True,
                                                      stop=False), ["ident", "maskT"], [fb], inc=False)
                    op("pe", lambda en: en.matmul(out=NB_[:, c0:256], lhsT=ident[:], rhs=bias_hi[:, h, c0:256],
                                                  start=True, stop=False), [("bias_hi", h // 4), "ident"], [nbn],
                       inc=False)
                    first_far = not masked
                    for kt in kts:
                        ring = ((i - 4 + kt) % RING) * 128
                        if kt < 3:
                            out = FA[:, kt * 128:(kt + 1) * 128]
                            bn = fb
                            st, sp = first_far, (kt == far[-1])
                            first_far = False
                        else:
                            out = NB_[:, (kt - 3) * 128:(kt - 2) * 128]
                            bn = nbn
                            st, sp = False, (kt == 4)
                        op("pe", lambda en: en.matmul(out=out, lhsT=kT_a[r0:r0 + 64, hp, ring:ring + 128],
                                                      rhs=qT_a[r0:r0 + 64, hp, sl], start=st, stop=sp),
                           [("kT_a", hp, (i - 4 + kt) % RING), ("qT_a", 0, hp)], [bn],
                           inc=(kt == 4 or (far and kt == far[-1])))

                def softmax(h):
                    pt = PT[h % 2]
                    pn = "PT%d" % (h % 2)
                    SA = psA[:, (h % 2) * 512:(h % 2 + 1) * 512]
                    b01 = "b%d" % (h % 2)
                    if far:
                        a, bnd = far[0], far[-1] + 1
                        op("act", lambda en: en.activation(
                            out=pt[:, a:bnd, :].rearrange("p a b -> p (a b)"), in_=SA[:, a * 128:bnd * 128],
                            func=AF.Exp, bias=bfar[:, h:h + 1]), [b01, "bfar"], [pn])
                    c0 = 0 if 3 in kts else 128
                    NB_, nbn = nearbank(h)
                    op("act", lambda en: en.activation(
                        out=pt[:, 3:5, :].rearrange("p a b -> p (a b)")[:, c0:256], in_=NB_[:, c0:256], func=AF.Exp),
                       [nbn], [pn])

                def pv(h):
                    pt = PT[h % 2]
                    pn = "PT%d" % (h % 2)
                    first = True
                    for kt in kts:
                        slot = (i - 4 + kt) % RING
                        last = (kt == 4)
                        op("pe", lambda en, first=first: en.matmul(out=O_h(h), lhsT=pt[:, kt, :],
                                                                 rhs=Vaug[:, slot, h, :], start=first, stop=last),
                           [pn, ("Vaug", slot), "Vaug_ones"], ["b3"], inc=last)
                        first = False

                def epilogue(half):
                    hs = slice(half * 4, half * 4 + 4)
                    cs = slice(half * 256, half * 256 + 256)
                    op("dve", lambda en: en.reciprocal(out=rinv[:, hs], in_=cap(psC[:, 0:512], [[65, 4]], off=64)),
                       ["b3"], [("rinv", half)])
                    op("dve", lambda en: en.tensor_tensor(
                        out=cap(oa_tmp[:, cs], [[64, 4], [1, 64]]),
                        in0=cap(psC[:, 0:512], [[65, 4], [1, 64]]),
                        in1=cap(rinv[:, hs], [[1, 4], [0, 64]]), op=ALU.mult),
                       ["b3", ("rinv", half)], [("oa_tmp", half)])
                    op("dve", lambda en: en.tensor_tensor(out=og_a[:, cs], in0=oa_tmp[:, cs], in1=gg_a[:, s, cs],
                                                         op=ALU.mult),
                       [("oa_tmp", half), ("gg_a", 0, s)], [("og_a", half)])

                scores(0)
                for h in range(8):
                    if h + 1 < 8:
                        scores(h + 1)
                    softmax(h)
                    yield
                    pv(h)
                    if h % 4 == 3:
                        epilogue(h // 4)
                    yield
                dst = ogT_a2[i % 2]
                transposes(og_a, 4, [("og_a", 0), ("og_a", 1)], "a")
                op("dve", lambda en: en.tensor_copy(dst[:], psTv[:, 0:4, :]), ["b7"], [("ogT_a", i % 2)])
                yield

            def ret(i):
                s = i % 2
                sl = slice(s * 128, (s + 1) * 128)
                for par in range(2):
                    for hh in range(4):
                        h = 2 * hh + par
                        hp, r0 = h // 2, 64 * (h % 2)
                        op("pe", lambda en: en.matmul(out=ORh(h), lhsT=kT_r[r0:r0 + 64, hp, sl],
                                                      rhs=qT_r[r0:r0 + 64, hp, sl], start=True, stop=True),
                           [("kT_r", s), ("qT_r", s)], ["b%d" % (5 + par)], inc=(hh == 3))
                yield
                for par in range(2):
                    op("dve", lambda en: en.tensor_tensor(
                        out=Pr[:, par * 4:(par + 1) * 4, :].rearrange("p a b -> p (a b)"),
                        in0=psD[:, par * 512:(par + 1) * 512],
                        in1=Dt[:, par * 4:(par + 1) * 4, :].rearrange("p a b -> p (a b)"), op=ALU.mult),
                       ["b%d" % (5 + par), "Dt"], [("Pr", par)])
                yield
                cross = i > 0
                for h in range(8):
                    hp, r0 = h // 2, 64 * (h % 2)
                    bn = "b%d" % (5 + h % 2)
                    op("pe", lambda en: en.matmul(out=ORh(h), lhsT=Pr[:, (h % 2) * 4 + h // 2, :],
                                                  rhs=v_r[:, s, h * 128:(h + 1) * 128], start=True, stop=not cross),
                       [("Pr", h % 2), ("v_r", s, h // 4)], [bn], inc=(not cross))
                    if cross:
                        op("pe", lambda en: en.matmul(out=ORh(h), lhsT=qcT_r[r0:r0 + 64, hp, sl],
                                                      rhs=state_bf[r0:r0 + 64, hp, :], start=False, stop=True),
                           [("qcT_r", s), "state_bf"], [bn], inc=True)
                yield
                for h in range(8):
                    op("dve", lambda en: en.bn_stats(out=bnst[:, h, :], in_=ORh(h)),
                       ["b%d" % (5 + h % 2)], ["bnst"])
                    if h % 4 == 3:
                        yield
                me, mo = bnst[:, :, 1], bnst[:, :, 4]
                M2e, M2o = bnst[:, :, 2], bnst[:, :, 5]
                dv = lambda fn, r, w: op("dve", fn, r, w)
                dv(lambda en: en.tensor_tensor(out=gq[:, 0, :], in0=me, in1=mo, op=ALU.add), ["bnst"], ["gq0"])
                dv(lambda en: en.tensor_tensor(out=gq[:, 1, :], in0=me, in1=mo, op=ALU.subtract), ["bnst"], ["gq1"])
                dv(lambda en: en.tensor_tensor(out=gq[:, 2, :], in0=gq[:, 1, :], in1=gq[:, 1, :], op=ALU.mult),
                   ["gq1"], ["gq2"])
                dv(lambda en: en.tensor_tensor(out=gq[:, 3, :], in0=M2e, in1=M2o, op=ALU.add), ["bnst"], ["gq3"])
                dv(lambda en: en.tensor_scalar(out=gq[:, 3, :], in0=gq[:, 3, :], scalar1=1.0 / 128, scalar2=GN_EPS,
                                               op0=ALU.mult, op1=ALU.add), ["gq3"], ["gq3"])
                dv(lambda en: en.scalar_tensor_tensor(out=gve[:], in0=gq[:, 2, :], scalar=0.25, in1=gq[:, 3, :],
                                                      op0=ALU.mult, op1=ALU.add), ["gq2", "gq3"], ["gve"])
                yield
                yield
                op("pool", lambda en: en.tensor_tensor(out=grs[:], in0=gve[:], in1=negh[:], op=ALU.pow),
                   ["gve", "negh"], ["grs"])
                yield
                yield
                yield
                dv(lambda en: en.scalar_tensor_tensor(out=gq[:, 4, :], in0=gq[:, 0, :], scalar=-0.5, in1=grs[:],
                                                      op0=ALU.mult, op1=ALU.mult), ["gq0", "grs"], ["gq4"])
                dv(lambda en: en.tensor_scalar(out=gq[:, 1, :], in0=gq[:, 0, :], scalar1=0.5, scalar2=None,
                                               op0=ALU.mult), ["gq0", "gq1"], ["gq1"])
                yield
                for hh in range(4):
                    h = hh
                    op("act", lambda en: en.activation(out=or_n[:, h * 128:(h + 1) * 128], in_=ORh(h),
                                                       func=AF.Identity, scale=grs[:, h:h + 1],
                                                       bias=gq[:, 4, h:h + 1]),
                       ["b%d" % (5 + h % 2), "gq4", "grs"], [("or_n", h)])
                    h = 4 + hh
                    op("dve", lambda en: en.tensor_scalar(out=or_n[:, h * 128:(h + 1) * 128], in0=ORh(h),
                                                         scalar1=gq[:, 1, h:h + 1], scalar2=grs[:, h:h + 1],
                                                         op0=ALU.subtract, op1=ALU.mult),
                       ["b%d" % (5 + h % 2), "gq1", "grs"], [("or_n", h)])
                    if hh % 2 == 1:
                        yield
                yield
                yield
                op("dve", lambda en: en.tensor_tensor(out=or_n[:], in0=or_n[:], in1=gg_r[:, s, :], op=ALU.mult),
                   [("or_n", h_) for h_ in range(8)] + [("gg_r", s, 0), ("gg_r", s, 1)],
                   ["or_n"] + [("or_n", h_) for h_ in range(8)])
                yield
                yield
                for h in range(8):
                    hp, p0 = h // 2, 64 * (h % 2)
                    op("pe", lambda en: en.matmul(out=psD[p0:p0 + 64, hp * 128:(hp + 1) * 128],
                                                  lhsT=k_dec[:, s, h * 64:(h + 1) * 64],
                                                  rhs=v_r[:, s, h * 128:(h + 1) * 128], start=True, stop=True),
                       [("k_dec", s), ("v_r", s, h // 4)], ["b5"], inc=(h == 7))
                dst = ogT_r2[i % 2]
                transposes(or_n, 8, ["or_n"] + [("or_n", h_) for h_ in range(8)], "r2")
                op("dve", lambda en: en.tensor_copy(dst[:], psTv), ["b7"], [("ogT_r", i % 2)])
                yield
                for hp in range(4):
                    op("dve", lambda en: en.scalar_tensor_tensor(
                        out=state[:, hp, :], in0=state[:, hp, :], scalar=cd2[:, hp:hp + 1],
                        in1=psD[:, hp * 128:(hp + 1) * 128], op0=ALU.mult, op1=ALU.add),
                       ["state", "cd2", "b5"], ["state"])
                yield
                op("pool", lambda en: en.tensor_copy(state_bf[:], state[:]), ["state"], ["state_bf"])
                yield

            def tail(i):
                s = i % 2
                oa = ogT_a2[i % 2]
                orr = ogT_r2[i % 2]
                P4 = psC[:, 512:1024]
                for n in range(2):
                    cs = slice(n * 512, (n + 1) * 512)
                    for k in range(4):
                        op("pe", lambda en: en.matmul(out=P4, lhsT=oa[:, k, :], rhs=w_oa[:, k, cs],
                                                      start=(k == 0), stop=(k == 3)),
                           [("ogT_a", i % 2), ("w", id(w_oa), k)], ["b4"], inc=(k == 3))
                    yield
                    op("dve", lambda en: en.scalar_tensor_tensor(out=u_bf[:, cs], in0=t_ma[:, s, cs], scalar=1.0, in1=P4,
                                                                op0=ALU.add, op1=ALU.mult),
                       [("tm", True, s, n), "b4"], [("u_bf", n)])
                    yield
                    for k in range(8):
                        op("pe", lambda en: en.matmul(out=P4, lhsT=orr[:, k, :], rhs=w_or[:, k, cs],
                                                      start=(k == 0), stop=(k == 7)),
                           [("ogT_r", i % 2), ("w", id(w_or), k)], ["b4"], inc=(k == 7))
                    yield
                    op("dve", lambda en: en.scalar_tensor_tensor(out=v_bf[:, cs], in0=t_mr[:, s, cs], scalar=1.0, in1=P4,
                                                                op0=ALU.add, op1=ALU.mult),
                       [("tm", False, s, n), "b4"], [("v_bf", n)])
                    op("dve", lambda en: en.tensor_tensor(out=u_bf[:, cs], in0=u_bf[:, cs], in1=v_bf[:, cs], op=ALU.add),
                       [("u_bf", n), ("v_bf", n)], [("u_bf", n)])
                    yield
                transposes(u_bf, 8, [("u_bf", 0), ("u_bf", 1)], "m")
                op("act", lambda en: en.activation(out=mixedT[:], in_=psTv, func=AF.Copy), ["b7"], ["mixedT"])
                yield
                yb = xy[2 + s]
                yn = "xy%d" % (2 + s)
                for n in range(2):
                    cs = slice(n * 512, (n + 1) * 512)
                    for k in range(8):
                        op("pe", lambda en: en.matmul(out=P4, lhsT=mixedT[:, k, :], rhs=w_o[:, k, cs],
                                                      start=(k == 0), stop=(k == 7)),
                           ["mixedT", ("w", id(w_o), k)], ["b4"], inc=(k == 7))
                    yield
                    op("dve", lambda en: en.tensor_tensor(out=yb[:, cs], in0=P4, in1=yb[:, cs], op=ALU.add),
                       ["b4", yn], [yn])
                    yield
                op("act", lambda en: en.activation(out=v_bf[:], in_=yb[:], func=AF.Square, accum_out=ss2[:, 0:1]),
                   [yn], [("v_bf", 0), ("v_bf", 1), "ss2_0"])
                yield
                op("pool", lambda en: en.tensor_scalar(out=ss2[:, 1:2], in0=ss2[:, 0:1], scalar1=1.0 / D,
                                                      scalar2=NORM_EPS, op0=ALU.mult, op1=ALU.add),
                   ["ss2_0"], ["ss2_1"])
                op("pool", lambda en: en.tensor_tensor(out=ss2[:, 2:3], in0=ss2[:, 1:2], in1=negh[:, 0:1],
                                                      op=ALU.pow), ["ss2_1", "negh"], ["ss2_2"])
                yield
                op("dve", lambda en: en.scalar_tensor_tensor(out=yb[:], in0=yb[:], scalar=ss2[:, 2:3], in1=fgain[:],
                                                            op0=ALU.mult, op1=ALU.mult),
                   [yn, "ss2_2", "fgain"], [yn])
                dma("sp", out_d[i * 128:(i + 1) * 128, :], yb[:], [yn], [], ("out", s))
                yield

            rrp = [0]

            def run(chains):
                chains = list(chains)
                chains0 = list(chains)
                rrp[0] = 0
                rt = {id(c): 0.0 for c in chains}
                while chains:
                    progressed = False
                    if GREEDY:
                        order = sorted(chains, key=lambda c_: rt[id(c_)])
                    elif POLICY.get("H") is not None:
                        k_ = rrp[0] % len(chains)
                        rr = chains[k_:] + chains[:k_]
                        lo = min(rt[id(c_)] for c_ in chains)
                        order = ([c_ for c_ in rr if rt[id(c_)] <= lo + POLICY["H"]]
                                 + sorted([c_ for c_ in rr if rt[id(c_)] > lo + POLICY["H"]], key=lambda c_: rt[id(c_)]))
                    else:
                        wts = POLICY.get("w", (1, 1, 1, 1))
                        slots = [c_ for ci, c_ in enumerate(chains0) if c_ in chains
                                 for _ in range(wts[min(ci, len(wts) - 1)])]
                        k_ = rrp[0] % len(slots)
                        order = slots[k_:] + slots[:k_]
                    rrp[0] += 1
                    for c in order:
                        sch.step_end = None
                        try:
                            r = next(c)
                        except StopIteration:
                            chains.remove(c)
                            progressed = True
                            break
                        if r == "blocked":
                            continue
                        if sch.step_end is not None:
                            rt[id(c)] = sch.step_end
                        progressed = True
                        break
                    assert progressed

            done = set()

            def mark(gen, key):
                yield from gen
                done.add(key)

            def wait(keys):
                while not all(k in done for k in keys):
                    yield "blocked"

            ATT_G = [0, 1, 2, 3]
            BK3 = [0, 1, 2, 3, 5, 6]
            REST_G = list(range(4, NG))

            def seq(*gens):
                for g in gens:
                    yield from g

            load_x(0)
            load_x(1)
            for g in range(NW):
                load_w(g)
            if stop >= 1:
                run([seq(phaseA1(0), phaseA2(0, WORDER, BK3)), resident_gen()])
            for b in range(NB):
                if stop < 2:
                    break
                nxt = b + 1 < NB
                if nxt:
                    load_x(2 * b + 2)
                    load_x(2 * b + 3)
                    for g in range(NW):
                        load_w(g)
                t0, t1_ = 2 * b, 2 * b + 1
                run([seq(att(t0), att(t1_)), seq(ret(t0), ret(t1_))] + ([phaseA1(b + 1)] if nxt else []))
                load_y(t0)
                load_y(t1_)
                tails = seq(tail(t0), mark(tail(t1_), ("tail", t1_)))
                if nxt:
                    run([tails, phaseA2(b + 1, WORDER, BK3, gate=[("tail", t1_)])])
                else:
                    run([tails])
            sch.finish("sp")

    print("[kernel] build: %d instructions, %d waits, %d dma sems, model %.0f us" % (
        sch.nins, sch.nwait, len(sch.dsem), sch.makespan))
    if DRY:
        return sch.makespan
    return nc


def _tables(NT):
    H = 8
    hs = np.arange(H, dtype=np.float64)
    lg = np.log1p(-np.exp2(-5.0 - hs))
    j = np.arange(128, dtype=np.float64)
    kk = j[:, None]
    qq = j[None, :]
    same = (kk // 64) == (qq // 64)
    past = (kk // 64) < (qq // 64)
    dist = np.abs(qq - kk)
    Dt = np.zeros((128, H, 128))
    for h in range(H):
        Dt[:, (h % 2) * 4 + h // 2, :] = np.where(same | past, np.exp(dist * lg[h]), 0.0) * 0.125
    cdT = np.zeros((128, 4, 128))
    cd2 = np.zeros((128, 4))
    for p in range(128):
        for pr in range(4):
            h = 2 * pr + p // 64
            cdT[p, pr, :] = 0.125 * np.exp((j + 1.0) * lg[h])
            cd2[p, pr] = np.exp(128.0 * lg[h])
    sd = np.exp((127.0 - j)[:, None] * lg[None, :])
    half = 32
    inv_freq = np.power(10000.0, -np.arange(half, dtype=np.float64) / half)
    pos = np.arange(NT * 128, dtype=np.float64)
    ang = pos[:, None] * inv_freq[None, :]
    c, sn = np.cos(ang), np.sin(ang)
    rot = np.concatenate([c, c, -sn, sn], axis=1)
    ident = np.eye(128)
    f = lambda a: np.ascontiguousarray(a, dtype=np.float32)
    return dict(Dt=f(Dt.reshape(128, 1024)), cdT=f(cdT.reshape(128, 512)), cd2=f(cd2), sd=f(sd), rot=f(rot),
                ident=f(ident))


def _bias_tables(rel_bias):
    k = np.arange(128)[:, None]
    q = np.arange(128)[None, :]
    biasT = np.zeros((128, 8, 2, 128), np.float32)
    for jj in range(2):
        dist = q - k + 128 * (1 - jj)
        idx = np.clip(dist, -128, 128) + 128
        biasT[:, :, jj, :] = np.transpose(rel_bias[:, idx], (1, 0, 2))
    m = (k >= 64) & (q < 64)
    biasT[:, :, 1, :] = np.where(m[:, None, :], np.float32(MASK_NEG), biasT[:, :, 1, :])
    bfar = np.ascontiguousarray(np.broadcast_to(rel_bias[:, 256][None, :], (128, 8)), dtype=np.float32)
    return np.ascontiguousarray(biasT.reshape(128, 2048)), bfar


def make_in_maps(x, norm_gain, w_in, rel_bias, gn_gain, w_out_attn, w_out_ret, w_out, final_gain, NT):
    f = lambda a: np.ascontiguousarray(np.asarray(a), dtype=np.float32)
    tabs = _tables(NT)
    biasT, bfar = _bias_tables(f(rel_bias)[0])
    shared = dict(
        w_in=f(w_in)[0], w_oa=f(w_out_attn)[0], w_or=f(w_out_ret)[0], w_o=f(w_out)[0],
        gain_pk=f(f(norm_gain)[0].reshape(8, 128).T), gn_pk=f(f(gn_gain)[0].reshape(8, 128).T),
        fgain=f(np.broadcast_to(f(final_gain)[None, :], (128, D))),
        biasT=biasT, bfar=bfar, **tabs)
    xs = f(x)
    maps = []
    for c in range(xs.shape[0]):
        m = dict(shared)
        m["x"] = np.ascontiguousarray(xs[c, :NT * 128, :])
        maps.append(m)
    return maps


_NC_CACHE = {}


def kernel(x, norm_gain, w_in, rel_bias, gn_gain, w_out_attn, w_out_ret, w_out, final_gain):
    x = np.asarray(x)
    B, S, _ = x.shape
    NT = S // 128
    NB = NT // 2
    if NB not in _NC_CACHE:
        _NC_CACHE[NB] = build(NB)
    nc = _NC_CACHE[NB]
    maps = make_in_maps(x, norm_gain, w_in, rel_bias, gn_gain, w_out_attn, w_out_ret, w_out, final_gain, NT)
    res = run_bass_kernel_spmd(nc, maps, core_ids=list(range(B)))
    out = np.stack([np.asarray(r["out"]) for r in res.results], axis=0)
    return out.astype(np.float32, copy=False)
```

```python
import numpy as np
from contextlib import ExitStack

import concourse.bass as bass
import concourse.mybir as mybir
from concourse.ap import AP
from concourse.bass_utils import run_bass_kernel_spmd

F32 = mybir.dt.float32
BF16 = mybir.dt.bfloat16
AF = mybir.ActivationFunctionType
ALU = mybir.AluOpType

D = 1024
KT = 8
RING = 6
NG = 14
NORM_EPS = 1e-6
GN_EPS = 1e-5
MASK_NEG = -30000.0


def cap(ap, free, off=0):
    return AP(ap.tensor, ap.offset + off, [list(ap.ap[0])] + [list(f) for f in free])


class _DryIns:
    def then_inc(self, *a, **kw):
        return self


_DRYINS = _DryIns()
DRY = False
POLICY = {}


class _Probe:
    def __init__(self, eng):
        self.eng = eng
        self.n = 128
        self.name = None

    def __getattr__(self, name):
        real = getattr(self.eng, name) if self.eng is not None else (lambda *a, **kw: _DRYINS)

        def f(*a, **kw):
            self.name = name
            if name == "matmul":
                ap = kw.get("rhs")
            elif name == "bn_stats":
                ap = kw.get("in_")
            else:
                ap = kw.get("out", a[0] if a else None)
            try:
                self.n = int(ap.free_size())
            except Exception:
                self.n = 128
            return real(*a, **kw)
        return f


class Sched:
    def __init__(self, nc, es):
        self.nc = nc
        self.es = es
        self.eng = {"pe": nc.tensor, "act": nc.scalar, "dve": nc.vector, "pool": nc.gpsimd, "sp": nc.sync}
        self.sem = {}
        self.cnt = {}
        for k in self.eng:
            self.sem[k] = es.enter_context(nc.semaphore("sem_" + k))
            self.cnt[k] = 0
        self.seen = {k: {} for k in self.eng}
        self.writer = {}
        self.readers = {}
        self.dsem = {}
        self.dcnt = {}
        self.pending = {k: False for k in self.eng}
        self.nwait = 0
        self.nins = 0
        self.t_eng = {k: 0.0 for k in self.eng}
        self.tok_end = {}
        self.step_end = None

    def _cost(self, e, name, n):
        if e == "pe":
            return max(n, 64) / 2400.0 + 0.015
        if e == "act":
            return 0.22 + n / 1200.0
        if e == "dve":
            return 0.07 + n / 960.0
        if e == "pool":
            return 0.9 if name == "tensor_tensor" and n <= 8 else 0.18 + n / 800.0
        return 0.05

    def _time(self, e, deps, cost, tok, done_extra=0.0):
        ready = 0.0
        for t in deps:
            te = self.tok_end.get(t, 0.0) + (0.15 if t[0] != e else 0.0)
            if te > ready:
                ready = te
        start = max(ready, self.t_eng[e])
        self.t_eng[e] = start + cost
        end = start + cost + done_extra
        if self.tok_end.get(tok, 0.0) < end:
            self.tok_end[tok] = end
        if self.step_end is None or self.step_end < end:
            self.step_end = end

    def _semobj(self, k):
        if isinstance(k, tuple):
            return self.dsem[k[1]]
        return self.sem[k]

    def _deps(self, reads, writes):
        deps = []
        for b in reads:
            if b in self.writer:
                deps.append(self.writer[b])
        for b in writes:
            if b in self.writer:
                deps.append(self.writer[b])
            deps.extend(self.readers.get(b, ()))
        return deps

    def _wait(self, e, deps):
        best = {}
        for (k, v) in deps:
            if best.get(k, 0) < v:
                best[k] = v
        for k, v in best.items():
            if k == e and e == "pe":
                continue
            if self.seen[e].get(k, 0) >= v:
                continue
            if not DRY:
                self.eng[e].wait_ge(self._semobj(k), v)
            self.nwait += 1
            self.seen[e][k] = v

    def _record(self, tok, reads, writes):
        for b in reads:
            self.readers.setdefault(b, []).append(tok)
        for b in writes:
            self.writer[b] = tok
            self.readers[b] = []

    @staticmethod
    def _split(reads, writes):
        isb = lambda b: isinstance(b, str) and len(b) == 2 and b[0] == "b" and b[1].isdigit()
        r2 = [b for b in reads if not isb(b)]
        w2 = list(writes) + [b for b in reads if isb(b)]
        return r2, w2

    def op(self, e, fn, reads=(), writes=(), inc=True):
        reads, writes = self._split(reads, writes)
        deps = self._deps(reads, writes)
        self._wait(e, deps)
        pr = _Probe(None if DRY else self.eng[e])
        ins = fn(pr)
        self.nins += 1
        if inc:
            self.cnt[e] += 1
            ins.then_inc(self.sem[e], 1)
            tok = (e, self.cnt[e])
            self.pending[e] = False
        else:
            tok = (e, self.cnt[e] + 1)
            self.pending[e] = True
        self._time(e, deps, self._cost(e, pr.name, pr.n), tok)
        self._record(tok, reads, writes)
        return ins

    def dma(self, q, out, in_, reads, writes, key):
        if key not in self.dsem:
            self.dsem[key] = None if DRY else self.es.enter_context(self.nc.semaphore("dsem_%d" % len(self.dsem)))
            self.dcnt[key] = 0
        deps = self._deps(reads, writes)
        self._wait(q, deps)
        ins = _DRYINS if DRY else self.eng[q].dma_start(out=out, in_=in_)
        self.nins += 1
        self.dcnt[key] += 16
        ins.then_inc(self.dsem[key], 16)
        tok = (("d", key), self.dcnt[key])
        try:
            nbytes = int(out.free_size()) * 128 * 4
        except Exception:
            nbytes = 1 << 19
        self._time(q, deps, 0.06, tok, done_extra=2.0 + nbytes / 250e3)
        self._record(tok, reads, writes)
        return ins

    def finish(self, q="sp"):
        self.makespan = max(self.t_eng.values())
        if DRY:
            return
        for key, sem in self.dsem.items():
            if self.dcnt[key] > 0:
                self.eng[q].wait_ge(sem, self.dcnt[key])
        for e in self.eng:
            assert not self.pending[e], e
            if e != q and self.cnt[e] > 0:
                self.eng[q].wait_ge(self.sem[e], self.cnt[e])


GREEDY = False


def build(NB, dbg_names=(), stop=99):
    NT = 2 * NB
    S = NT * 128
    nc = bass.Bass("TRN2", target_bir_lowering=False, dynamic_dma_scratch_size=1024)

    def din(name, shape, dt=F32):
        return nc.dram_tensor(name, shape, dt, kind="ExternalInput").ap()

    x_d = din("x", [S, D])
    win_d = din("w_in", [D, 7168])
    woa_d = din("w_oa", [512, D])
    wor_d = din("w_or", [D, D])
    wo_d = din("w_o", [D, D])
    gpk_d = din("gain_pk", [128, 8])
    gnpk_d = din("gn_pk", [128, 8])
    fg_d = din("fgain", [128, D])
    biasT_d = din("biasT", [128, 2048])
    bfar_d = din("bfar", [128, 8])
    ident_d = din("ident", [128, 128])
    Dt_d = din("Dt", [128, 1024])
    cdT_d = din("cdT", [128, 512])
    sd_d = din("sd", [128, 8])
    cd2_d = din("cd2", [128, 4])
    rot_d = din("rot", [NT * 128, 128])
    out_d = nc.dram_tensor("out", [S, D], F32, kind="ExternalOutput").ap()
    wsc_d = nc.dram_tensor("wsc", [NG, 128, 4096], BF16).ap()
    dbg_out = {}

    with ExitStack() as es:
        def sb(name, shape, dt):
            return es.enter_context(nc.sbuf_tensor(name, shape, dt))

        def ps(name, shape, dt):
            return es.enter_context(nc.psum_tensor(name, shape, dt))

        w_oa = sb("w_oa_sb", [128, 4, D], BF16)
        w_or = sb("w_or_sb", [128, 8, D], BF16)
        w_o = sb("w_o_sb", [128, 8, D], BF16)
        NW = 4
        wbuf = [sb("wbuf%d" % i, [128, 8, 512], BF16) for i in range(NW)]
        hT = sb("hT", [128, 8, 256], BF16)
        xy = [sb("xy%d" % i, [128, D], F32) for i in range(4)]
        h_bf = sb("h_bf", [128, D], BF16)
        qT_a2 = [sb("qT_a%d" % i, [128, 4, 256], BF16) for i in range(1)] * 2
        kT_a = sb("kT_a", [128, 4, 768], BF16)
        Vaug = sb("Vaug", [128, 6, 8, 65], BF16)
        gg_a2 = [sb("gg_a%d" % i, [128, 2, 512], BF16) for i in range(1)] * 2
        qk_rot2 = sb("qk_rot", [128, 2, 1024], BF16)
        k_dec = sb("k_dec", [128, 2, 512], BF16)
        qT_r = sb("qT_r", [128, 4, 256], BF16)
        qcT_r = sb("qcT_r", [128, 4, 256], BF16)
        kT_r = sb("kT_r", [128, 4, 256], BF16)
        v_r = sb("v_r", [128, 2, 1024], BF16)
        gg_r = sb("gg_r", [128, 2, 1024], BF16)
        t_ma = sb("t_ma", [128, 2, 1024], BF16)
        t_mr = sb("t_mr", [128, 2, 1024], BF16)
        PT = [sb("PT%d" % i, [128, 5, 128], BF16) for i in range(2)]
        Pr = sb("Pr", [128, 8, 128], BF16)
        state = sb("state", [128, 4, 128], F32)
        state_bf = sb("state_bf", [128, 4, 128], BF16)
        oa_tmp = sb("oa_tmp", [128, 512], F32)
        og_a = sb("og_a", [128, 512], BF16)
        ogT_a = sb("ogT_a", [128, 4, 128], BF16)
        ogT_a_b = sb("ogT_a_b", [128, 4, 128], BF16)
        or_n = sb("or_n", [128, 1024], BF16)
        ogT_r = sb("ogT_r", [128, 8, 128], BF16)
        ogT_r_b = sb("ogT_r_b", [128, 8, 128], BF16)
        u_bf = sb("u_bf", [128, 1024], BF16)
        v_bf = sb("v_bf", [128, 1024], BF16)
        mixedT = sb("mixedT", [128, 8, 128], BF16)
        t1 = sb("t1", [128, 512], F32)
        t2 = sb("t2", [128, 512], F32)
        rot = [sb("rot%d" % i, [128, 128], F32) for i in range(2)]
        Dt = sb("Dt_sb", [128, 8, 128], F32)
        cdT = sb("cdT_sb", [128, 4, 128], F32)
        bias_hi = sb("bias_hi", [128, 8, 256], BF16)
        fgain = sb("fgain_sb", [128, D], F32)
        ident_f = sb("ident_f", [128, 128], F32)
        ident = sb("ident_b", [128, 128], BF16)
        maskT = sb("maskT", [128, 128], BF16)
        sd = sb("sd_sb", [128, 8], F32)
        cd2 = sb("cd2_sb", [128, 4], F32)
        bfar = sb("bfar_sb", [128, 8], F32)
        gpk = sb("gpk_sb", [128, 8], F32)
        gnpk = sb("gnpk_sb", [128, 8], F32)
        gnh = sb("gnh_sb", [128, 8], F32)
        ss = sb("ss", [128, 8], F32)
        ss2 = sb("ss2", [128, 8], F32)
        negh = sb("negh", [128, 8], F32)
        bnst = sb("bnst", [128, 8, 6], F32)
        gq = sb("gq", [128, 5, 8], F32)
        grs = sb("grs", [128, 8], F32)
        gve = sb("gve", [128, 8], F32)
        rinv = sb("rinv", [128, 8], F32)

        psA = ps("psA", [128, 1024], F32)
        psB = ps("psB", [128, 512], F32)
        psC = ps("psC", [128, 1024], F32)
        psD = ps("psD", [128, 1024], F32)
        psT32 = ps("psT", [128, 512], F32)
        psT = psT32[:, :].bitcast(BF16)
        bankap = {0: psA[:, 0:512], 1: psA[:, 512:1024], 2: psB[:, 0:512], 3: psC[:, 0:512], 4: psC[:, 512:1024],
                  5: psD[:, 0:512], 6: psD[:, 512:1024], 7: psT32[:, 0:512]}
        psTv = psT.rearrange("p (k t) -> p k t", k=8)

        sch = Sched(nc, es)
        if True:
            op = sch.op
            dma = sch.dma

            consts = [(ident_f[:], ident_d), (Dt[:].rearrange("p a b -> p (a b)"), Dt_d),
                      (cdT[:].rearrange("p a b -> p (a b)"), cdT_d),
                      (fgain[:], fg_d),
                      (sd[:], sd_d), (cd2[:], cd2_d), (bfar[:], bfar_d), (gpk[:], gpk_d), (gnpk[:], gnpk_d)]
            names = ["ident_f", "Dt", "cdT", "fgain", "sd", "cd2", "bfar", "gpk", "gnpk"]
            for (o, i), nm in zip(consts, names):
                dma("sp", o, i[:, :], [], [nm], "const")
            for nm in names:
                sch.writer[nm] = (("d", "const"), sch.dcnt["const"])
            for hb in range(2):
                xs, xn = xy[2 + hb], ("sin", 2 + hb)
                dma("sp", xs[:], biasT_d[:, hb * 1024:(hb + 1) * 1024], [], [xn], ("sin", 2 + hb))
                hi = bias_hi[:, hb * 4:(hb + 1) * 4, :].rearrange("p a b -> p (a b)")
                op("dve", lambda e: e.tensor_copy(hi, xs[:]), [xn], [("bias_hi", hb)])
            op("dve", lambda e: e.tensor_copy(ident[:], ident_f[:]), ["ident_f"], ["ident"])
            op("dve", lambda e: e.memset(negh[:], -0.5), [], ["negh"])
            op("dve", lambda e: e.memset(maskT[:], 0.0), [], ["maskT"])
            op("dve", lambda e: e.memset(maskT[0:64, 64:128], MASK_NEG), [], ["maskT"])
            op("dve", lambda e: e.memset(state[:], 0.0), [], ["state"])
            op("dve", lambda e: e.memset(state_bf[:], 0.0), [], ["state_bf"])
            op("dve", lambda e: e.memset(Vaug[:, :, :, 64:65], 1.0), [], ["Vaug_ones"])
            op("dve", lambda e: e.memset(PT[0][:], 0.0), [], ["PT0"])
            op("dve", lambda e: e.memset(PT[1][:], 0.0), [], ["PT1"])
            op("dve", lambda e: e.tensor_scalar(out=gnh[:], in0=gnpk[:], scalar1=0.5, scalar2=None,
                                               op0=ALU.mult), ["gnpk"], ["gnh"])

            F32v = lambda t2d: t2d.bitcast(F32)
            sin = [(xy[q][:], ["xy%d" % q]) for q in range(4)]
            for q in range(NW):
                wv = F32v(wbuf[q][:].rearrange("p k c -> p (k c)"))
                sin.append((wv[:, 0:1024], ["wbuf%d" % q]))
                sin.append((wv[:, 1024:2048], ["wbuf%d" % q]))
            sout = [(h_bf[:], ["h_bf"]), (u_bf[:], [("u_bf", 0), ("u_bf", 1)]), (v_bf[:], [("v_bf", 0), ("v_bf", 1)]),
                    (or_n[:], ["or_n"])]
            for s_ in range(2):
                sout.append((t_ma[:, s_, :], [("tm", True, s_, 0), ("tm", True, s_, 1)]))
                sout.append((t_mr[:, s_, :], [("tm", False, s_, 0), ("tm", False, s_, 1)]))
                sout.append((gg_r[:, s_, :], [("gg_r", s_, 0), ("gg_r", s_, 1)]))
            NSI, NSO = len(sin), len(sout)
            win_chunks = [(k, cb) for k in range(KT) for cb in range(7)]
            LA = NSI - 2
            for j in range(len(win_chunks) + LA):
                if j < len(win_chunks):
                    k, cb = win_chunks[j]
                    dma("sp", sin[j % NSI][0], win_d[k * 128:(k + 1) * 128, cb * 1024:(cb + 1) * 1024],
                        [], [("sin", j % NSI)], ("sin", j % NSI))
                if j >= LA:
                    jc = j - LA
                    k, cb = win_chunks[jc]
                    xs, xn = sin[jc % NSI][0], ("sin", jc % NSI)
                    st, sn = sout[jc % NSO][0], ("sout", jc % NSO)
                    if jc % 2 == 0:
                        op("dve", lambda en: en.tensor_scalar(
                            out=st, in0=xs, scalar1=gpk[:, k:k + 1], scalar2=None, op0=ALU.mult),
                           [xn, "gpk"], [sn])
                    else:
                        op("act", lambda en: en.activation(
                            out=st, in_=xs, func=AF.Copy, scale=gpk[:, k:k + 1]),
                           [xn, "gpk"], [sn])
                    dst = AP(wsc_d.tensor, wsc_d.offset + (2 * cb) * 128 * 4096 + k * 512,
                             [[4096, 128], [128 * 4096, 2], [1, 512]])
                    dma("act", dst, st.rearrange("p (g c) -> p g c", g=2),
                        [sn], [("wscq", jc % NSO)], ("sout", jc % NSO))
            for q, (_, names) in enumerate(sin):
                toks = [sch.writer[("sin", q)]] + list(sch.readers.get(("sin", q), []))
                for nm in names:
                    sch.readers.setdefault(nm, []).extend(toks)
            for q, (_, names) in enumerate(sout):
                toks = [sch.writer[("sout", q)]] + list(sch.readers.get(("sout", q), []))
                for nm in names:
                    sch.readers.setdefault(nm, []).extend(toks)

            res_chunks = ([(w_oa, woa_d, k, None, 0.5) for k in range(4)]
                          + [(w_or, wor_d, k, gnh, None) for k in range(8)]
                          + [(w_o, wo_d, k, None, 0.5) for k in range(8)])

            def resident_gen():
                n = len(res_chunks)
                for j in range(n + 1):
                    if j < n:
                        dst, dram, k, sc_t, const = res_chunks[j]
                        sl_ = 2 + j % 2
                        dma("sp", xy[sl_][:], dram[k * 128:(k + 1) * 128, :], [], ["xy%d" % sl_], ("x", sl_))
                    if j >= 1:
                        dst, dram, k, sc_t, const = res_chunks[j - 1]
                        sl_ = 2 + (j - 1) % 2
                        xs, xn = xy[sl_], "xy%d" % sl_
                        wname = ("w", id(dst), k)
                        if sc_t is not None:
                            op("dve", lambda en: en.tensor_scalar(
                                out=dst[:, k, :], in0=xs[:], scalar1=sc_t[:, k:k + 1], scalar2=None,
                                op0=ALU.mult), [xn, "gnh"], [wname])
                        elif j % 2 == 0:
                            op("dve", lambda en: en.tensor_scalar(
                                out=dst[:, k, :], in0=xs[:], scalar1=const, scalar2=None, op0=ALU.mult),
                               [xn], [wname])
                        else:
                            op("act", lambda en: en.activation(
                                out=dst[:, k, :], in_=xs[:], func=AF.Copy, scale=const), [xn], [wname])
                    yield

            def load_x(i):
                slot = i % 2
                dma("sp", xy[slot][:], x_d[i * 128:(i + 1) * 128, :], [], ["xy%d" % slot], ("x", slot))
                dma("sp", rot[slot][:], rot_d[i * 128:(i + 1) * 128, :], [], ["rot%d" % slot], ("rot", slot))

            def load_y(i):
                slot = 2 + (i % 2)
                dma("sp", xy[slot][:], x_d[i * 128:(i + 1) * 128, :], [], ["xy%d" % slot], ("x", slot))

            WORDER = [0, 1, 2, 3, 6, 7, 4, 8, 9, 5, 10, 11, 12, 13]
            WPOS = {g_: p_ for p_, g_ in enumerate(WORDER)}

            def load_w(pos):
                g = WORDER[pos]
                dma("sp", wbuf[pos % NW][:].rearrange("p k c -> p (k c)"), wsc_d[g, :, :],
                    [("wscq", q_) for q_ in range(NSO)], ["wbuf%d" % (pos % NW)], ("w", pos % NW))

            projbanks = [0, 1, 2, 3, 5, 6]
            pb = [0]

            def next_bank():
                b = projbanks[pb[0] % len(projbanks)]
                pb[0] += 1
                return b

            evq = [0]

            def evac_eng():
                evq[0] += 1
                return "act" if evq[0] % 2 else "dve"

            def copy_op(e, out, in_, reads, writes, scale=None):
                if e == "act":
                    if scale is None:
                        op("act", lambda en: en.activation(out=out, in_=in_, func=AF.Copy), reads, writes)
                    else:
                        op("act", lambda en: en.activation(out=out, in_=in_, func=AF.Copy, scale=scale),
                           reads, writes)
                else:
                    if scale is None:
                        op("dve", lambda en: en.tensor_copy(out, in_), reads, writes)
                    else:
                        op("dve", lambda en: en.tensor_scalar(out=out, in0=in_, scalar1=scale, scalar2=None,
                                                            op0=ALU.mult), reads, writes)

            def transposes(src, nblk, reads, tag):
                for k in range(nblk):
                    op("pe", lambda en, k=k: en.transpose(out=psTv[:, k, :], in_=src[:, k * 128:(k + 1) * 128],
                                                         identity=ident[:]),
                       reads + ["ident"], ["b7"], inc=(k == nblk - 1))

            def phaseA1(b):
                for s in range(2):
                    i = 2 * b + s
                    xs = xy[s]
                    xn = "xy%d" % s
                    op("dve", lambda en: en.scalar_tensor_tensor(out=t1[:, 0:512].bitcast(BF16), in0=xs[:], scalar=1.0,
                                                                in1=xs[:], op0=ALU.mult, op1=ALU.mult,
                                                                accum_out=ss[:, 0:1]), [xn], ["t1", "ss0"])
                    yield
                    op("pool", lambda en: en.tensor_scalar(out=ss[:, 1:2], in0=ss[:, 0:1], scalar1=1.0 / D,
                                                          scalar2=NORM_EPS, op0=ALU.mult, op1=ALU.add),
                       ["ss0"], ["ss1"])
                    op("pool", lambda en: en.tensor_tensor(out=ss[:, 2:3], in0=ss[:, 1:2], in1=negh[:, 0:1],
                                                          op=ALU.pow), ["ss1", "negh"], ["ss2"])
                    yield
                    yield
                    op("dve", lambda en: en.tensor_scalar(out=h_bf[:], in0=xs[:], scalar1=ss[:, 2:3],
                                                         scalar2=None, op0=ALU.mult), [xn, "ss2"], ["h_bf"])
                    yield
                    transposes(h_bf, 8, ["h_bf"], "h")
                    op("dve", lambda en: en.tensor_copy(hT[:, :, s * 128:(s + 1) * 128], psTv), ["b7"], [("hT", s)])
                    yield

            def phaseA2(b, groups, banks, gate=None):
                projbanks[:] = banks
                deferred = []
                qT_a = qT_a2[b % 2]
                gg_a = gg_a2[b % 2]
                for g in groups:
                    pos = WPOS[g]
                    wb = wbuf[pos % NW]
                    wn = "wbuf%d" % (pos % NW)
                    if g == 10 and gate is not None:
                        yield from wait(gate)
                    if g < 2:
                        for fp in range(2):
                            bk = next_bank()
                            for f2 in range(2):
                                ft = 2 * fp + f2
                                P = bankap[bk][:, f2 * 256:(f2 + 1) * 256]
                                for k in range(KT):
                                    op("pe", lambda en, k=k: en.matmul(out=P, lhsT=wb[:, k, ft * 128:(ft + 1) * 128],
                                                                      rhs=hT[:, k, :], start=(k == 0), stop=(k == KT - 1)),
                                       [wn, ("hT", 0), ("hT", 1)], ["b%d" % bk], inc=(k == KT - 1))
                            yield
                            e = evac_eng()
                            P2 = bankap[bk]
                            if g == 0:
                                copy_op(e, qT_a[:, 2 * fp:2 * fp + 2, :].rearrange("p a b -> p (a b)"), P2, ["b%d" % bk],
                                        [("qT_a", 0, 2 * fp), ("qT_a", 0, 2 * fp + 1)], scale=0.125)
                            else:
                                r0 = ((2 * b) % RING) * 128
                                copy_op(e, kT_a[:, 2 * fp:2 * fp + 2, r0:r0 + 256],
                                        P2.rearrange("p (a b) -> p a b", a=2), ["b%d" % bk],
                                        [("kT_a", 2 * fp + a_, (2 * b + t_) % RING) for a_ in range(2) for t_ in range(2)])
                    else:
                        for s in range(2):
                            i = 2 * b + s
                            bk = next_bank()
                            P = bankap[bk]
                            bn = "b%d" % bk
                            for k in range(KT):
                                op("pe", lambda en, k=k: en.matmul(out=P, lhsT=hT[:, k, s * 128:(s + 1) * 128],
                                                                  rhs=wb[:, k, :], start=(k == 0), stop=(k == KT - 1)),
                                   [wn, ("hT", s)], [bn], inc=(k == KT - 1))
                            yield
                            if deferred and g != 5:
                                deferred.pop(0)()
                            if g == 2:
                                sl = i % RING
                                copy_op(evac_eng(), Vaug[:, sl, :, 0:64],
                                        cap(P, [[64, 8], [1, 64]]), [bn], [("Vaug", sl)])
                            elif g in (3, 8, 9):
                                op("act", lambda en: en.activation(out=t1[:], in_=P, func=AF.Tanh, scale=0.5),
                                   [bn], ["t1"])
                                yield
                                if g == 3:
                                    dst = gg_a[:, s, :]
                                    dn = ("gg_a", 0, s)
                                else:
                                    dst = gg_r[:, s, (g - 8) * 512:(g - 7) * 512]
                                    dn = ("gg_r", s, g - 8)
                                op("dve", lambda en: en.scalar_tensor_tensor(out=dst, in0=t1[:], scalar=1.0, in1=P,
                                                                            op0=ALU.add, op1=ALU.mult),
                                   ["t1", bn], [dn])
                            elif g in (4, 5):
                                qk_rot = qk_rot2[:, s, :]
                                rt = rot[s]
                                rn = "rot%d" % s
                                cc = cap(rt[:, 0:64], [[0, 8], [1, 64]])
                                ssn = cap(rt[:, 64:128], [[0, 8], [32, 2], [1, 32]])
                                Pv = cap(P, [[64, 8], [1, 64]])
                                Psw = cap(P, [[64, 8], [-32, 2], [1, 32]], off=32)
                                op("dve", lambda en: en.tensor_tensor(out=cap(t1[:], [[64, 8], [1, 64]]), in0=Pv,
                                                                     in1=cc, op=ALU.mult), [bn, rn], ["t1"])
                                op("dve", lambda en: en.tensor_tensor(out=cap(t2[:], [[64, 8], [32, 2], [1, 32]]),
                                                                     in0=Psw, in1=ssn, op=ALU.mult), [bn, rn], ["t2"])
                                off = (g - 4) * 512
                                op("dve", lambda en: en.tensor_tensor(out=qk_rot[:, off:off + 512], in0=t1[:],
                                                                     in1=t2[:], op=ALU.add),
                                   ["t1", "t2"], [("qk_rot", s, g - 4)])
                                if g == 5:
                                    op("dve", lambda en: en.tensor_tensor(
                                        out=cap(k_dec[:, s, :], [[64, 8], [1, 64]]),
                                        in0=cap(qk_rot[:, 512:1024], [[64, 8], [1, 64]]),
                                        in1=cap(sd[:], [[1, 8], [0, 64]]), op=ALU.mult),
                                       [("qk_rot", s, 1), "sd"], [("k_dec", s)])
                                    def rot_T(s=s, qk_rot=qk_rot):
                                        transposes(qk_rot, 8, [("qk_rot", s, 0), ("qk_rot", s, 1)], "r")
                                        sl = slice(s * 128, (s + 1) * 128)
                                        op("act", lambda en: en.activation(out=qT_r[:, :, sl], in_=psTv[:, 0:4, :],
                                                                           func=AF.Copy), ["b7"], [("qT_r", s)])
                                        op("act", lambda en: en.activation(out=kT_r[:, :, sl], in_=psTv[:, 4:8, :],
                                                                           func=AF.Copy), ["b7"], [("kT_r", s)])
                                        op("dve", lambda en: en.tensor_tensor(out=qcT_r[:, :, sl], in0=psTv[:, 0:4, :],
                                                                             in1=cdT[:], op=ALU.mult),
                                           ["b7", "cdT"], [("qcT_r", s)])
                                    deferred.append(rot_T)
                            elif g in (6, 7):
                                copy_op(evac_eng(), v_r[:, s, (g - 6) * 512:(g - 5) * 512], P, [bn],
                                        [("v_r", s, g - 6)])
                            else:
                                tgt = t_ma if g < 12 else t_mr
                                hh = (g - 10) % 2
                                op("act", lambda en: en.activation(out=tgt[:, s, hh * 512:(hh + 1) * 512], in_=P,
                                                                   func=AF.Tanh, scale=0.5),
                                   [bn], [("tm", g < 12, s, hh)])
                    if pos + NW < NG:
                        load_w(pos + NW)
                    yield
                while deferred:
                    deferred.pop(0)()

            ogT_a2 = [ogT_a, ogT_a_b]
            ogT_r2 = [ogT_r, ogT_r_b]

            def O_h(h):
                c0 = (h % 4) * 65
                return psC[:, c0:c0 + 65]

            def ORh(h):
                c0 = (h % 2) * 512 + (h // 2) * 128
                return psD[:, c0:c0 + 128]

            def att(i):
                s = i % 2
                sl = slice(s * 128, (s + 1) * 128)
                kts = [kt for kt in range(5) if i - 4 + kt >= 0]
                far = [kt for kt in kts if kt < 3]
                bp = (i // 2) % 2
                qT_a = qT_a2[bp]
                gg_a = gg_a2[bp]

                def nearbank(h):
                    if h % 2 == 0:
                        return psB[:, 0:256], "b2"
                    return psC[:, 512:768], "b4"

                def scores(h):
                    hp, r0 = h // 2, 64 * (h % 2)
                    NB_, nbn = nearbank(h)
                    fb = "b%d" % (h % 2)
                    FA = psA[:, (h % 2) * 512:(h % 2 + 1) * 512]
                    c0 = 0 if 3 in kts else 128
                    masked = 0 in kts
                    if masked:
                        op("pe", lambda en: en.matmul(out=FA[:, 0:128], lhsT=ident[:], rhs=maskT[:], start=True,
                                                      stop=False), ["ident", "maskT"], [fb], inc=False)
                    op("pe", lambda en: en.matmul(out=NB_[:, c0:256], lhsT=ident[:], rhs=bias_hi[:, h, c0:256],
                                                  start=True, stop=False), [("bias_hi", h // 4), "ident"], [nbn],
                       inc=False)
                    first_far = not masked
                    for kt in kts:
                        ring = ((i - 4 + kt) % RING) * 128
                        if kt < 3:
                            out = FA[:, kt * 128:(kt + 1) * 128]
                            bn = fb
                            st, sp = first_far, (kt == far[-1])
                            first_far = False
                        else:
                            out = NB_[:, (kt - 3) * 128:(kt - 2) * 128]
                            bn = nbn
                            st, sp = False, (kt == 4)
                        op("pe", lambda en: en.matmul(out=out, lhsT=kT_a[r0:r0 + 64, hp, ring:ring + 128],
                                                      rhs=qT_a[r0:r0 + 64, hp, sl], start=st, stop=sp),
                           [("kT_a", hp, (i - 4 + kt) % RING), ("qT_a", 0, hp)], [bn],
                           inc=(kt == 4 or (far and kt == far[-1])))

                def softmax(h):
                    pt = PT[h % 2]
                    pn = "PT%d" % (h % 2)
                    SA = psA[:, (h % 2) * 512:(h % 2 + 1) * 512]
                    b01 = "b%d" % (h % 2)
                    if far:
                        a, bnd = far[0], far[-1] + 1
                        op("act", lambda en: en.activation(
                            out=pt[:, a:bnd, :].rearrange("p a b -> p (a b)"), in_=SA[:, a * 128:bnd * 128],
                            func=AF.Exp, bias=bfar[:, h:h + 1]), [b01, "bfar"], [pn])
                    c0 = 0 if 3 in kts else 128
                    NB_, nbn = nearbank(h)
                    op("act", lambda en: en.activation(
                        out=pt[:, 3:5, :].rearrange("p a b -> p (a b)")[:, c0:256], in_=NB_[:, c0:256], func=AF.Exp),
                       [nbn], [pn])

                def pv(h):
                    pt = PT[h % 2]
                    pn = "PT%d" % (h % 2)
                    first = True
                    for kt in kts:
                        slot = (i - 4 + kt) % RING
                        last = (kt == 4)
                        op("pe", lambda en, first=first: en.matmul(out=O_h(h), lhsT=pt[:, kt, :],
                                                                 rhs=Vaug[:, slot, h, :], start=first, stop=last),
                           [pn, ("Vaug", slot), "Vaug_ones"], ["b3"], inc=last)
                        first = False

                def epilogue(half):
                    hs = slice(half * 4, half * 4 + 4)
                    cs = slice(half * 256, half * 256 + 256)
                    op("dve", lambda en: en.reciprocal(out=rinv[:, hs], in_=cap(psC[:, 0:512], [[65, 4]], off=64)),
                       ["b3"], [("rinv", half)])
                    op("dve", lambda en: en.tensor_tensor(
                        out=cap(oa_tmp[:, cs], [[64, 4], [1, 64]]),
                        in0=cap(psC[:, 0:512], [[65, 4], [1, 64]]),
                        in1=cap(rinv[:, hs], [[1, 4], [0, 64]]), op=ALU.mult),
                       ["b3", ("rinv", half)], [("oa_tmp", half)])
                    op("dve", lambda en: en.tensor_tensor(out=og_a[:, cs], in0=oa_tmp[:, cs], in1=gg_a[:, s, cs],
                                                         op=ALU.mult),
                       [("oa_tmp", half), ("gg_a", 0, s)], [("og_a", half)])

                scores(0)
                for h in range(8):
                    if h + 1 < 8:
                        scores(h + 1)
                    softmax(h)
                    yield
                    pv(h)
                    if h % 4 == 3:
                        epilogue(h // 4)
                    yield
                dst = ogT_a2[i % 2]
                transposes(og_a, 4, [("og_a", 0), ("og_a", 1)], "a")
                op("dve", lambda en: en.tensor_copy(dst[:], psTv[:, 0:4, :]), ["b7"], [("ogT_a", i % 2)])
                yield

            def ret(i):
                s = i % 2
                sl = slice(s * 128, (s + 1) * 128)
                for par in range(2):
                    for hh in range(4):
                        h = 2 * hh + par
                        hp, r0 = h // 2, 64 * (h % 2)
                        op("pe", lambda en: en.matmul(out=ORh(h), lhsT=kT_r[r0:r0 + 64, hp, sl],
                                                      rhs=qT_r[r0:r0 + 64, hp, sl], start=True, stop=True),
                           [("kT_r", s), ("qT_r", s)], ["b%d" % (5 + par)], inc=(hh == 3))
                yield
                for par in range(2):
                    op("dve", lambda en: en.tensor_tensor(
                        out=Pr[:, par * 4:(par + 1) * 4, :].rearrange("p a b -> p (a b)"),
                        in0=psD[:, par * 512:(par + 1) * 512],
                        in1=Dt[:, par * 4:(par + 1) * 4, :].rearrange("p a b -> p (a b)"), op=ALU.mult),
                       ["b%d" % (5 + par), "Dt"], [("Pr", par)])
                yield
                yield
                cross = i > 0
                for h in range(8):
                    hp, r0 = h // 2, 64 * (h % 2)
                    bn = "b%d" % (5 + h % 2)
                    op("pe", lambda en: en.matmul(out=ORh(h), lhsT=Pr[:, (h % 2) * 4 + h // 2, :],
                                                  rhs=v_r[:, s, h * 128:(h + 1) * 128], start=True, stop=not cross),
                       [("Pr", h % 2), ("v_r", s, h // 4)], [bn], inc=(not cross))
                    if cross:
                        op("pe", lambda en: en.matmul(out=ORh(h), lhsT=qcT_r[r0:r0 + 64, hp, sl],
                                                      rhs=state_bf[r0:r0 + 64, hp, :], start=False, stop=True),
                           [("qcT_r", s), "state_bf"], [bn], inc=True)
                yield
                for h in range(8):
                    op("dve", lambda en: en.bn_stats(out=bnst[:, h, :], in_=ORh(h)),
                       ["b%d" % (5 + h % 2)], ["bnst"])
                    if h % 4 == 3:
                        yield
                me, mo = bnst[:, :, 1], bnst[:, :, 4]
                M2e, M2o = bnst[:, :, 2], bnst[:, :, 5]
                dv = lambda fn, r, w: op("dve", fn, r, w)
                dv(lambda en: en.tensor_tensor(out=gq[:, 0, :], in0=me, in1=mo, op=ALU.add), ["bnst"], ["gq0"])
                dv(lambda en: en.tensor_tensor(out=gq[:, 1, :], in0=me, in1=mo, op=ALU.subtract), ["bnst"], ["gq1"])
                dv(lambda en: en.tensor_tensor(out=gq[:, 2, :], in0=gq[:, 1, :], in1=gq[:, 1, :], op=ALU.mult),
                   ["gq1"], ["gq2"])
                dv(lambda en: en.tensor_tensor(out=gq[:, 3, :], in0=M2e, in1=M2o, op=ALU.add), ["bnst"], ["gq3"])
                dv(lambda en: en.tensor_scalar(out=gq[:, 3, :], in0=gq[:, 3, :], scalar1=1.0 / 128, scalar2=GN_EPS,
                                               op0=ALU.mult, op1=ALU.add), ["gq3"], ["gq3"])
                dv(lambda en: en.scalar_tensor_tensor(out=gve[:], in0=gq[:, 2, :], scalar=0.25, in1=gq[:, 3, :],
                                                      op0=ALU.mult, op1=ALU.add), ["gq2", "gq3"], ["gve"])
                yield
                yield
                op("pool", lambda en: en.tensor_tensor(out=grs[:], in0=gve[:], in1=negh[:], op=ALU.pow),
                   ["gve", "negh"], ["grs"])
                yield
                yield
                yield
                dv(lambda en: en.scalar_tensor_tensor(out=gq[:, 4, :], in0=gq[:, 0, :], scalar=-0.5, in1=grs[:],
                                                      op0=ALU.mult, op1=ALU.mult), ["gq0", "grs"], ["gq4"])
                dv(lambda en: en.tensor_scalar(out=gq[:, 1, :], in0=gq[:, 0, :], scalar1=0.5, scalar2=None,
                                               op0=ALU.mult), ["gq0", "gq1"], ["gq1"])
                yield
                for hh in range(4):
                    h = hh
                    op("act", lambda en: en.activation(out=or_n[:, h * 128:(h + 1) * 128], in_=ORh(h),
                                                       func=AF.Identity, scale=grs[:, h:h + 1],
                                                       bias=gq[:, 4, h:h + 1]),
                       ["b%d" % (5 + h % 2), "gq4", "grs"], [("or_n", h)])
                    h = 4 + hh
                    op("dve", lambda en: en.tensor_scalar(out=or_n[:, h * 128:(h + 1) * 128], in0=ORh(h),
                                                         scalar1=gq[:, 1, h:h + 1], scalar2=grs[:, h:h + 1],
                                                         op0=ALU.subtract, op1=ALU.mult),
                       ["b%d" % (5 + h % 2), "gq1", "grs"], [("or_n", h)])
                    if hh % 2 == 1:
                        yield
                yield
                yield
                op("dve", lambda en: en.tensor_tensor(out=or_n[:], in0=or_n[:], in1=gg_r[:, s, :], op=ALU.mult),
                   [("or_n", h_) for h_ in range(8)] + [("gg_r", s, 0), ("gg_r", s, 1)],
                   ["or_n"] + [("or_n", h_) for h_ in range(8)])
                yield
                yield
                for h in range(8):
                    hp, p0 = h // 2, 64 * (h % 2)
                    op("pe", lambda en: en.matmul(out=psD[p0:p0 + 64, hp * 128:(hp + 1) * 128],
                                                  lhsT=k_dec[:, s, h * 64:(h + 1) * 64],
                                                  rhs=v_r[:, s, h * 128:(h + 1) * 128], start=True, stop=True),
                       [("k_dec", s), ("v_r", s, h // 4)], ["b5"], inc=(h == 7))
                dst = ogT_r2[i % 2]
                transposes(or_n, 8, ["or_n"] + [("or_n", h_) for h_ in range(8)], "r2")
                op("dve", lambda en: en.tensor_copy(dst[:], psTv), ["b7"], [("ogT_r", i % 2)])
                yield
                for hp in range(4):
                    op("dve", lambda en: en.scalar_tensor_tensor(
                        out=state[:, hp, :], in0=state[:, hp, :], scalar=cd2[:, hp:hp + 1],
                        in1=psD[:, hp * 128:(hp + 1) * 128], op0=ALU.mult, op1=ALU.add),
                       ["state", "cd2", "b5"], ["state"])
                yield
                op("pool", lambda en: en.tensor_copy(state_bf[:], state[:]), ["state"], ["state_bf"])
                yield

            def tail(i):
                s = i % 2
                oa = ogT_a2[i % 2]
                orr = ogT_r2[i % 2]
                P4 = psC[:, 512:1024]
                for n in range(2):
                    cs = slice(n * 512, (n + 1) * 512)
                    for k in range(4):
                        op("pe", lambda en: en.matmul(out=P4, lhsT=oa[:, k, :], rhs=w_oa[:, k, cs],
                                                      start=(k == 0), stop=(k == 3)),
                           [("ogT_a", i % 2), ("w", id(w_oa), k)], ["b4"], inc=(k == 3))
                    yield
                    op("dve", lambda en: en.scalar_tensor_tensor(out=u_bf[:, cs], in0=t_ma[:, s, cs], scalar=1.0, in1=P4,
                                                                op0=ALU.add, op1=ALU.mult),
                       [("tm", True, s, n), "b4"], [("u_bf", n)])
                    yield
                    for k in range(8):
                        op("pe", lambda en: en.matmul(out=P4, lhsT=orr[:, k, :], rhs=w_or[:, k, cs],
                                                      start=(k == 0), stop=(k == 7)),
                           [("ogT_r", i % 2), ("w", id(w_or), k)], ["b4"], inc=(k == 7))
                    yield
                    op("dve", lambda en: en.scalar_tensor_tensor(out=v_bf[:, cs], in0=t_mr[:, s, cs], scalar=1.0, in1=P4,
                                                                op0=ALU.add, op1=ALU.mult),
                       [("tm", False, s, n), "b4"], [("v_bf", n)])
                    op("dve", lambda en: en.tensor_tensor(out=u_bf[:, cs], in0=u_bf[:, cs], in1=v_bf[:, cs], op=ALU.add),
                       [("u_bf", n), ("v_bf", n)], [("u_bf", n)])
                    yield
                transposes(u_bf, 8, [("u_bf", 0), ("u_bf", 1)], "m")
                op("act", lambda en: en.activation(out=mixedT[:], in_=psTv, func=AF.Copy), ["b7"], ["mixedT"])
                yield
                yb = xy[2 + s]
                yn = "xy%d" % (2 + s)
                for n in range(2):
                    cs = slice(n * 512, (n + 1) * 512)
                    for k in range(8):
                        op("pe", lambda en: en.matmul(out=P4, lhsT=mixedT[:, k, :], rhs=w_o[:, k, cs],
                                                      start=(k == 0), stop=(k == 7)),
                           ["mixedT", ("w", id(w_o), k)], ["b4"], inc=(k == 7))
                    yield
                    op("dve", lambda en: en.tensor_tensor(out=yb[:, cs], in0=P4, in1=yb[:, cs], op=ALU.add),
                       ["b4", yn], [yn])
                    yield
                op("act", lambda en: en.activation(out=v_bf[:], in_=yb[:], func=AF.Square, accum_out=ss2[:, 0:1]),
                   [yn], [("v_bf", 0), ("v_bf", 1), "ss2_0"])
                yield
                op("pool", lambda en: en.tensor_scalar(out=ss2[:, 1:2], in0=ss2[:, 0:1], scalar1=1.0 / D,
                                                      scalar2=NORM_EPS, op0=ALU.mult, op1=ALU.add),
                   ["ss2_0"], ["ss2_1"])
                op("pool", lambda en: en.tensor_tensor(out=ss2[:, 2:3], in0=ss2[:, 1:2], in1=negh[:, 0:1],
                                                      op=ALU.pow), ["ss2_1", "negh"], ["ss2_2"])
                yield
                op("dve", lambda en: en.scalar_tensor_tensor(out=yb[:], in0=yb[:], scalar=ss2[:, 2:3], in1=fgain[:],
                                                            op0=ALU.mult, op1=ALU.mult),
                   [yn, "ss2_2", "fgain"], [yn])
                dma("sp", out_d[i * 128:(i + 1) * 128, :], yb[:], [yn], [], ("out", s))
                yield

            rrp = [0]

            def run(chains):
                chains = list(chains)
                chains0 = list(chains)
                rrp[0] = 0
                rt = {id(c): 0.0 for c in chains}
                while chains:
                    progressed = False
                    if GREEDY:
                        order = sorted(chains, key=lambda c_: rt[id(c_)])
                    elif POLICY.get("H") is not None:
                        k_ = rrp[0] % len(chains)
                        rr = chains[k_:] + chains[:k_]
                        lo = min(rt[id(c_)] for c_ in chains)
                        order = ([c_ for c_ in rr if rt[id(c_)] <= lo + POLICY["H"]]
                                 + sorted([c_ for c_ in rr if rt[id(c_)] > lo + POLICY["H"]], key=lambda c_: rt[id(c_)]))
                    else:
                        wts = POLICY.get("w", (1, 1, 1, 1))
                        slots = [c_ for ci, c_ in enumerate(chains0) if c_ in chains
                                 for _ in range(wts[min(ci, len(wts) - 1)])]
                        k_ = rrp[0] % len(slots)
                        order = slots[k_:] + slots[:k_]
                    rrp[0] += 1
                    for c in order:
                        sch.step_end = None
                        try:
                            r = next(c)
                        except StopIteration:
                            chains.remove(c)
                            progressed = True
                            break
                        if r == "blocked":
                            continue
                        if sch.step_end is not None:
                            rt[id(c)] = sch.step_end
                        progressed = True
                        break
                    assert progressed

            done = set()

            def mark(gen, key):
                yield from gen
                done.add(key)

            def wait(keys):
                while not all(k in done for k in keys):
                    yield "blocked"

            ATT_G = [0, 1, 2, 3]
            BK3 = [0, 1, 2, 3, 5, 6]
            REST_G = list(range(4, NG))

            def seq(*gens):
                for g in gens:
                    yield from g

            load_x(0)
            load_x(1)
            for g in range(NW):
                load_w(g)
            if stop >= 1:
                run([seq(phaseA1(0), phaseA2(0, WORDER, BK3)), resident_gen()])
            for b in range(NB):
                if stop < 2:
                    break
                nxt = b + 1 < NB
                if nxt:
                    load_x(2 * b + 2)
                    load_x(2 * b + 3)
                    for g in range(NW):
                        load_w(g)
                t0, t1_ = 2 * b, 2 * b + 1
                run([seq(att(t0), att(t1_)), seq(ret(t0), ret(t1_))] + ([phaseA1(b + 1)] if nxt else []))
                load_y(t0)
                load_y(t1_)
                tails = seq(tail(t0), mark(tail(t1_), ("tail", t1_)))
                if nxt:
                    run([tails, phaseA2(b + 1, WORDER, BK3, gate=[("tail", t1_)])])
                else:
                    run([tails])
            sch.finish("sp")

    print("[kernel] build: %d instructions, %d waits, %d dma sems, model %.0f us" % (
        sch.nins, sch.nwait, len(sch.dsem), sch.makespan))
    if DRY:
        return sch.makespan
    return nc


def _tables(NT):
    H = 8
    hs = np.arange(H, dtype=np.float64)
    lg = np.log1p(-np.exp2(-5.0 - hs))
    j = np.arange(128, dtype=np.float64)
    kk = j[:, None]
    qq = j[None, :]
    same = (kk // 64) == (qq // 64)
    past = (kk // 64) < (qq // 64)
    dist = np.abs(qq - kk)
    Dt = np.zeros((128, H, 128))
    for h in range(H):
        Dt[:, (h % 2) * 4 + h // 2, :] = np.where(same | past, np.exp(dist * lg[h]), 0.0) * 0.125
    cdT = np.zeros((128, 4, 128))
    cd2 = np.zeros((128, 4))
    for p in range(128):
        for pr in range(4):
            h = 2 * pr + p // 64
            cdT[p, pr, :] = 0.125 * np.exp((j + 1.0) * lg[h])
            cd2[p, pr] = np.exp(128.0 * lg[h])
    sd = np.exp((127.0 - j)[:, None] * lg[None, :])
    half = 32
    inv_freq = np.power(10000.0, -np.arange(half, dtype=np.float64) / half)
    pos = np.arange(NT * 128, dtype=np.float64)
    ang = pos[:, None] * inv_freq[None, :]
    c, sn = np.cos(ang), np.sin(ang)
    rot = np.concatenate([c, c, -sn, sn], axis=1)
    ident = np.eye(128)
    f = lambda a: np.ascontiguousarray(a, dtype=np.float32)
    return dict(Dt=f(Dt.reshape(128, 1024)), cdT=f(cdT.reshape(128, 512)), cd2=f(cd2), sd=f(sd), rot=f(rot),
                ident=f(ident))


def _bias_tables(rel_bias):
    k = np.arange(128)[:, None]
    q = np.arange(128)[None, :]
    biasT = np.zeros((128, 8, 2, 128), np.float32)
    for jj in range(2):
        dist = q - k + 128 * (1 - jj)
        idx = np.clip(dist, -128, 128) + 128
        biasT[:, :, jj, :] = np.transpose(rel_bias[:, idx], (1, 0, 2))
    m = (k >= 64) & (q < 64)
    biasT[:, :, 1, :] = np.where(m[:, None, :], np.float32(MASK_NEG), biasT[:, :, 1, :])
    bfar = np.ascontiguousarray(np.broadcast_to(rel_bias[:, 256][None, :], (128, 8)), dtype=np.float32)
    return np.ascontiguousarray(biasT.reshape(128, 2048)), bfar


def make_in_maps(x, norm_gain, w_in, rel_bias, gn_gain, w_out_attn, w_out_ret, w_out, final_gain, NT):
    f = lambda a: np.ascontiguousarray(np.asarray(a), dtype=np.float32)
    tabs = _tables(NT)
    biasT, bfar = _bias_tables(f(rel_bias)[0])
    shared = dict(
        w_in=f(w_in)[0], w_oa=f(w_out_attn)[0], w_or=f(w_out_ret)[0], w_o=f(w_out)[0],
        gain_pk=f(f(norm_gain)[0].reshape(8, 128).T), gn_pk=f(f(gn_gain)[0].reshape(8, 128).T),
        fgain=f(np.broadcast_to(f(final_gain)[None, :], (128, D))),
        biasT=biasT, bfar=bfar, **tabs)
    xs = f(x)
    maps = []
    for c in range(xs.shape[0]):
        m = dict(shared)
        m["x"] = np.ascontiguousarray(xs[c, :NT * 128, :])
        maps.append(m)
    return maps


_NC_CACHE = {}


def kernel(x, norm_gain, w_in, rel_bias, gn_gain, w_out_attn, w_out_ret, w_out, final_gain):
    x = np.asarray(x)
    B, S, _ = x.shape
    NT = S // 128
    NB = NT // 2
    if NB not in _NC_CACHE:
        _NC_CACHE[NB] = build(NB)
    nc = _NC_CACHE[NB]
    maps = make_in_maps(x, norm_gain, w_in, rel_bias, gn_gain, w_out_attn, w_out_ret, w_out, final_gain, NT)
    res = run_bass_kernel_spmd(nc, maps, core_ids=list(range(B)))
    out = np.stack([np.asarray(r["out"]) for r in res.results], axis=0)
    return out.astype(np.float32, copy=False)
```

```python
import numpy as np
from contextlib import ExitStack

import concourse.bass as bass
import concourse.mybir as mybir
from concourse.ap import AP
from concourse.bass_utils import run_bass_kernel_spmd

F32 = mybir.dt.float32
BF16 = mybir.dt.bfloat16
AF = mybir.ActivationFunctionType
ALU = mybir.AluOpType

D = 1024
KT = 8
RING = 6
NG = 14
NORM_EPS = 1e-6
GN_EPS = 1e-5
MASK_NEG = -30000.0


def cap(ap, free, off=0):
    return AP(ap.tensor, ap.offset + off, [list(ap.ap[0])] + [list(f) for f in free])


class _DryIns:
    def then_inc(self, *a, **kw):
        return self


_DRYINS = _DryIns()
DRY = False
POLICY = {}


class _Probe:
    def __init__(self, eng):
        self.eng = eng
        self.n = 128
        self.name = None

    def __getattr__(self, name):
        real = getattr(self.eng, name) if self.eng is not None else (lambda *a, **kw: _DRYINS)

        def f(*a, **kw):
            self.name = name
            if name == "matmul":
                ap = kw.get("rhs")
            elif name == "bn_stats":
                ap = kw.get("in_")
            else:
                ap = kw.get("out", a[0] if a else None)
            try:
                self.n = int(ap.free_size())
            except Exception:
                self.n = 128
            return real(*a, **kw)
        return f


class Sched:
    def __init__(self, nc, es):
        self.nc = nc
        self.es = es
        self.eng = {"pe": nc.tensor, "act": nc.scalar, "dve": nc.vector, "pool": nc.gpsimd, "sp": nc.sync}
        self.sem = {}
        self.cnt = {}
        for k in self.eng:
            self.sem[k] = es.enter_context(nc.semaphore("sem_" + k))
            self.cnt[k] = 0
        self.seen = {k: {} for k in self.eng}
        self.writer = {}
        self.readers = {}
        self.dsem = {}
        self.dcnt = {}
        self.pending = {k: False for k in self.eng}
        self.nwait = 0
        self.nins = 0
        self.t_eng = {k: 0.0 for k in self.eng}
        self.tok_end = {}
        self.step_end = None

    def _cost(self, e, name, n):
        if e == "pe":
            return max(n, 64) / 2400.0 + 0.015
        if e == "act":
            return 0.22 + n / 1200.0
        if e == "dve":
            return 0.07 + n / 960.0
        if e == "pool":
            return 0.9 if name == "tensor_tensor" and n <= 8 else 0.18 + n / 800.0
        return 0.05

    def _time(self, e, deps, cost, tok, done_extra=0.0):
        ready = 0.0
        for t in deps:
            te = self.tok_end.get(t, 0.0) + (0.15 if t[0] != e else 0.0)
            if te > ready:
                ready = te
        start = max(ready, self.t_eng[e])
        self.t_eng[e] = start + cost
        end = start + cost + done_extra
        if self.tok_end.get(tok, 0.0) < end:
            self.tok_end[tok] = end
        if self.step_end is None or self.step_end < end:
            self.step_end = end

    def _semobj(self, k):
        if isinstance(k, tuple):
            return self.dsem[k[1]]
        return self.sem[k]

    def _deps(self, reads, writes):
        deps = []
        for b in reads:
            if b in self.writer:
                deps.append(self.writer[b])
        for b in writes:
            if b in self.writer:
                deps.append(self.writer[b])
            deps.extend(self.readers.get(b, ()))
        return deps

    def _wait(self, e, deps):
        best = {}
        for (k, v) in deps:
            if best.get(k, 0) < v:
                best[k] = v
        for k, v in best.items():
            if k == e and e == "pe":
                continue
            if self.seen[e].get(k, 0) >= v:
                continue
            if not DRY:
                self.eng[e].wait_ge(self._semobj(k), v)
            self.nwait += 1
            self.seen[e][k] = v

    def _record(self, tok, reads, writes):
        for b in reads:
            self.readers.setdefault(b, []).append(tok)
        for b in writes:
            self.writer[b] = tok
            self.readers[b] = []

    @staticmethod
    def _split(reads, writes):
        isb = lambda b: isinstance(b, str) and len(b) == 2 and b[0] == "b" and b[1].isdigit()
        r2 = [b for b in reads if not isb(b)]
        w2 = list(writes) + [b for b in reads if isb(b)]
        return r2, w2

    def op(self, e, fn, reads=(), writes=(), inc=True):
        reads, writes = self._split(reads, writes)
        deps = self._deps(reads, writes)
        self._wait(e, deps)
        pr = _Probe(None if DRY else self.eng[e])
        ins = fn(pr)
        self.nins += 1
        if inc:
            self.cnt[e] += 1
            ins.then_inc(self.sem[e], 1)
            tok = (e, self.cnt[e])
            self.pending[e] = False
        else:
            tok = (e, self.cnt[e] + 1)
            self.pending[e] = True
        self._time(e, deps, self._cost(e, pr.name, pr.n), tok)
        self._record(tok, reads, writes)
        return ins

    def dma(self, q, out, in_, reads, writes, key):
        if key not in self.dsem:
            self.dsem[key] = None if DRY else self.es.enter_context(self.nc.semaphore("dsem_%d" % len(self.dsem)))
            self.dcnt[key] = 0
        deps = self._deps(reads, writes)
        self._wait(q, deps)
        ins = _DRYINS if DRY else self.eng[q].dma_start(out=out, in_=in_)
        self.nins += 1
        self.dcnt[key] += 16
        ins.then_inc(self.dsem[key], 16)
        tok = (("d", key), self.dcnt[key])
        try:
            nbytes = int(out.free_size()) * 128 * 4
        except Exception:
            nbytes = 1 << 19
        self._time(q, deps, 0.06, tok, done_extra=2.0 + nbytes / 250e3)
        self._record(tok, reads, writes)
        return ins

    def finish(self, q="sp"):
        self.makespan = max(self.t_eng.values())
        if DRY:
            return
        for key, sem in self.dsem.items():
            if self.dcnt[key] > 0:
                self.eng[q].wait_ge(sem, self.dcnt[key])
        for e in self.eng:
            assert not self.pending[e], e
            if e != q and self.cnt[e] > 0:
                self.eng[q].wait_ge(self.sem[e], self.cnt[e])


GREEDY = False


def build(NB, dbg_names=(), stop=99):
    NT = 2 * NB
    S = NT * 128
    nc = bass.Bass("TRN2", target_bir_lowering=False, dynamic_dma_scratch_size=1024)

    def din(name, shape, dt=F32):
        return nc.dram_tensor(name, shape, dt, kind="ExternalInput").ap()

    x_d = din("x", [S, D])
    win_d = din("w_in", [D, 7168])
    woa_d = din("w_oa", [512, D])
    wor_d = din("w_or", [D, D])
    wo_d = din("w_o", [D, D])
    gpk_d = din("gain_pk", [128, 8])
    gnpk_d = din("gn_pk", [128, 8])
    fg_d = din("fgain", [128, D])
    biasT_d = din("biasT", [128, 2048])
    bfar_d = din("bfar", [128, 8])
    ident_d = din("ident", [128, 128])
    Dt_d = din("Dt", [128, 1024])
    cdT_d = din("cdT", [128, 512])
    sd_d = din("sd", [128, 8])
    cd2_d = din("cd2", [128, 4])
    rot_d = din("rot", [NT * 128, 128])
    out_d = nc.dram_tensor("out", [S, D], F32, kind="ExternalOutput").ap()
    wsc_d = nc.dram_tensor("wsc", [NG, 128, 4096], BF16).ap()
    dbg_out = {}

    with ExitStack() as es:
        def sb(name, shape, dt):
            return es.enter_context(nc.sbuf_tensor(name, shape, dt))

        def ps(name, shape, dt):
            return es.enter_context(nc.psum_tensor(name, shape, dt))

        w_oa = sb("w_oa_sb", [128, 4, D], BF16)
        w_or = sb("w_or_sb", [128, 8, D], BF16)
        w_o = sb("w_o_sb", [128, 8, D], BF16)
        NW = 4
        wbuf = [sb("wbuf%d" % i, [128, 8, 512], BF16) for i in range(NW)]
        hT = sb("hT", [128, 8, 256], BF16)
        xy = [sb("xy%d" % i, [128, D], F32) for i in range(4)]
        h_bf = sb("h_bf", [128, D], BF16)
        qT_a2 = [sb("qT_a%d" % i, [128, 4, 256], BF16) for i in range(1)] * 2
        kT_a = sb("kT_a", [128, 4, 768], BF16)
        Vaug = sb("Vaug", [128, 6, 8, 65], BF16)
        gg_a2 = [sb("gg_a%d" % i, [128, 2, 512], BF16) for i in range(1)] * 2
        qk_rot2 = sb("qk_rot", [128, 2, 1024], BF16)
        k_dec = sb("k_dec", [128, 2, 512], BF16)
        qT_r = sb("qT_r", [128, 4, 256], BF16)
        qcT_r = sb("qcT_r", [128, 4, 256], BF16)
        kT_r = sb("kT_r", [128, 4, 256], BF16)
        v_r = sb("v_r", [128, 2, 1024], BF16)
        gg_r = sb("gg_r", [128, 2, 1024], BF16)
        t_ma = sb("t_ma", [128, 2, 1024], BF16)
        t_mr = sb("t_mr", [128, 2, 1024], BF16)
        PT = [sb("PT%d" % i, [128, 5, 128], BF16) for i in range(2)]
        Pr = sb("Pr", [128, 8, 128], BF16)
        state = sb("state", [128, 4, 128], F32)
        state_bf = sb("state_bf", [128, 4, 128], BF16)
        oa_tmp = sb("oa_tmp", [128, 512], F32)
        og_a = sb("og_a", [128, 512], BF16)
        ogT_a = sb("ogT_a", [128, 4, 128], BF16)
        ogT_a_b = sb("ogT_a_b", [128, 4, 128], BF16)
        or_n = sb("or_n", [128, 1024], BF16)
        ogT_r = sb("ogT_r", [128, 8, 128], BF16)
        ogT_r_b = sb("ogT_r_b", [128, 8, 128], BF16)
        u_bf = sb("u_bf", [128, 1024], BF16)
        v_bf = sb("v_bf", [128, 1024], BF16)
        mixedT = sb("mixedT", [128, 8, 128], BF16)
        t1 = sb("t1", [128, 512], F32)
        t2 = sb("t2", [128, 512], F32)
        rot = [sb("rot%d" % i, [128, 128], F32) for i in range(2)]
        Dt = sb("Dt_sb", [128, 8, 128], F32)
        cdT = sb("cdT_sb", [128, 4, 128], F32)
        bias_hi = sb("bias_hi", [128, 8, 256], BF16)
        fgain = sb("fgain_sb", [128, D], F32)
        ident_f = sb("ident_f", [128, 128], F32)
        ident = sb("ident_b", [128, 128], BF16)
        maskT = sb("maskT", [128, 128], BF16)
        sd = sb("sd_sb", [128, 8], F32)
        cd2 = sb("cd2_sb", [128, 4], F32)
        bfar = sb("bfar_sb", [128, 8], F32)
        gpk = sb("gpk_sb", [128, 8], F32)
        gnpk = sb("gnpk_sb", [128, 8], F32)
        gnh = sb("gnh_sb", [128, 8], F32)
        ss = sb("ss", [128, 8], F32)
        ss2 = sb("ss2", [128, 8], F32)
        negh = sb("negh", [128, 8], F32)
        bnst = sb("bnst", [128, 8, 6], F32)
        gq = sb("gq", [128, 5, 8], F32)
        grs = sb("grs", [128, 8], F32)
        gve = sb("gve", [128, 8], F32)
        rinv = sb("rinv", [128, 8], F32)

        psA = ps("psA", [128, 1024], F32)
        psB = ps("psB", [128, 512], F32)
        psC = ps("psC", [128, 1024], F32)
        psD = ps("psD", [128, 1024], F32)
        psT32 = ps("psT", [128, 512], F32)
        psT = psT32[:, :].bitcast(BF16)
        bankap = {0: psA[:, 0:512], 1: psA[:, 512:1024], 2: psB[:, 0:512], 3: psC[:, 0:512], 4: psC[:, 512:1024],
                  5: psD[:, 0:512], 6: psD[:, 512:1024], 7: psT32[:, 0:512]}
        psTv = psT.rearrange("p (k t) -> p k t", k=8)

        sch = Sched(nc, es)
        if True:
            op = sch.op
            dma = sch.dma

            consts = [(ident_f[:], ident_d), (Dt[:].rearrange("p a b -> p (a b)"), Dt_d),
                      (cdT[:].rearrange("p a b -> p (a b)"), cdT_d),
                      (fgain[:], fg_d),
                      (sd[:], sd_d), (cd2[:], cd2_d), (bfar[:], bfar_d), (gpk[:], gpk_d), (gnpk[:], gnpk_d)]
            names = ["ident_f", "Dt", "cdT", "fgain", "sd", "cd2", "bfar", "gpk", "gnpk"]
            for (o, i), nm in zip(consts, names):
                dma("sp", o, i[:, :], [], [nm], "const")
            for nm in names:
                sch.writer[nm] = (("d", "const"), sch.dcnt["const"])
            for hb in range(2):
                xs, xn = xy[2 + hb], ("sin", 2 + hb)
                dma("sp", xs[:], biasT_d[:, hb * 1024:(hb + 1) * 1024], [], [xn], ("sin", 2 + hb))
                hi = bias_hi[:, hb * 4:(hb + 1) * 4, :].rearrange("p a b -> p (a b)")
                op("dve", lambda e: e.tensor_copy(hi, xs[:]), [xn], [("bias_hi", hb)])
            op("dve", lambda e: e.tensor_copy(ident[:], ident_f[:]), ["ident_f"], ["ident"])
            op("dve", lambda e: e.memset(negh[:], -0.5), [], ["negh"])
            op("dve", lambda e: e.memset(maskT[:], 0.0), [], ["maskT"])
            op("dve", lambda e: e.memset(maskT[0:64, 64:128], MASK_NEG), [], ["maskT"])
            op("dve", lambda e: e.memset(state[:], 0.0), [], ["state"])
            op("dve", lambda e: e.memset(state_bf[:], 0.0), [], ["state_bf"])
            op("dve", lambda e: e.memset(Vaug[:, :, :, 64:65], 1.0), [], ["Vaug_ones"])
            op("dve", lambda e: e.memset(PT[0][:], 0.0), [], ["PT0"])
            op("dve", lambda e: e.memset(PT[1][:], 0.0), [], ["PT1"])
            op("dve", lambda e: e.tensor_scalar(out=gnh[:], in0=gnpk[:], scalar1=0.5, scalar2=None,
                                               op0=ALU.mult), ["gnpk"], ["gnh"])

            F32v = lambda t2d: t2d.bitcast(F32)
            sin = [(xy[q][:], ["xy%d" % q]) for q in range(4)]
            for q in range(NW):
                wv = F32v(wbuf[q][:].rearrange("p k c -> p (k c)"))
                sin.append((wv[:, 0:1024], ["wbuf%d" % q]))
                sin.append((wv[:, 1024:2048], ["wbuf%d" % q]))
            sout = [(h_bf[:], ["h_bf"]), (u_bf[:], [("u_bf", 0), ("u_bf", 1)]), (v_bf[:], [("v_bf", 0), ("v_bf", 1)]),
                    (or_n[:], ["or_n"])]
            for s_ in range(2):
                sout.append((t_ma[:, s_, :], [("tm", True, s_, 0), ("tm", True, s_, 1)]))
                sout.append((t_mr[:, s_, :], [("tm", False, s_, 0), ("tm", False, s_, 1)]))
                sout.append((gg_r[:, s_, :], [("gg_r", s_, 0), ("gg_r", s_, 1)]))
            NSI, NSO = len(sin), len(sout)
            win_chunks = [(k, cb) for k in range(KT) for cb in range(7)]
            LA = NSI - 2
            for j in range(len(win_chunks) + LA):
                if j < len(win_chunks):
                    k, cb = win_chunks[j]
                    dma("sp", sin[j % NSI][0], win_d[k * 128:(k + 1) * 128, cb * 1024:(cb + 1) * 1024],
                        [], [("sin", j % NSI)], ("sin", j % NSI))
                if j >= LA:
                    jc = j - LA
                    k, cb = win_chunks[jc]
                    xs, xn = sin[jc % NSI][0], ("sin", jc % NSI)
                    st, sn = sout[jc % NSO][0], ("sout", jc % NSO)
                    if jc % 2 == 0:
                        op("dve", lambda en: en.tensor_scalar(
                            out=st, in0=xs, scalar1=gpk[:, k:k + 1], scalar2=None, op0=ALU.mult),
                           [xn, "gpk"], [sn])
                    else:
                        op("act", lambda en: en.activation(
                            out=st, in_=xs, func=AF.Copy, scale=gpk[:, k:k + 1]),
                           [xn, "gpk"], [sn])
                    dst = AP(wsc_d.tensor, wsc_d.offset + (2 * cb) * 128 * 4096 + k * 512,
                             [[4096, 128], [128 * 4096, 2], [1, 512]])
                    dma("act", dst, st.rearrange("p (g c) -> p g c", g=2),
                        [sn], [("wscq", jc % NSO)], ("sout", jc % NSO))
            for q, (_, names) in enumerate(sin):
                toks = [sch.writer[("sin", q)]] + list(sch.readers.get(("sin", q), []))
                for nm in names:
                    sch.readers.setdefault(nm, []).extend(toks)
            for q, (_, names) in enumerate(sout):
                toks = [sch.writer[("sout", q)]] + list(sch.readers.get(("sout", q), []))
                for nm in names:
                    sch.readers.setdefault(nm, []).extend(toks)

            res_chunks = ([(w_oa, woa_d, k, None, 0.5) for k in range(4)]
                          + [(w_or, wor_d, k, gnh, None) for k in range(8)]
                          + [(w_o, wo_d, k, None, 0.5) for k in range(8)])

            def resident_gen():
                n = len(res_chunks)
                for j in range(n + 1):
                    if j < n:
                        dst, dram, k, sc_t, const = res_chunks[j]
                        sl_ = 2 + j % 2
                        dma("sp", xy[sl_][:], dram[k * 128:(k + 1) * 128, :], [], ["xy%d" % sl_], ("x", sl_))
                    if j >= 1:
                        dst, dram, k, sc_t, const = res_chunks[j - 1]
                        sl_ = 2 + (j - 1) % 2
                        xs, xn = xy[sl_], "xy%d" % sl_
                        wname = ("w", id(dst), k)
                        if sc_t is not None:
                            op("dve", lambda en: en.tensor_scalar(
                                out=dst[:, k, :], in0=xs[:], scalar1=sc_t[:, k:k + 1], scalar2=None,
                                op0=ALU.mult), [xn, "gnh"], [wname])
                        elif j % 2 == 0:
                            op("dve", lambda en: en.tensor_scalar(
                                out=dst[:, k, :], in0=xs[:], scalar1=const, scalar2=None, op0=ALU.mult),
                               [xn], [wname])
                        else:
                            op("act", lambda en: en.activation(
                                out=dst[:, k, :], in_=xs[:], func=AF.Copy, scale=const), [xn], [wname])
                    yield

            def load_x(i):
                slot = i % 2
                dma("sp", xy[slot][:], x_d[i * 128:(i + 1) * 128, :], [], ["xy%d" % slot], ("x", slot))
                dma("sp", rot[slot][:], rot_d[i * 128:(i + 1) * 128, :], [], ["rot%d" % slot], ("rot", slot))

            def load_y(i):
                slot = 2 + (i % 2)
                dma("sp", xy[slot][:], x_d[i * 128:(i + 1) * 128, :], [], ["xy%d" % slot], ("x", slot))

            WORDER = [0, 1, 2, 3, 6, 7, 4, 8, 9, 5, 10, 11, 12, 13]
            WPOS = {g_: p_ for p_, g_ in enumerate(WORDER)}

            def load_w(pos):
                g = WORDER[pos]
                dma("sp", wbuf[pos % NW][:].rearrange("p k c -> p (k c)"), wsc_d[g, :, :],
                    [("wscq", q_) for q_ in range(NSO)], ["wbuf%d" % (pos % NW)], ("w", pos % NW))

            projbanks = [0, 1, 2, 3, 5, 6]
            pb = [0]

            def next_bank():
                b = projbanks[pb[0] % len(projbanks)]
                pb[0] += 1
                return b

            evq = [0]

            def evac_eng():
                evq[0] += 1
                return "act" if evq[0] % 2 else "dve"

            def copy_op(e, out, in_, reads, writes, scale=None):
                if e == "act":
                    if scale is None:
                        op("act", lambda en: en.activation(out=out, in_=in_, func=AF.Copy), reads, writes)
                    else:
                        op("act", lambda en: en.activation(out=out, in_=in_, func=AF.Copy, scale=scale),
                           reads, writes)
                else:
                    if scale is None:
                        op("dve", lambda en: en.tensor_copy(out, in_), reads, writes)
                    else:
                        op("dve", lambda en: en.tensor_scalar(out=out, in0=in_, scalar1=scale, scalar2=None,
                                                            op0=ALU.mult), reads, writes)

            def transposes(src, nblk, reads, tag):
                for k in range(nblk):
                    op("pe", lambda en, k=k: en.transpose(out=psTv[:, k, :], in_=src[:, k * 128:(k + 1) * 128],
                                                         identity=ident[:]),
                       reads + ["ident"], ["b7"], inc=(k == nblk - 1))

            def phaseA1(b):
                for s in range(2):
                    i = 2 * b + s
                    xs = xy[s]
                    xn = "xy%d" % s
                    op("dve", lambda en: en.scalar_tensor_tensor(out=t1[:, 0:512].bitcast(BF16), in0=xs[:], scalar=1.0,
                                                                in1=xs[:], op0=ALU.mult, op1=ALU.mult,
                                                                accum_out=ss[:, 0:1]), [xn], ["t1", "ss0"])
                    yield
                    op("pool", lambda en: en.tensor_scalar(out=ss[:, 1:2], in0=ss[:, 0:1], scalar1=1.0 / D,
                                                          scalar2=NORM_EPS, op0=ALU.mult, op1=ALU.add),
                       ["ss0"], ["ss1"])
                    op("pool", lambda en: en.tensor_tensor(out=ss[:, 2:3], in0=ss[:, 1:2], in1=negh[:, 0:1],
                                                          op=ALU.pow), ["ss1", "negh"], ["ss2"])
                    yield
                    yield
                    op("dve", lambda en: en.tensor_scalar(out=h_bf[:], in0=xs[:], scalar1=ss[:, 2:3],
                                                         scalar2=None, op0=ALU.mult), [xn, "ss2"], ["h_bf"])
                    yield
                    transposes(h_bf, 8, ["h_bf"], "h")
                    op("dve", lambda en: en.tensor_copy(hT[:, :, s * 128:(s + 1) * 128], psTv), ["b7"], [("hT", s)])
                    yield

            def phaseA2(b, groups, banks, gate=None):
                projbanks[:] = banks
                deferred = []
                qT_a = qT_a2[b % 2]
                gg_a = gg_a2[b % 2]
                for g in groups:
                    pos = WPOS[g]
                    wb = wbuf[pos % NW]
                    wn = "wbuf%d" % (pos % NW)
                    if g == 10 and gate is not None:
                        yield from wait(gate)
                    if g < 2:
                        for fp in range(2):
                            bk = next_bank()
                            for f2 in range(2):
                                ft = 2 * fp + f2
                                P = bankap[bk][:, f2 * 256:(f2 + 1) * 256]
                                for k in range(KT):
                                    op("pe", lambda en, k=k: en.matmul(out=P, lhsT=wb[:, k, ft * 128:(ft + 1) * 128],
                                                                      rhs=hT[:, k, :], start=(k == 0), stop=(k == KT - 1)),
                                       [wn, ("hT", 0), ("hT", 1)], ["b%d" % bk], inc=(k == KT - 1))
                            yield
                            e = evac_eng()
                            P2 = bankap[bk]
                            if g == 0:
                                copy_op(e, qT_a[:, 2 * fp:2 * fp + 2, :].rearrange("p a b -> p (a b)"), P2, ["b%d" % bk],
                                        [("qT_a", 0, 2 * fp), ("qT_a", 0, 2 * fp + 1)], scale=0.125)
                            else:
                                r0 = ((2 * b) % RING) * 128
                                copy_op(e, kT_a[:, 2 * fp:2 * fp + 2, r0:r0 + 256],
                                        P2.rearrange("p (a b) -> p a b", a=2), ["b%d" % bk],
                                        [("kT_a", 2 * fp + a_, (2 * b + t_) % RING) for a_ in range(2) for t_ in range(2)])
                    else:
                        for s in range(2):
                            i = 2 * b + s
                            bk = next_bank()
                            P = bankap[bk]
                            bn = "b%d" % bk
                            for k in range(KT):
                                op("pe", lambda en, k=k: en.matmul(out=P, lhsT=hT[:, k, s * 128:(s + 1) * 128],
                                                                  rhs=wb[:, k, :], start=(k == 0), stop=(k == KT - 1)),
                                   [wn, ("hT", s)], [bn], inc=(k == KT - 1))
                            yield
                            if deferred and g != 5:
                                deferred.pop(0)()
                            if g == 2:
                                sl = i % RING
                                copy_op(evac_eng(), Vaug[:, sl, :, 0:64],
                                        cap(P, [[64, 8], [1, 64]]), [bn], [("Vaug", sl)])
                            elif g in (3, 8, 9):
                                op("act", lambda en: en.activation(out=t1[:], in_=P, func=AF.Tanh, scale=0.5),
                                   [bn], ["t1"])
                                yield
                                if g == 3:
                                    dst = gg_a[:, s, :]
                                    dn = ("gg_a", 0, s)
                                else:
                                    dst = gg_r[:, s, (g - 8) * 512:(g - 7) * 512]
                                    dn = ("gg_r", s, g - 8)
                                op("dve", lambda en: en.scalar_tensor_tensor(out=dst, in0=t1[:], scalar=1.0, in1=P,
                                                                            op0=ALU.add, op1=ALU.mult),
                                   ["t1", bn], [dn])
                            elif g in (4, 5):
                                qk_rot = qk_rot2[:, s, :]
                                rt = rot[s]
                                rn = "rot%d" % s
                                cc = cap(rt[:, 0:64], [[0, 8], [1, 64]])
                                ssn = cap(rt[:, 64:128], [[0, 8], [32, 2], [1, 32]])
                                Pv = cap(P, [[64, 8], [1, 64]])
                                Psw = cap(P, [[64, 8], [-32, 2], [1, 32]], off=32)
                                op("dve", lambda en: en.tensor_tensor(out=cap(t1[:], [[64, 8], [1, 64]]), in0=Pv,
                                                                     in1=cc, op=ALU.mult), [bn, rn], ["t1"])
                                op("dve", lambda en: en.tensor_tensor(out=cap(t2[:], [[64, 8], [32, 2], [1, 32]]),
                                                                     in0=Psw, in1=ssn, op=ALU.mult), [bn, rn], ["t2"])
                                off = (g - 4) * 512
                                op("dve", lambda en: en.tensor_tensor(out=qk_rot[:, off:off + 512], in0=t1[:],
                                                                     in1=t2[:], op=ALU.add),
                                   ["t1", "t2"], [("qk_rot", s, g - 4)])
                                if g == 5:
                                    op("dve", lambda en: en.tensor_tensor(
                                        out=cap(k_dec[:, s, :], [[64, 8], [1, 64]]),
                                        in0=cap(qk_rot[:, 512:1024], [[64, 8], [1, 64]]),
                                        in1=cap(sd[:], [[1, 8], [0, 64]]), op=ALU.mult),
                                       [("qk_rot", s, 1), "sd"], [("k_dec", s)])
                                    def rot_T(s=s, qk_rot=qk_rot):
                                        transposes(qk_rot, 8, [("qk_rot", s, 0), ("qk_rot", s, 1)], "r")
                                        sl = slice(s * 128, (s + 1) * 128)
                                        op("act", lambda en: en.activation(out=qT_r[:, :, sl], in_=psTv[:, 0:4, :],
                                                                           func=AF.Copy), ["b7"], [("qT_r", s)])
                                        op("act", lambda en: en.activation(out=kT_r[:, :, sl], in_=psTv[:, 4:8, :],
                                                                           func=AF.Copy), ["b7"], [("kT_r", s)])
                                        op("dve", lambda en: en.tensor_tensor(out=qcT_r[:, :, sl], in0=psTv[:, 0:4, :],
                                                                             in1=cdT[:], op=ALU.mult),
                                           ["b7", "cdT"], [("qcT_r", s)])
                                    deferred.append(rot_T)
                            elif g in (6, 7):
                                copy_op(evac_eng(), v_r[:, s, (g - 6) * 512:(g - 5) * 512], P, [bn],
                                        [("v_r", s, g - 6)])
                            else:
                                tgt = t_ma if g < 12 else t_mr
                                hh = (g - 10) % 2
                                op("act", lambda en: en.activation(out=tgt[:, s, hh * 512:(hh + 1) * 512], in_=P,
                                                                   func=AF.Tanh, scale=0.5),
                                   [bn], [("tm", g < 12, s, hh)])
                    if pos + NW < NG:
                        load_w(pos + NW)
                    yield
                while deferred:
                    deferred.pop(0)()

            ogT_a2 = [ogT_a, ogT_a_b]
            ogT_r2 = [ogT_r, ogT_r_b]

            def O_h(h):
                c0 = (h % 4) * 65
                return psC[:, c0:c0 + 65]

            def ORh(h):
                c0 = (h % 2) * 512 + (h // 2) * 128
                return psD[:, c0:c0 + 128]

            def att(i):
                s = i % 2
                sl = slice(s * 128, (s + 1) * 128)
                kts = [kt for kt in range(5) if i - 4 + kt >= 0]
                far = [kt for kt in kts if kt < 3]
                bp = (i // 2) % 2
                qT_a = qT_a2[bp]
                gg_a = gg_a2[bp]

                def nearbank(h):
                    if h % 2 == 0:
                        return psB[:, 0:256], "b2"
                    return psC[:, 512:768], "b4"

                def scores(h):
                    hp, r0 = h // 2, 64 * (h % 2)
                    NB_, nbn = nearbank(h)
                    fb = "b%d" % (h % 2)
                    FA = psA[:, (h % 2) * 512:(h % 2 + 1) * 512]
                    c0 = 0 if 3 in kts else 128
                    masked = 0 in kts
                    if masked:
                        op("pe", lambda en: en.matmul(out=FA[:, 0:128], lhsT=ident[:], rhs=maskT[:], start=True,
                                                      stop=False), ["ident", "maskT"], [fb], inc=False)
                    op("pe", lambda en: en.matmul(out=NB_[:, c0:256], lhsT=ident[:], rhs=bias_hi[:, h, c0:256],
                                                  start=True, stop=False), [("bias_hi", h // 4), "ident"], [nbn],
                       inc=False)
                    first_far = not masked
                    for kt in kts:
                        ring = ((i - 4 + kt) % RING) * 128
                        if kt < 3:
                            out = FA[:, kt * 128:(kt + 1) * 128]
                            bn = fb
                            st, sp = first_far, (kt == far[-1])
                            first_far = False
                        else:
                            out = NB_[:, (kt - 3) * 128:(kt - 2) * 128]
                            bn = nbn
                            st, sp = False, (kt == 4)
                        op("pe", lambda en: en.matmul(out=out, lhsT=kT_a[r0:r0 + 64, hp, ring:ring + 128],
                                                      rhs=qT_a[r0:r0 + 64, hp, sl], start=st, stop=sp),
                           [("kT_a", hp, (i - 4 + kt) % RING), ("qT_a", 0, hp)], [bn],
                           inc=(kt == 4 or (far and kt == far[-1])))

                def softmax(h):
                    pt = PT[h % 2]
                    pn = "PT%d" % (h % 2)
                    SA = psA[:, (h % 2) * 512:(h % 2 + 1) * 512]
                    b01 = "b%d" % (h % 2)
                    if far:
                        a, bnd = far[0], far[-1] + 1
                        op("act", lambda en: en.activation(
                            out=pt[:, a:bnd, :].rearrange("p a b -> p (a b)"), in_=SA[:, a * 128:bnd * 128],
                            func=AF.Exp, bias=bfar[:, h:h + 1]), [b01, "bfar"], [pn])
                    c0 = 0 if 3 in kts else 128
                    NB_, nbn = nearbank(h)
                    op("act", lambda en: en.activation(
                        out=pt[:, 3:5, :].rearrange("p a b -> p (a b)")[:, c0:256], in_=NB_[:, c0:256], func=AF.Exp),
                       [nbn], [pn])

                def pv(h):
                    pt = PT[h % 2]
                    pn = "PT%d" % (h % 2)
                    first = True
                    for kt in kts:
                        slot = (i - 4 + kt) % RING
                        last = (kt == 4)
                        op("pe", lambda en, first=first: en.matmul(out=O_h(h), lhsT=pt[:, kt, :],
                                                                 rhs=Vaug[:, slot, h, :], start=first, stop=last),
                           [pn, ("Vaug", slot), "Vaug_ones"], ["b3"], inc=last)
                        first = False

                def epilogue(half):
                    hs = slice(half * 4, half * 4 + 4)
                    cs = slice(half * 256, half * 256 + 256)
                    op("dve", lambda en: en.reciprocal(out=rinv[:, hs], in_=cap(psC[:, 0:512], [[65, 4]], off=64)),
                       ["b3"], [("rinv", half)])
                    op("dve", lambda en: en.tensor_tensor(
                        out=cap(oa_tmp[:, cs], [[64, 4], [1, 64]]),
                        in0=cap(psC[:, 0:512], [[65, 4], [1, 64]]),
                        in1=cap(rinv[:, hs], [[1, 4], [0, 64]]), op=ALU.mult),
                       ["b3", ("rinv", half)], [("oa_tmp", half)])
                    op("dve", lambda en: en.tensor_tensor(out=og_a[:, cs], in0=oa_tmp[:, cs], in1=gg_a[:, s, cs],
                                                         op=ALU.mult),
                       [("oa_tmp", half), ("gg_a", 0, s)], [("og_a", half)])

                scores(0)
                for h in range(8):
                    if h + 1 < 8:
                        scores(h + 1)
                    softmax(h)
                    yield
                    pv(h)
                    if h % 4 == 3:
                        epilogue(h // 4)
                    yield
                dst = ogT_a2[i % 2]
                transposes(og_a, 4, [("og_a", 0), ("og_a", 1)], "a")
                op("dve", lambda en: en.tensor_copy(dst[:], psTv[:, 0:4, :]), ["b7"], [("ogT_a", i % 2)])
                yield

            def ret(i):
                s = i % 2
                sl = slice(s * 128, (s + 1) * 128)
                for par in range(2):
                    for hh in range(4):
                        h = 2 * hh + par
                        hp, r0 = h // 2, 64 * (h % 2)
                        op("pe", lambda en: en.matmul(out=ORh(h), lhsT=kT_r[r0:r0 + 64, hp, sl],
                                                      rhs=qT_r[r0:r0 + 64, hp, sl], start=True, stop=True),
                           [("kT_r", s), ("qT_r", s)], ["b%d" % (5 + par)], inc=(hh == 3))
                yield
                for par in range(2):
                    op("dve", lambda en: en.tensor_tensor(
                        out=Pr[:, par * 4:(par + 1) * 4, :].rearrange("p a b -> p (a b)"),
                        in0=psD[:, par * 512:(par + 1) * 512],
                        in1=Dt[:, par * 4:(par + 1) * 4, :].rearrange("p a b -> p (a b)"), op=ALU.mult),
                       ["b%d" % (5 + par), "Dt"], [("Pr", par)])
                yield
                cross = i > 0
                for h in range(8):
                    hp, r0 = h // 2, 64 * (h % 2)
                    bn = "b%d" % (5 + h % 2)
                    op("pe", lambda en: en.matmul(out=ORh(h), lhsT=Pr[:, (h % 2) * 4 + h // 2, :],
                                                  rhs=v_r[:, s, h * 128:(h + 1) * 128], start=True, stop=not cross),
                       [("Pr", h % 2), ("v_r", s, h // 4)], [bn], inc=(not cross))
                    if cross:
                        op("pe", lambda en: en.matmul(out=ORh(h), lhsT=qcT_r[r0:r0 + 64, hp, sl],
                                                      rhs=state_bf[r0:r0 + 64, hp, :], start=False, stop=True),
                           [("qcT_r", s), "state_bf"], [bn], inc=True)
                yield
                for h in range(8):
                    op("dve", lambda en: en.bn_stats(out=bnst[:, h, :], in_=ORh(h)),
                       ["b%d" % (5 + h % 2)], ["bnst"])
                    if h % 4 == 3:
                        yield
                me, mo = bnst[:, :, 1], bnst[:, :, 4]
                M2e, M2o = bnst[:, :, 2], bnst[:, :, 5]
                dv = lambda fn, r, w: op("dve", fn, r, w)
                dv(lambda en: en.tensor_tensor(out=gq[:, 0, :], in0=me, in1=mo, op=ALU.add), ["bnst"], ["gq0"])
                dv(lambda en: en.tensor_tensor(out=gq[:, 1, :], in0=me, in1=mo, op=ALU.subtract), ["bnst"], ["gq1"])
                dv(lambda en: en.tensor_tensor(out=gq[:, 2, :], in0=gq[:, 1, :], in1=gq[:, 1, :], op=ALU.mult),
                   ["gq1"], ["gq2"])
                dv(lambda en: en.tensor_tensor(out=gq[:, 3, :], in0=M2e, in1=M2o, op=ALU.add), ["bnst"], ["gq3"])
                dv(lambda en: en.tensor_scalar(out=gq[:, 3, :], in0=gq[:, 3, :], scalar1=1.0 / 128, scalar2=GN_EPS,
                                               op0=ALU.mult, op1=ALU.add), ["gq3"], ["gq3"])
                dv(lambda en: en.scalar_tensor_tensor(out=gve[:], in0=gq[:, 2, :], scalar=0.25, in1=gq[:, 3, :],
                                                      op0=ALU.mult, op1=ALU.add), ["gq2", "gq3"], ["gve"])
                yield
                yield
                op("pool", lambda en: en.tensor_tensor(out=grs[:], in0=gve[:], in1=negh[:], op=ALU.pow),
                   ["gve", "negh"], ["grs"])
                yield
                yield
                yield
                dv(lambda en: en.scalar_tensor_tensor(out=gq[:, 4, :], in0=gq[:, 0, :], scalar=-0.5, in1=grs[:],
                                                      op0=ALU.mult, op1=ALU.mult), ["gq0", "grs"], ["gq4"])
                dv(lambda en: en.tensor_scalar(out=gq[:, 1, :], in0=gq[:, 0, :], scalar1=0.5, scalar2=None,
                                               op0=ALU.mult), ["gq0", "gq1"], ["gq1"])
                yield
                for hh in range(4):
                    h = hh
                    op("act", lambda en: en.activation(out=or_n[:, h * 128:(h + 1) * 128], in_=ORh(h),
                                                       func=AF.Identity, scale=grs[:, h:h + 1],
                                                       bias=gq[:, 4, h:h + 1]),
                       ["b%d" % (5 + h % 2), "gq4", "grs"], [("or_n", h)])
                    h = 4 + hh
                    op("dve", lambda en: en.tensor_scalar(out=or_n[:, h * 128:(h + 1) * 128], in0=ORh(h),
                                                         scalar1=gq[:, 1, h:h + 1], scalar2=grs[:, h:h + 1],
                                                         op0=ALU.subtract, op1=ALU.mult),
                       ["b%d" % (5 + h % 2), "gq1", "grs"], [("or_n", h)])
                    if hh % 2 == 1:
                        yield
                yield
                yield
                op("dve", lambda en: en.tensor_tensor(out=or_n[:], in0=or_n[:], in1=gg_r[:, s, :], op=ALU.mult),
                   [("or_n", h_) for h_ in range(8)] + [("gg_r", s, 0), ("gg_r", s, 1)],
                   ["or_n"] + [("or_n", h_) for h_ in range(8)])
                yield
                yield
                for h in range(8):
                    hp, p0 = h // 2, 64 * (h % 2)
                    op("pe", lambda en: en.matmul(out=psD[p0:p0 + 64, hp * 128:(hp + 1) * 128],
                                                  lhsT=k_dec[:, s, h * 64:(h + 1) * 64],
                                                  rhs=v_r[:, s, h * 128:(h + 1) * 128], start=True, stop=True),
                       [("k_dec", s), ("v_r", s, h // 4)], ["b5"], inc=(h == 7))
                dst = ogT_r2[i % 2]
                transposes(or_n, 8, ["or_n"] + [("or_n", h_) for h_ in range(8)], "r2")
                op("act", lambda en: en.activation(out=dst[:], in_=psTv, func=AF.Copy), ["b7"], [("ogT_r", i % 2)])
                yield
                for hp in range(4):
                    op("dve", lambda en: en.scalar_tensor_tensor(
                        out=state[:, hp, :], in0=state[:, hp, :], scalar=cd2[:, hp:hp + 1],
                        in1=psD[:, hp * 128:(hp + 1) * 128], op0=ALU.mult, op1=ALU.add),
                       ["state", "cd2", "b5"], ["state"])
                yield
                op("pool", lambda en: en.tensor_copy(state_bf[:], state[:]), ["state"], ["state_bf"])
                yield

            def tail(i):
                s = i % 2
                oa = ogT_a2[i % 2]
                orr = ogT_r2[i % 2]
                P4 = psC[:, 512:1024]
                for n in range(2):
                    cs = slice(n * 512, (n + 1) * 512)
                    for k in range(4):
                        op("pe", lambda en: en.matmul(out=P4, lhsT=oa[:, k, :], rhs=w_oa[:, k, cs],
                                                      start=(k == 0), stop=(k == 3)),
                           [("ogT_a", i % 2), ("w", id(w_oa), k)], ["b4"], inc=(k == 3))
                    yield
                    op("dve", lambda en: en.scalar_tensor_tensor(out=u_bf[:, cs], in0=t_ma[:, s, cs], scalar=1.0, in1=P4,
                                                                op0=ALU.add, op1=ALU.mult),
                       [("tm", True, s, n), "b4"], [("u_bf", n)])
                    yield
                    for k in range(8):
                        op("pe", lambda en: en.matmul(out=P4, lhsT=orr[:, k, :], rhs=w_or[:, k, cs],
                                                      start=(k == 0), stop=(k == 7)),
                           [("ogT_r", i % 2), ("w", id(w_or), k)], ["b4"], inc=(k == 7))
                    yield
                    op("dve", lambda en: en.scalar_tensor_tensor(out=v_bf[:, cs], in0=t_mr[:, s, cs], scalar=1.0, in1=P4,
                                                                op0=ALU.add, op1=ALU.mult),
                       [("tm", False, s, n), "b4"], [("v_bf", n)])
                    op("dve", lambda en: en.tensor_tensor(out=u_bf[:, cs], in0=u_bf[:, cs], in1=v_bf[:, cs], op=ALU.add),
                       [("u_bf", n), ("v_bf", n)], [("u_bf", n)])
                    yield
                transposes(u_bf, 8, [("u_bf", 0), ("u_bf", 1)], "m")
                op("act", lambda en: en.activation(out=mixedT[:], in_=psTv, func=AF.Copy), ["b7"], ["mixedT"])
                yield
                yb = xy[2 + s]
                yn = "xy%d" % (2 + s)
                for n in range(2):
                    cs = slice(n * 512, (n + 1) * 512)
                    for k in range(8):
                        op("pe", lambda en: en.matmul(out=P4, lhsT=mixedT[:, k, :], rhs=w_o[:, k, cs],
                                                      start=(k == 0), stop=(k == 7)),
                           ["mixedT", ("w", id(w_o), k)], ["b4"], inc=(k == 7))
                    yield
                    op("dve", lambda en: en.tensor_tensor(out=yb[:, cs], in0=P4, in1=yb[:, cs], op=ALU.add),
                       ["b4", yn], [yn])
                    yield
                op("act", lambda en: en.activation(out=v_bf[:], in_=yb[:], func=AF.Square, accum_out=ss2[:, 0:1]),
                   [yn], [("v_bf", 0), ("v_bf", 1), "ss2_0"])
                yield
                op("pool", lambda en: en.tensor_scalar(out=ss2[:, 1:2], in0=ss2[:, 0:1], scalar1=1.0 / D,
                                                      scalar2=NORM_EPS, op0=ALU.mult, op1=ALU.add),
                   ["ss2_0"], ["ss2_1"])
                op("pool", lambda en: en.tensor_tensor(out=ss2[:, 2:3], in0=ss2[:, 1:2], in1=negh[:, 0:1],
                                                      op=ALU.pow), ["ss2_1", "negh"], ["ss2_2"])
                yield
                op("dve", lambda en: en.scalar_tensor_tensor(out=yb[:], in0=yb[:], scalar=ss2[:, 2:3], in1=fgain[:],
                                                            op0=ALU.mult, op1=ALU.mult),
                   [yn, "ss2_2", "fgain"], [yn])
                dma("sp", out_d[i * 128:(i + 1) * 128, :], yb[:], [yn], [], ("out", s))
                yield

            rrp = [0]

            def run(chains):
                chains = list(chains)
                chains0 = list(chains)
                rrp[0] = 0
                rt = {id(c): 0.0 for c in chains}
                while chains:
                    progressed = False
                    if GREEDY:
                        order = sorted(chains, key=lambda c_: rt[id(c_)])
                    elif POLICY.get("H") is not None:
                        k_ = rrp[0] % len(chains)
                        rr = chains[k_:] + chains[:k_]
                        lo = min(rt[id(c_)] for c_ in chains)
                        order = ([c_ for c_ in rr if rt[id(c_)] <= lo + POLICY["H"]]
                                 + sorted([c_ for c_ in rr if rt[id(c_)] > lo + POLICY["H"]], key=lambda c_: rt[id(c_)]))
                    else:
                        wts = POLICY.get("w", (1, 1, 1, 1))
                        slots = [c_ for ci, c_ in enumerate(chains0) if c_ in chains
                                 for _ in range(wts[min(ci, len(wts) - 1)])]
                        k_ = rrp[0] % len(slots)
                        order = slots[k_:] + slots[:k_]
                    rrp[0] += 1
                    for c in order:
                        sch.step_end = None
                        try:
                            r = next(c)
                        except StopIteration:
                            chains.remove(c)
                            progressed = True
                            break
                        if r == "blocked":
                            continue
                        if sch.step_end is not None:
                            rt[id(c)] = sch.step_end
                        progressed = True
                        break
                    assert progressed

            done = set()

            def mark(gen, key):
                yield from gen
                done.add(key)

            def wait(keys):
                while not all(k in done for k in keys):
                    yield "blocked"

            ATT_G = [0, 1, 2, 3]
            BK3 = [0, 1, 2, 3, 5, 6]
            REST_G = list(range(4, NG))

            def seq(*gens):
                for g in gens:
                    yield from g

            load_x(0)
            load_x(1)
            for g in range(NW):
                load_w(g)
            if stop >= 1:
                run([seq(phaseA1(0), phaseA2(0, WORDER, BK3)), resident_gen()])
            for b in range(NB):
                if stop < 2:
                    break
                nxt = b + 1 < NB
                if nxt:
                    load_x(2 * b + 2)
                    load_x(2 * b + 3)
                    for g in range(NW):
                        load_w(g)
                t0, t1_ = 2 * b, 2 * b + 1
                run([seq(att(t0), att(t1_)), seq(ret(t0), ret(t1_))] + ([phaseA1(b + 1)] if nxt else []))
                load_y(t0)
                load_y(t1_)
                tails = seq(tail(t0), mark(tail(t1_), ("tail", t1_)))
                if nxt:
                    run([tails, phaseA2(b + 1, WORDER, BK3, gate=[("tail", t1_)])])
                else:
                    run([tails])
            sch.finish("sp")

    print("[kernel] build: %d instructions, %d waits, %d dma sems, model %.0f us" % (
        sch.nins, sch.nwait, len(sch.dsem), sch.makespan))
    if DRY:
        return sch.makespan
    return nc


def _tables(NT):
    H = 8
    hs = np.arange(H, dtype=np.float64)
    lg = np.log1p(-np.exp2(-5.0 - hs))
    j = np.arange(128, dtype=np.float64)
    kk = j[:, None]
    qq = j[None, :]
    same = (kk // 64) == (qq // 64)
    past = (kk // 64) < (qq // 64)
    dist = np.abs(qq - kk)
    Dt = np.zeros((128, H, 128))
    for h in range(H):
        Dt[:, (h % 2) * 4 + h // 2, :] = np.where(same | past, np.exp(dist * lg[h]), 0.0) * 0.125
    cdT = np.zeros((128, 4, 128))
    cd2 = np.zeros((128, 4))
    for p in range(128):
        for pr in range(4):
            h = 2 * pr + p // 64
            cdT[p, pr, :] = 0.125 * np.exp((j + 1.0) * lg[h])
            cd2[p, pr] = np.exp(128.0 * lg[h])
    sd = np.exp((127.0 - j)[:, None] * lg[None, :])
    half = 32
    inv_freq = np.power(10000.0, -np.arange(half, dtype=np.float64) / half)
    pos = np.arange(NT * 128, dtype=np.float64)
    ang = pos[:, None] * inv_freq[None, :]
    c, sn = np.cos(ang), np.sin(ang)
    rot = np.concatenate([c, c, -sn, sn], axis=1)
    ident = np.eye(128)
    f = lambda a: np.ascontiguousarray(a, dtype=np.float32)
    return dict(Dt=f(Dt.reshape(128, 1024)), cdT=f(cdT.reshape(128, 512)), cd2=f(cd2), sd=f(sd), rot=f(rot),
                ident=f(ident))


def _bias_tables(rel_bias):
    k = np.arange(128)[:, None]
    q = np.arange(128)[None, :]
    biasT = np.zeros((128, 8, 2, 128), np.float32)
    for jj in range(2):
        dist = q - k + 128 * (1 - jj)
        idx = np.clip(dist, -128, 128) + 128
        biasT[:, :, jj, :] = np.transpose(rel_bias[:, idx], (1, 0, 2))
    m = (k >= 64) & (q < 64)
    biasT[:, :, 1, :] = np.where(m[:, None, :], np.float32(MASK_NEG), biasT[:, :, 1, :])
    bfar = np.ascontiguousarray(np.broadcast_to(rel_bias[:, 256][None, :], (128, 8)), dtype=np.float32)
    return np.ascontiguousarray(biasT.reshape(128, 2048)), bfar


def make_in_maps(x, norm_gain, w_in, rel_bias, gn_gain, w_out_attn, w_out_ret, w_out, final_gain, NT):
    f = lambda a: np.ascontiguousarray(np.asarray(a), dtype=np.float32)
    tabs = _tables(NT)
    biasT, bfar = _bias_tables(f(rel_bias)[0])
    shared = dict(
        w_in=f(w_in)[0], w_oa=f(w_out_attn)[0], w_or=f(w_out_ret)[0], w_o=f(w_out)[0],
        gain_pk=f(f(norm_gain)[0].reshape(8, 128).T), gn_pk=f(f(gn_gain)[0].reshape(8, 128).T),
        fgain=f(np.broadcast_to(f(final_gain)[None, :], (128, D))),
        biasT=biasT, bfar=bfar, **tabs)
    xs = f(x)
    maps = []
    for c in range(xs.shape[0]):
        m = dict(shared)
        m["x"] = np.ascontiguousarray(xs[c, :NT * 128, :])
        maps.append(m)
    return maps


_NC_CACHE = {}


def kernel(x, norm_gain, w_in, rel_bias, gn_gain, w_out_attn, w_out_ret, w_out, final_gain):
    x = np.asarray(x)
    B, S, _ = x.shape
    NT = S // 128
    NB = NT // 2
    if NB not in _NC_CACHE:
        _NC_CACHE[NB] = build(NB)
    nc = _NC_CACHE[NB]
    maps = make_in_maps(x, norm_gain, w_in, rel_bias, gn_gain, w_out_attn, w_out_ret, w_out, final_gain, NT)
    res = run_bass_kernel_spmd(nc, maps, core_ids=list(range(B)))
    out = np.stack([np.asarray(r["out"]) for r in res.results], axis=0)
    return out.astype(np.float32, copy=False)
```
